# Optimizing a Trainium2 kernel written in Bass

```python
import jax, jax.numpy as jnp
from jax import lax
import numpy as np

D_MODEL = 2048
BATCH = 4
SEQ = 8192
DEPTH = 1
DEC_BATCH = 32
DEC_SEQ = 64
PAST_LEN = 2048

CHUNK = 64
Q_BLOCK = 128
N_HEADS = 8
QK_NOPE = 128
QK_ROPE = 64
V_HEAD = 128
Q_RANK = 512
KV_RANK = 256
MLA_WIDTH = N_HEADS * V_HEAD
POOL_WINDOWS = (2, 4, 8, 16)
N_POOL_GROUPS = len(POOL_WINDOWS)
POOL_WIDTH = D_MODEL - MLA_WIDTH
POOL_GROUP = POOL_WIDTH // N_POOL_GROUPS
POOL_HIST = max(POOL_WINDOWS) - 1
IN_WIDTH = Q_RANK + KV_RANK + QK_ROPE + POOL_WIDTH
D_FF = 5632
PLE_DIM = 256
ROPE_THETA = 10000.0
LN_EPS = 1e-5
RMS_EPS = 1e-6
ATTN_SCALE = (QK_NOPE + QK_ROPE) ** -0.5
DEEPNORM_ALPHA = (2.0 * DEPTH) ** 0.25
DEEPNORM_BETA = (8.0 * DEPTH) ** -0.25

kernel_name = "hybrid_mla_pool_streaming_encoder_step"


def layer_norm(x, g, b):
    xf = x.astype(jnp.float32)
    mu = jnp.mean(xf, -1, keepdims=True)
    var = jnp.mean(jnp.square(xf - mu), -1, keepdims=True)
    return ((xf - mu) * lax.rsqrt(var + LN_EPS) * g + b).astype(x.dtype)


def rms_norm(x, g):
    xf = x.astype(jnp.float32)
    return (xf * lax.rsqrt(jnp.mean(jnp.square(xf), -1, keepdims=True) + RMS_EPS) * g).astype(x.dtype)


def swiglu(x, w_gu, w_down):
    gate, up = jnp.split(x @ w_gu, 2, axis=-1)
    return (jax.nn.silu(gate) * up) @ w_down


def rope_tables(pos):
    inv = 1.0 / (ROPE_THETA ** (jnp.arange(0, QK_ROPE, 2, dtype=jnp.float32) / QK_ROPE))
    ang = pos.astype(jnp.float32)[:, None] * inv[None, :]
    ang = jnp.concatenate([ang, ang], -1)
    return jnp.cos(ang), jnp.sin(ang)


def apply_rope(x, cos, sin):
    x1, x2 = jnp.split(x, 2, axis=-1)
    rot = jnp.concatenate([-x2, x1], -1)
    return (x * cos + rot * sin).astype(x.dtype)


def latent_attention(q_lat, q_rope, c_kv, k_rope, mask):
    s = (jnp.einsum('bqhc,bkc->bhqk', q_lat, c_kv, preferred_element_type=jnp.float32)
         + jnp.einsum('bqhr,bkr->bhqk', q_rope, k_rope, preferred_element_type=jnp.float32)) * ATTN_SCALE
    if mask is not None:
        s = jnp.where(mask, s, -jnp.inf)
    pr = jax.nn.softmax(s, axis=-1).astype(c_kv.dtype)
    return jnp.einsum('bhqk,bkc->bqhc', pr, c_kv)


def prompt_attention(q_lat, q_rope, c_kv, k_rope):
    B, S = q_lat.shape[:2]
    nb = S // Q_BLOCK
    key_chunk = jnp.arange(S) // CHUNK

    def block(args):
        ql, qr, i = args
        q_chunk = (i * Q_BLOCK + jnp.arange(Q_BLOCK)) // CHUNK
        mask = key_chunk[None, :] <= q_chunk[:, None]
        return latent_attention(ql, qr, c_kv, k_rope, mask)

    ql = q_lat.reshape(B, nb, Q_BLOCK, N_HEADS, KV_RANK).swapaxes(0, 1)
    qr = q_rope.reshape(B, nb, Q_BLOCK, N_HEADS, QK_ROPE).swapaxes(0, 1)
    o = lax.map(block, (ql, qr, jnp.arange(nb)))
    return o.swapaxes(0, 1).reshape(B, S, N_HEADS, KV_RANK)


def pool_mixer(u, hist, offset, w_pool, pool_scale):
    B, L = u.shape[:2]
    full = jnp.concatenate([hist, u], 1)
    upf = full.astype(jnp.float32)
    cs = jnp.cumsum(upf, axis=1)
    cs = jnp.concatenate([jnp.zeros_like(cs[:, :1]), cs], 1)
    end = cs[:, POOL_HIST + 1:]
    pos = offset + jnp.arange(L)
    means = []
    for g, w in enumerate(POOL_WINDOWS):
        sl = slice(g * POOL_GROUP, (g + 1) * POOL_GROUP)
        start = cs[:, POOL_HIST + 1 - w: POOL_HIST + 1 - w + L, sl]
        cnt = jnp.minimum(w, pos + 1).astype(jnp.float32)[None, :, None]
        means.append((end[..., sl] - start) / cnt)
    d = (jnp.concatenate(means, -1) - upf[:, POOL_HIST:]).astype(u.dtype)
    d = d.reshape(B, L, N_POOL_GROUPS, POOL_GROUP)
    y = jnp.einsum('blgc,gcd->blgd', d, w_pool).reshape(B, L, POOL_WIDTH) * pool_scale
    return y, full[:, -POOL_HIST:]


def encoder_layer(x, p, past_c, past_kr, pool_hist, offset, w):
    B, L, _ = x.shape
    x = layer_norm(DEEPNORM_ALPHA * x + 0.5 * swiglu(x, w['ffn1_gu'], w['ffn1_down']), w['ln1_g'], w['ln1_b'])
    z = x @ w['w_in']
    c_q = rms_norm(z[..., :Q_RANK], w['g_q'])
    c_kv = rms_norm(z[..., Q_RANK:Q_RANK + KV_RANK], w['g_kv'])
    k_rope = z[..., Q_RANK + KV_RANK:Q_RANK + KV_RANK + QK_ROPE]
    u = z[..., Q_RANK + KV_RANK + QK_ROPE:]
    cos, sin = rope_tables(offset + jnp.arange(L))
    q_nope = jnp.einsum('blc,chd->blhd', c_q, w['w_uq_nope'])
    q_rope = apply_rope(jnp.einsum('blc,chr->blhr', c_q, w['w_uq_rope']), cos[:, None, :], sin[:, None, :])
    k_rope = apply_rope(k_rope, cos, sin)
    q_lat = jnp.einsum('blhd,chd->blhc', q_nope, w['w_uk'])
    if past_c is None:
        o_lat = prompt_attention(q_lat, q_rope, c_kv, k_rope)
    else:
        o_lat = latent_attention(q_lat, q_rope,
                                 jnp.concatenate([past_c, c_kv], 1),
                                 jnp.concatenate([past_kr, k_rope], 1), None)
    o_attn = jnp.einsum('blhc,chd->blhd', o_lat, w['w_uv']).reshape(B, L, MLA_WIDTH)
    o_pool, new_hist = pool_mixer(u, pool_hist, offset, w['w_pool'], w['pool_scale'])
    mix = jnp.concatenate([o_attn, o_pool], -1) @ w['w_o']
    x = layer_norm(DEEPNORM_ALPHA * x + mix, w['ln2_g'], w['ln2_b'])
    x = layer_norm(DEEPNORM_ALPHA * x + 0.5 * swiglu(x, w['ffn2_gu'], w['ffn2_down']), w['ln3_g'], w['ln3_b'])
    ple = jax.nn.sigmoid(x @ w['w_ple_gate']) * (p @ w['w_ple_proj'])
    x = layer_norm(DEEPNORM_ALPHA * x + ple, w['ln4_g'], w['ln4_b'])
    return x, c_kv, k_rope, new_hist


def setup_inputs(seed: int = 0) -> dict:
    key = jax.random.key(seed)
    ks = iter(jax.random.split(key, 48))
    f = jnp.float32

    def nrm(shape, scale=1.0):
        return jax.random.normal(next(ks), shape, f) * scale

    def gain(n):
        return 1.0 + nrm((DEPTH, n), 0.02)

    return {
        "x_prompt": nrm((BATCH, SEQ, D_MODEL)),
        "x_sample": nrm((DEC_BATCH, DEC_SEQ, D_MODEL)),
        "cache_kv_latent": nrm((DEPTH, DEC_BATCH, PAST_LEN, KV_RANK)),
        "cache_k_rope": nrm((DEPTH, DEC_BATCH, PAST_LEN, QK_ROPE)),
        "state_pool": nrm((DEPTH, DEC_BATCH, POOL_HIST, POOL_WIDTH)),
        "p_prompt": nrm((DEPTH, BATCH, SEQ, PLE_DIM)),
        "p_sample": nrm((DEPTH, DEC_BATCH, DEC_SEQ, PLE_DIM)),
        "ffn1_gu": nrm((DEPTH, D_MODEL, 2 * D_FF), D_MODEL ** -0.5),
        "ffn1_down": nrm((DEPTH, D_FF, D_MODEL), DEEPNORM_BETA * D_FF ** -0.5),
        "ln1_g": gain(D_MODEL),
        "ln1_b": nrm((DEPTH, D_MODEL), 0.02),
        "w_in": nrm((DEPTH, D_MODEL, IN_WIDTH), D_MODEL ** -0.5),
        "g_q": gain(Q_RANK),
        "g_kv": gain(KV_RANK),
        "w_uq_nope": nrm((DEPTH, Q_RANK, N_HEADS, QK_NOPE), Q_RANK ** -0.5),
        "w_uq_rope": nrm((DEPTH, Q_RANK, N_HEADS, QK_ROPE), Q_RANK ** -0.5),
        "w_uk": nrm((DEPTH, KV_RANK, N_HEADS, QK_NOPE), KV_RANK ** -0.5),
        "w_uv": nrm((DEPTH, KV_RANK, N_HEADS, V_HEAD), KV_RANK ** -0.5),
        "w_pool": nrm((DEPTH, N_POOL_GROUPS, POOL_GROUP, POOL_GROUP), POOL_GROUP ** -0.5),
        "pool_scale": 1.0 + nrm((DEPTH, POOL_WIDTH), 0.1),
        "w_o": nrm((DEPTH, D_MODEL, D_MODEL), DEEPNORM_BETA * D_MODEL ** -0.5),
        "ln2_g": gain(D_MODEL),
        "ln2_b": nrm((DEPTH, D_MODEL), 0.02),
        "ffn2_gu": nrm((DEPTH, D_MODEL, 2 * D_FF), D_MODEL ** -0.5),
        "ffn2_down": nrm((DEPTH, D_FF, D_MODEL), DEEPNORM_BETA * D_FF ** -0.5),
        "ln3_g": gain(D_MODEL),
        "ln3_b": nrm((DEPTH, D_MODEL), 0.02),
        "w_ple_gate": nrm((DEPTH, D_MODEL, D_MODEL), D_MODEL ** -0.5),
        "w_ple_proj": nrm((DEPTH, PLE_DIM, D_MODEL), DEEPNORM_BETA * PLE_DIM ** -0.5),
        "ln4_g": gain(D_MODEL),
        "ln4_b": nrm((DEPTH, D_MODEL), 0.02),
    }


def reference(x_prompt, x_sample, cache_kv_latent, cache_k_rope, state_pool, p_prompt, p_sample,
              ffn1_gu, ffn1_down, ln1_g, ln1_b, w_in, g_q, g_kv, w_uq_nope, w_uq_rope, w_uk, w_uv,
              w_pool, pool_scale, w_o, ln2_g, ln2_b, ffn2_gu, ffn2_down, ln3_g, ln3_b,
              w_ple_gate, w_ple_proj, ln4_g, ln4_b):
    hp, hs = x_prompt, x_sample
    ckv_p, kr_p, pool_p, ckv_s, kr_s, pool_s = [], [], [], [], [], []
    for i in range(DEPTH):
        w = {
            'ffn1_gu': ffn1_gu[i], 'ffn1_down': ffn1_down[i], 'ln1_g': ln1_g[i], 'ln1_b': ln1_b[i],
            'w_in': w_in[i], 'g_q': g_q[i], 'g_kv': g_kv[i],
            'w_uq_nope': w_uq_nope[i], 'w_uq_rope': w_uq_rope[i], 'w_uk': w_uk[i], 'w_uv': w_uv[i],
            'w_pool': w_pool[i], 'pool_scale': pool_scale[i], 'w_o': w_o[i],
            'ln2_g': ln2_g[i], 'ln2_b': ln2_b[i],
            'ffn2_gu': ffn2_gu[i], 'ffn2_down': ffn2_down[i], 'ln3_g': ln3_g[i], 'ln3_b': ln3_b[i],
            'w_ple_gate': w_ple_gate[i], 'w_ple_proj': w_ple_proj[i], 'ln4_g': ln4_g[i], 'ln4_b': ln4_b[i],
        }
        zero_hist = jnp.zeros((hp.shape[0], POOL_HIST, POOL_WIDTH), hp.dtype)
        hp, c1, k1, s1 = encoder_layer(hp, p_prompt[i], None, None, zero_hist, 0, w)
        hs, c2, k2, s2 = encoder_layer(hs, p_sample[i], cache_kv_latent[i], cache_k_rope[i],
                                       state_pool[i], PAST_LEN, w)
        ckv_p.append(c1); kr_p.append(k1); pool_p.append(s1)
        ckv_s.append(c2); kr_s.append(k2); pool_s.append(s2)
    new_kv_latent_prompt = jnp.stack(ckv_p)
    new_k_rope_prompt = jnp.stack(kr_p)
    new_pool_prompt = jnp.stack(pool_p)
    new_kv_latent_sample = jnp.stack(ckv_s)
    new_k_rope_sample = jnp.stack(kr_s)
    new_pool_sample = jnp.stack(pool_s)
    return (hp, hs, new_kv_latent_prompt, new_k_rope_prompt, new_pool_prompt,
            new_kv_latent_sample, new_k_rope_sample, new_pool_sample)
```

```python
import contextlib
import numpy as np
import ml_dtypes
import concourse.bass as bass
import concourse.mybir as mybir
from concourse.bass_utils import run_bass_kernel_spmd

F32 = mybir.dt.float32
BF16 = mybir.dt.bfloat16
AF = mybir.ActivationFunctionType
ALU = mybir.AluOpType
AX = mybir.AxisListType

D = 2048
DFF = 5632
NCH = 16
ALPHA = 2.0 ** 0.25
SCALE = 192.0 ** -0.5
LN_EPS = 1e-5
RMS_EPS = 1e-6
NT = 8
TP = 512
TS = 256
NEG = -30000.0
WINS = (2, 4, 8, 16)
SLABS = [(0, 11), (11, 11), (22, 11), (33, 11)]


class Rec:
    ENG = ["pe", "act", "dve", "pool", "sp"]

    def __init__(self):
        self.q = {e: [] for e in self.ENG}
        self.cnt = {e: 0 for e in self.ENG}
        self.epoch = {e: 0 for e in self.ENG}
        self.seen = {e: {} for e in self.ENG}
        self.lastw = {}
        self.readers = {}
        self.chan = {}
        self.semnames = []

    def _sem(self, name):
        if name not in self.semnames:
            self.semnames.append(name)
        return name

    def _tok_eng(self, e):
        if self.cnt[e] >= 20000:
            self.epoch[e] += 1
            self.cnt[e] = 0
        self.cnt[e] += 1
        return (self._sem(f"e_{e}_{self.epoch[e]}"), self.cnt[e], e)

    def _deps(self, reads, writes):
        toks = []
        for k in reads:
            t = self.lastw.get(k)
            if t is not None:
                toks.append(t)
        for k in writes:
            t = self.lastw.get(k)
            if t is not None:
                toks.append(t)
            toks.extend(self.readers.get(k, {}).values())
        return toks

    def _commit(self, tok, reads, writes):
        for k in reads:
            d = self.readers.setdefault(k, {})
            o = d.get(tok[0])
            if o is None or o[1] < tok[1]:
                d[tok[0]] = tok
        for k in writes:
            self.lastw[k] = tok
            self.readers[k] = {}

    def _emit_waits(self, e, toks):
        for (name, val, src) in toks:
            if src == e and e == "pe":
                continue
            if self.seen[e].get(name, 0) >= val:
                continue
            self.seen[e][name] = val
            self.q[e].append(("w", name, val))

    @staticmethod
    def _excl(reads, writes):
        extra = tuple(k for k in reads if k.startswith("ps") and k[2:].isdigit() and k not in writes)
        return tuple(reads), tuple(writes) + extra

    def op(self, e, fns, reads=(), writes=()):
        if callable(fns):
            fns = [fns]
        reads, writes = self._excl(reads, writes)
        self._emit_waits(e, self._deps(reads, writes))
        tok = self._tok_eng(e)
        for f in fns[:-1]:
            self.q[e].append(("i", f, None, 0))
        self.q[e].append(("i", fns[-1], tok[0], 1))
        self._commit(tok, reads, writes)
        return tok

    def dma(self, e, fn, chan, reads=(), writes=(), join=False):
        toks = self._deps(reads, writes)
        if join:
            toks = [t for t in toks if t[0] != f"d_{chan}"]
        self._emit_waits(e, toks)
        n = self.chan.get(chan, 0) + 1
        self.chan[chan] = n
        tok = (self._sem(f"d_{chan}"), 16 * n, "dma")
        self.q[e].append(("i", fn, tok[0], 16))
        self._commit(tok, reads, writes)
        return tok

    def coll(self, fn, chan, reads=(), writes=()):
        self._emit_waits("pool", self._deps(reads, writes))
        tok = (self._sem(f"c_{chan}"), 1, "cc")
        self.q["pool"].append(("i", fn, tok[0], None))
        self._commit(tok, reads, writes)
        return tok

    def all_tokens(self):
        toks = []
        for e in self.ENG:
            for ep in range(self.epoch[e] + 1):
                name = f"e_{e}_{ep}"
                if name in self.semnames:
                    toks.append((name, self.cnt[e] if ep == self.epoch[e] else 20000, e))
        for ch, n in self.chan.items():
            toks.append((f"d_{ch}", 16 * n, "dma"))
        return toks

    def barrier(self):
        toks = self.all_tokens()
        for e in self.ENG:
            self._emit_waits(e, [t for t in toks if not (t[2] == e)])
        self.lastw = {}
        self.readers = {}


def MM(out, l, r, st, sp):
    return lambda e: e.matmul(out, lhsT=l, rhs=r, start=st, stop=sp)


def TR(out, in_, ident):
    return lambda e: e.transpose(out, in_, ident)


def ACTF(out, in_, func, scale=None, bias=None):
    def f(e):
        kw = {}
        if scale is not None:
            kw["scale"] = scale
        if bias is not None:
            kw["bias"] = bias
        return e.activation(out=out, in_=in_, func=func, **kw)
    return f


def TT(out, a, b, op):
    return lambda e: e.tensor_tensor(out=out, in0=a, in1=b, op=op)


def TSC(out, a, s1, s2, op0, op1=None):
    def f(e):
        if op1 is None:
            return e.tensor_scalar(out=out, in0=a, scalar1=s1, scalar2=None, op0=op0)
        return e.tensor_scalar(out=out, in0=a, scalar1=s1, scalar2=s2, op0=op0, op1=op1)
    return f


def STT(out, a, s, b, op0, op1):
    return lambda e: e.scalar_tensor_tensor(out=out, in0=a, scalar=s, in1=b, op0=op0, op1=op1)


def CP(out, in_):
    return lambda e: e.tensor_copy(out=out, in_=in_)


def MS(ap, v):
    return lambda e: e.memset(ap, v)


def RCP(out, in_):
    return lambda e: e.reciprocal(out=out, in_=in_)


def RED(out, in_, op):
    return lambda e: e.tensor_reduce(out=out, in_=in_, axis=AX.X, op=op)


def DMA(out, in_):
    return lambda e: e.dma_start(out=out, in_=in_)


def DMAS(out, in_):
    return lambda e: e.dma_start(out=out, in_=in_, allow_slow_non_contiguous=True)


import os
STAGE = int(os.environ.get("KSTAGE", "99"))
KSUB = int(os.environ.get("KSUB", "99"))
KCORES = int(os.environ.get("KCORES", "8"))
KSAMPLE = int(os.environ.get("KSAMPLE", "0"))
KNT = int(os.environ.get("KNT", "8"))


class _Stop(Exception):
    pass


def sub(k):
    if KSUB == k:
        raise _Stop()


def build_program():
    nc = bass.Bass("TRN2", target_bir_lowering=False)
    R = Rec()
    es = contextlib.ExitStack()

    def finish():
        R.barrier()
        return nc, R, es

    def din(name, shape, dt=F32):
        return nc.dram_tensor(name, list(shape), dt, kind="ExternalInput").ap()

    def dout(name, shape, dt=F32):
        return nc.dram_tensor(name, list(shape), dt, kind="ExternalOutput").ap()

    def dscr(name, shape, dt=F32):
        return nc.dram_tensor(name, list(shape), dt).ap()

    xp = din("xp", [NT, TP, D]); xs = din("xs", [TS, D])
    pp = din("pp", [NT, TP, 256]); pss = din("ps", [TS, 256])
    ckvc = din("ckvc", [4, 2048, 256]); ckrc = din("ckrc", [4, 2048, 64])
    histT = din("histT", [128, 8 * 4 * 16])
    w_f1gu = din("ffn1_gu", [D, 2 * DFF]); w_f1d = din("ffn1_down", [DFF, D])
    w_f2gu = din("ffn2_gu", [D, 2 * DFF]); w_f2d = din("ffn2_down", [DFF, D])
    w_in = din("w_in", [D, 1856]); w_inp = din("w_in_perm", [D, 64])
    w_uqn = din("w_uqn", [512, 1024]); w_uqr = din("w_uqr", [512, 512]); w_uqrp = din("w_uqr_perm", [512, 512])
    w_ukT = din("w_ukT", [128, 8 * 256]); w_uv = din("w_uv", [256, 1024])
    w_pool = din("w_pool", [4 * 256, 256]); w_o = din("w_o", [D, D])
    w_pg = din("w_ple_gate", [D, D]); w_pp = din("w_ple_proj", [256, D])
    lnp_d = din("lnp", [128, 8 * 16]); gq_d = din("gq", [128, 4]); gkv_d = din("gkv", [128, 2]); psc_d = din("pscale", [128, 8])
    id32_d = din("ident32", [128, 128]); idb_d = din("identb", [128, 128], BF16)
    rope_d = din("ropetab", [NT + 1, 64, 2 * TP])
    mask_d = din("maskb", [128, 8 * TP], BF16)
    ptab_d = din("pooltab", [NT + 1, 128, 64])
    sel_d = din("sel", [128, 2])

    yp = dout("yp", [NT, TP, D]); ys = dout("ys", [TS, D])
    okvp = dout("okvp", [NT, TP, 256]); okrp = dout("okrp", [NT, TP, 64]); opoolp = dout("opoolp", [16, 1024])
    okvs = dout("okvs", [TS, 256]); okrs = dout("okrs", [TS, 64]); opools = dout("opools", [16, 4 * 1024])

    x1sp = dscr("x1sp", [NT + 1, 128, NCH * TP])
    cqsp = dscr("cqsp", [NT + 1, 128, 4 * TP], BF16)
    usp = dscr("usp", [NT + 1, 128, 8 * TP])
    oasp = dscr("oasp", [NT + 1, 128, 8 * TP], BF16)
    XBR = 288
    xb_in = [nc.dram_tensor(f"xb_in{k}", [2 * XBR, 1024], BF16) for k in range(4)]
    xb_out = [nc.dram_tensor(f"xb_out{k}", [4 * XBR, 1024], BF16) for k in range(4)]
    XFR = NT * 128 + 256
    xf_in = nc.dram_tensor("xf_in", [XFR, 128], F32)
    xf_out = nc.dram_tensor("xf_out", [2 * XFR, 128], F32)

    def sb(name, shape, dt=F32):
        return es.enter_context(nc.sbuf_tensor("sb_" + name, list(shape), dt))

    id32 = sb("id32", [128, 128]); idb = sb("idb", [128, 128], BF16); onesb = sb("onesb", [128, 128], BF16)
    lnp = sb("lnp", [128, 8 * 16]); gq = sb("gq", [128, 4]); gkv = sb("gkv", [128, 2]); psc = sb("psc", [128, 8])
    sel = sb("sel", [128, 2])
    lnpa = sb("lnpa", [128, 4 * 16])
    nkmax = sb("nkmax", [128, 4]); nkt = sb("nkt", [128, 8]); negnk = sb("negnk", [128, 8])
    wsl = [sb(f"wsl{i}", [128, 8192], BF16) for i in range(3)]
    tmp = [sb(f"tmp{i}", [128, 512]) for i in range(6)]
    UB = 136 * 1024
    U = sb("U", [128, UB // 4])
    ps = [es.enter_context(nc.psum_tensor(f"psum{i}", [128, 512], F32)) for i in range(8)]

    class Carver:
        def __init__(self):
            self.off = 0

        def take(self, shape_free, dt):
            n = int(np.prod(shape_free))
            nbytes = n * (4 if dt == F32 else 2)
            nbytes4 = (nbytes + 31) // 32 * 32
            a = self.off // 4
            self.off += nbytes4
            assert self.off <= UB, f"U overflow {self.off} > {UB}"
            v = U[:, a:a + nbytes4 // 4]
            if dt != F32:
                v = v.bitcast(BF16)
            v = v[:, 0:n]
            if len(shape_free) == 2:
                v = v.rearrange("p (a b) -> p a b", a=shape_free[0])
            elif len(shape_free) == 3:
                v = v.rearrange("p (a b c) -> p a b c", a=shape_free[0], b=shape_free[1])
            elif len(shape_free) == 4:
                v = v.rearrange("p (a b c d) -> p a b c d", a=shape_free[0], b=shape_free[1], c=shape_free[2])
            return v

    def psb(i):
        return ps[i][:].bitcast(BF16)

    wstate = {"i": 0}
    wcache = {}
    NWBLK = 100
    wscr = dscr("wscr", [NWBLK, 128, 8192], BF16)

    def wload(parts, bid=None):
        s = wstate["i"] % 3
        wstate["i"] += 1
        key = f"wsl{s}"
        used = max(off + src.shape[1] * src.shape[2] for (off, src) in parts)
        if bid is not None and bid in wcache:
            idx = wcache[bid]
            R.dma("sp", DMA(wsl[s][:, 0:used], wscr[idx, :, 0:used]), f"wc{s}", reads=(f"wscr{idx}",), writes=(key,))
            return wsl[s], key
        for (off, src) in parts:
            a, b = src.shape[1], src.shape[2]
            dst = wsl[s][:, off:off + a * b].rearrange("p (a b) -> p a b", a=a)
            R.dma("pool", DMA(dst, src), f"w{s}", reads=(), writes=(key,))
        if bid is not None:
            idx = len(wcache)
            assert idx < NWBLK
            wcache[bid] = idx
            R.dma("sp", DMA(wscr[idx, :, 0:used], wsl[s][:, 0:used]), f"wb{s}", reads=(key,), writes=(f"wscr{idx}",))
        return wsl[s], key

    def wview(slot, off, a, b):
        return slot[:, off:off + a * b].rearrange("p (a b) -> p a b", a=a)

    for (dst, src, k) in [(id32, id32_d, "id32"), (idb, idb_d, "idb"), (lnp, lnp_d, "lnp"), (gq, gq_d, "gq"),
                          (gkv, gkv_d, "gkv"), (psc, psc_d, "psc"), (sel, sel_d, "sel")]:
        R.dma("pool", DMA(dst[:], src[:, :]), "const", writes=(k,))
    R.op("dve", MS(onesb[:], 1.0), writes=("onesb",))
    R.op("dve", MS(nkmax[:], 0.0), writes=("nkmax",))
    R.barrier()
    for i in range(4):
        R.op("dve", TSC(lnpa[:, i * 16:(i + 1) * 16], lnp[:, (2 * i + 1) * 16:(2 * i + 2) * 16], ALPHA, None, ALU.mult),
             reads=("lnp",), writes=("lnpa",))
    R.barrier()

    class StatsHook:
        def __init__(self, T, x32, xk, mean_bank, ex2_bank):
            self.T, self.x32, self.xk, self.mb, self.eb = T, x32, xk, mean_bank, ex2_bank
            self.q = []

        def chunk(self, c):
            T = self.T
            i = c % 4
            R.op("act", ACTF(lnsq[:, i, 0:T], self.x32[:, c, 0:T], AF.Copy), reads=(f"{self.xk}{c}",), writes=(f"lnsqA{i}",))
            R.op("act", ACTF(lnsq[:, 4 + i, 0:T], self.x32[:, c, 0:T], AF.Square), reads=(f"{self.xk}{c}",), writes=(f"lnsqB{i}",))

            def pe(c=c, i=i):
                R.op("pe", MM(ps[self.mb][:, 0:T], onesb[:], lnsq[:, i, 0:T], c == 0, c == 15), reads=(f"lnsqA{i}", "onesb"), writes=(f"ps{self.mb}",))
                R.op("pe", MM(ps[self.eb][:, 0:T], onesb[:], lnsq[:, 4 + i, 0:T], c == 0, c == 15), reads=(f"lnsqB{i}", "onesb"), writes=(f"ps{self.eb}",))
            self.q.append(pe)
            while len(self.q) > 2:
                self.q.pop(0)()

        def flush(self):
            while self.q:
                self.q.pop(0)()

    def layer_norm(T, x32, xb, lnidx, xk, xbk, post_scale=None, pre=None):
        g = lnp[:, (2 * lnidx) * 16:(2 * lnidx + 1) * 16]
        b = lnp[:, (2 * lnidx + 1) * 16:(2 * lnidx + 2) * 16]
        mb, eb = (pre.mb, pre.eb) if pre is not None else (0, 1)
        mean_ps, ex2_ps = ps[mb][:, 0:T], ps[eb][:, 0:T]
        if pre is not None:
            pre.flush()
        for hh in range(0 if pre is not None else 4):
            cs = slice(hh * 4, hh * 4 + 4)
            ls = slice((hh % 2) * 4, (hh % 2) * 4 + 4)
            lk = f"lnsq{hh % 2}"
            keys = tuple(f"{xk}{c}" for c in range(hh * 4, hh * 4 + 4))
            bkeys = tuple(f"{xbk}{c}" for c in range(hh * 4, hh * 4 + 4))
            R.op("dve", CP(xb[:, cs, 0:T], x32[:, cs, 0:T]), reads=keys, writes=bkeys)
            R.op("act", ACTF(lnsq[:, ls, 0:T], x32[:, cs, 0:T], AF.Square), reads=keys, writes=(lk,))
            fns = []
            for c in range(4):
                cc = hh * 4 + c
                fns.append(MM(mean_ps, onesb[:], xb[:, cc, 0:T], cc == 0, cc == 15))
            R.op("pe", fns, reads=bkeys + ("onesb",), writes=("ps0",))
            fns = []
            for c in range(4):
                cc = hh * 4 + c
                fns.append(MM(ex2_ps, onesb[:], lnsq[:, (hh % 2) * 4 + c, 0:T], cc == 0, cc == 15))
            R.op("pe", fns, reads=(lk, "onesb"), writes=("ps1",))
        mean, rstd, nmr, t3 = tmp[0][:, 0:T], tmp[1][:, 0:T], tmp[2][:, 0:T], tmp[3][:, 0:T]
        R.op("act", ACTF(mean, mean_ps, AF.Copy, scale=1.0 / D), reads=(f"ps{mb}",), writes=("tmp0",))
        R.op("dve", TT(t3, mean, mean, ALU.mult), reads=("tmp0",), writes=("tmp3",))
        R.op("dve", STT(rstd, ex2_ps, 1.0 / D, t3, ALU.mult, ALU.subtract), reads=(f"ps{eb}", "tmp3"), writes=("tmp1",))
        R.op("dve", TSC(rstd, rstd, LN_EPS, None, ALU.add), reads=("tmp1",), writes=("tmp1",))
        R.op("act", ACTF(rstd, rstd, AF.Sqrt), reads=("tmp1",), writes=("tmp1",))
        R.op("dve", RCP(rstd, rstd), reads=("tmp1",), writes=("tmp1",))
        R.op("dve", STT(nmr, mean, -1.0, rstd, ALU.mult, ALU.mult), reads=("tmp0", "tmp1"), writes=("tmp2",))
        ba = lnpa[:, lnidx * 16:(lnidx + 1) * 16]
        for c in range(NCH):
            k, bk = f"{xk}{c}", f"{xbk}{c}"
            ta = tmp[4 + (c % 2)][:, 0:T]
            tk = f"tmp{4 + (c % 2)}"
            R.op("dve", STT(ta, x32[:, c, 0:T], g[:, c:c + 1], rstd, ALU.mult, ALU.mult), reads=(k, "tmp1", "lnp"), writes=(tk,))
            R.op("dve", STT(ta, nmr, g[:, c:c + 1], ta, ALU.mult, ALU.add), reads=(tk, "tmp2", "lnp"), writes=(tk,))
            R.op("act", ACTF(xb[:, c, 0:T], ta, AF.Identity, bias=b[:, c:c + 1]), reads=(tk, "lnp"), writes=(bk,))
            if post_scale is None:
                R.op("act", ACTF(x32[:, c, 0:T], ta, AF.Identity, bias=b[:, c:c + 1]), reads=(tk, "lnp"), writes=(k,))
            else:
                R.op("act", ACTF(x32[:, c, 0:T], ta, AF.Identity, scale=post_scale, bias=ba[:, c:c + 1]), reads=(tk, "lnpa"), writes=(k,))

    def ffn(T, x32, xb, aT, wgu, wd, xk, xbk, wname, hook=None, bgw=None):
        wguv = wgu.rearrange("(k p) f -> p k f", p=128)
        wdv = wd.rearrange("(k p) d -> p k d", p=128)
        xbkeys = tuple(f"{xbk}{c}" for c in range(NCH))
        fcount = 0
        for (f0, nf) in SLABS:
            fl = 0
            while fl < nf:
                n = min(2, nf - fl)
                fa = f0 + fl
                slot, wk = wload([(0, wguv[:, :, fa * 128:(fa + n) * 128]),
                                  (16 * n * 128, wguv[:, :, DFF + fa * 128:DFF + (fa + n) * 128])], bid=(wname, "gu", fa))
                gv = wview(slot, 0, 16, n * 128)
                uv = wview(slot, 16 * n * 128, 16, n * 128)
                for i in range(n):
                    par = fcount % 2
                    fcount += 1
                    bg, bu = 2 * par, 2 * par + 1
                    R.op("pe", [MM(ps[bg][:, 0:T], gv[:, k, i * 128:(i + 1) * 128], xb[:, k, 0:T], k == 0, k == 15)
                                for k in range(16)], reads=xbkeys + (wk,), writes=(f"ps{bg}",))
                    R.op("pe", [MM(ps[bu][:, 0:T], uv[:, k, i * 128:(i + 1) * 128], xb[:, k, 0:T], k == 0, k == 15)
                                for k in range(16)], reads=xbkeys + (wk,), writes=(f"ps{bu}",))
                    ts_ = tmp[par][:, 0:T]
                    R.op("act", ACTF(ts_, ps[bg][:, 0:T], AF.Silu), reads=(f"ps{bg}",), writes=(f"tmp{par}",))
                    R.op("dve", TT(aT[:, fl + i, 0:T], ts_, ps[bu][:, 0:T], ALU.mult),
                         reads=(f"tmp{par}", f"ps{bu}"), writes=(f"aT{fl + i}",))
                    if bgw and fcount % 3 == 0:
                        bgw.pop(0)()
                fl += n
            akeys = tuple(f"aT{i}" for i in range(nf))
            for dg in range(4):
                slot, wk = wload([(0, wdv[:, f0:f0 + nf, dg * 512:(dg + 1) * 512])], bid=(wname, "d", f0, dg))
                dv = wview(slot, 0, nf, 512)
                for dc in range(4):
                    bnk = 4 + dc
                    c = dg * 4 + dc
                    R.op("pe", [MM(ps[bnk][:, 0:T], dv[:, k, dc * 128:(dc + 1) * 128], aT[:, k, 0:T], k == 0, k == nf - 1)
                                for k in range(nf)], reads=akeys + (wk,), writes=(f"ps{bnk}",))
                    R.op("dve", STT(x32[:, c, 0:T], ps[bnk][:, 0:T], 0.5, x32[:, c, 0:T], ALU.mult, ALU.add),
                         reads=(f"ps{bnk}", f"{xk}{c}"), writes=(f"{xk}{c}",))
                    if hook is not None and f0 == SLABS[-1][0]:
                        hook.chunk(c)
        while bgw:
            bgw.pop(0)()

    def load_transposed(T, src_tok, x32, xb, xtok, xk, xbk, scale):
        for blk in range(T // 128):
            xt = xtok[blk % 2]
            xtk = f"xtok{blk % 2}"
            R.dma("pool", DMA(xt[:], src_tok[blk * 128:(blk + 1) * 128, :]), xtk, writes=(xtk,))
            if KSUB == 9:
                continue
            for g in range(4):
                bnk = (blk * 4 + g) % 4
                R.op("pe", [TR(ps[bnk][:, i * 128:(i + 1) * 128], xt[:, (4 * g + i) * 128:(4 * g + i + 1) * 128], id32[:])
                            for i in range(4)], reads=(xtk, "id32"), writes=(f"ps{bnk}",))
                if KSUB == 8:
                    continue
                pv = ps[bnk][:].rearrange("p (a b) -> p a b", a=4)
                keys = tuple(f"{xk}{4 * g + i}" for i in range(4))
                bkeys = tuple(f"{xbk}{4 * g + i}" for i in range(4))
                if KSUB != 6:
                    R.op("act", ACTF(x32[:, 4 * g:4 * g + 4, blk * 128:(blk + 1) * 128], pv, AF.Copy, scale=scale),
                         reads=(f"ps{bnk}",), writes=keys)
                if KSUB != 7:
                    R.op("dve", CP(xb[:, 4 * g:4 * g + 4, blk * 128:(blk + 1) * 128], pv), reads=(f"ps{bnk}",), writes=bkeys)

    def rms_stats(T, src32, nchunk, sq, sqk, srck, bank, eps, rstd_tmp):
        R.op("act", ACTF(sq[:, 0:nchunk, 0:T], src32[:, 0:nchunk, 0:T], AF.Square), reads=srck, writes=(sqk,))
        R.op("pe", [MM(ps[bank][:, 0:T], onesb[:], sq[:, c, 0:T], c == 0, c == nchunk - 1) for c in range(nchunk)],
             reads=(sqk, "onesb"), writes=(f"ps{bank}",))
        rs = tmp[rstd_tmp][:, 0:T]
        tk = f"tmp{rstd_tmp}"
        R.op("dve", TSC(rs, ps[bank][:, 0:T], 1.0 / (128 * nchunk), eps, ALU.mult, ALU.add), reads=(f"ps{bank}",), writes=(tk,))
        R.op("act", ACTF(rs, rs, AF.Sqrt), reads=(tk,), writes=(tk,))
        R.op("dve", RCP(rs, rs), reads=(tk,), writes=(tk,))
        return rs, tk

    cv = Carver()
    x32 = cv.take([16, TP], F32)
    xb = cv.take([16, TP], BF16)
    aT = cv.take([11, TP], BF16)
    lnsq = cv.take([8, TP], BF16)
    xtok = [cv.take([D], F32), cv.take([D], F32)]
    cq32 = cv.take([4, TP], F32)
    cqn = cv.take([4, TP], BF16)
    ckv32 = cv.take([2, TP], F32)
    kvb = cv.take([2, TP], BF16)
    sqb = cv.take([4, TP], BF16)
    kr32 = cv.take([TP], F32)
    krb = cv.take([TP], BF16)
    rtab = cv.take([2 * TP], F32)
    kvtok = cv.take([4, 320], F32)
    vtokb = cv.take([4, 256], BF16)
    ust = cv.take([4, TP], F32)
    ptk = cv.take([2, 512], F32)
    s_kvb = sb("s_kvb", [128, 2, TS], BF16)
    s_krb = sb("s_krb", [64, TS], BF16)
    s_vtok = sb("s_vtok", [64, 4, 256], BF16)

    xb_in_aps = [t.ap() for t in xb_in]
    xf_in_ap = xf_in.ap()

    def phaseA(ti, T, src_tok, is_sample):
        nblk = T // 128
        load_transposed(T, src_tok, x32, xb, xtok, "x32_", "xb_", ALPHA)
        sub(6)
        sub(7)
        sub(8)
        sub(9)
        sub(10)
        hk = StatsHook(T, x32, "x32_", 0, 1)
        ffn(T, x32, xb, aT, w_f1gu, w_f1d, "x32_", "xb_", "f1", hook=hk)
        sub(11)
        layer_norm(T, x32, xb, 0, "x32_", "xb_", pre=hk)
        sub(12)
        xkeys = tuple(f"x32_{c}" for c in range(NCH))
        xbkeys = tuple(f"xb_{c}" for c in range(NCH))
        R.dma("pool", DMA(x1sp[ti, :, 0:NCH * T].rearrange("p (c t) -> p c t", c=NCH), x32[:, :, 0:T]), "st_x1",
              reads=xkeys, writes=(f"x1sp{ti}",))
        sub(13)
        winv = w_in.rearrange("(k p) f -> p k f", p=128)
        slot, wk = wload([(0, winv[:, :, 0:512])], bid=("win", "cq"))
        wv = wview(slot, 0, 16, 512)
        for m in range(4):
            R.op("pe", [MM(ps[m][:, 0:T], wv[:, k, m * 128:(m + 1) * 128], xb[:, k, 0:T], k == 0, k == 15) for k in range(16)],
                 reads=xbkeys + (wk,), writes=(f"ps{m}",))
            R.op("act", ACTF(cq32[:, m, 0:T], ps[m][:, 0:T], AF.Copy), reads=(f"ps{m}",), writes=("cq32",))
        sub(14)
        winpv = w_inp.rearrange("(k p) f -> p k f", p=128)
        slot, wk = wload([(0, winv[:, :, 512:832]), (16 * 320, winpv[:, :, 0:64])], bid=("win", "kv"))
        wv = wview(slot, 0, 16, 320)
        wvp = wview(slot, 16 * 320, 16, 64)
        R.dma("pool", DMA(rtab[0:64, :], rope_d[ti, :, :]), "rtab", writes=("rtab",))
        for m in range(2):
            R.op("pe", [MM(ps[m][:, 0:T], wv[:, k, m * 128:(m + 1) * 128], xb[:, k, 0:T], k == 0, k == 15) for k in range(16)],
                 reads=xbkeys + (wk,), writes=(f"ps{m}",))
            R.op("act", ACTF(ckv32[:, m, 0:T], ps[m][:, 0:T], AF.Copy), reads=(f"ps{m}",), writes=("ckv32",))
        R.op("pe", [MM(ps[2][0:64, 0:T], wv[:, k, 256:320], xb[:, k, 0:T], k == 0, k == 15) for k in range(16)],
             reads=xbkeys + (wk,), writes=("ps2",))
        R.op("pe", [MM(ps[3][0:64, 0:T], wvp[:, k, 0:64], xb[:, k, 0:T], k == 0, k == 15) for k in range(16)],
             reads=xbkeys + (wk,), writes=("ps3",))
        kvb_t = s_kvb if is_sample else kvb
        krb_t = s_krb if is_sample else krb
        t1, t2 = tmp[1][0:64, 0:T], tmp[2][0:64, 0:T]
        R.op("dve", TT(t1, ps[2][0:64, 0:T], rtab[0:64, 0:T], ALU.mult), reads=("ps2", "rtab"), writes=("tmp1",))
        R.op("dve", TT(t2, ps[3][0:64, 0:T], rtab[0:64, TP:TP + T], ALU.mult), reads=("ps3", "rtab"), writes=("tmp2",))
        R.op("dve", TT(kr32[0:64, 0:T], t1, t2, ALU.add), reads=("tmp1", "tmp2"), writes=("kr32",))
        R.op("act", ACTF(krb_t[0:64, 0:T], kr32[0:64, 0:T], AF.Copy), reads=("kr32",), writes=("krb",))
        sub(16)
        for ub in range(2):
            slot, wk = wload([(0, winv[:, :, 832 + ub * 512:832 + (ub + 1) * 512])], bid=("win", "u", ub))
            wv = wview(slot, 0, 16, 512)
            for m in range(4):
                R.op("pe", [MM(ps[m][:, 0:T], wv[:, k, m * 128:(m + 1) * 128], xb[:, k, 0:T], k == 0, k == 15) for k in range(16)],
                     reads=xbkeys + (wk,), writes=(f"ps{m}",))
                R.op("act", ACTF(ust[:, m, 0:T], ps[m][:, 0:T], AF.Copy), reads=(f"ps{m}",), writes=("ust",))
            R.dma("pool", DMA(usp[ti, :, ub * 4 * T:(ub + 1) * 4 * T].rearrange("p (c t) -> p c t", c=4), ust[:, :, 0:T]), "st_u",
                  reads=("ust",), writes=(f"usp{ti}",))
            if not is_sample:
                R.dma("pool", DMA(xf_in_ap[ti * 128:(ti + 1) * 128, ub * 64:(ub + 1) * 64].rearrange("p (c t) -> p c t", c=4),
                                ust[:, :, T - 16:T]), "st_xu", reads=("ust",), writes=(f"xf_in_u{ti}_{ub}",))
            if (not is_sample and ti == NT - 1) or is_sample:
                nseq = 4 if is_sample else 1
                L = T // nseq
                for s in range(nseq):
                    pb = (ub * nseq + s) % 2
                    R.op("pe", [TR(ps[5][0:16, m * 128:(m + 1) * 128], ust[:, m, s * L + L - 16:s * L + L], id32[:]) for m in range(4)],
                         reads=("ust", "id32"), writes=("ps5",))
                    R.op("act", ACTF(ptk[0:16, pb, :], ps[5][0:16, 0:512], AF.Copy), reads=("ps5",), writes=(f"ptk{pb}",))
                    dst = opools[:, s * 1024 + ub * 512:s * 1024 + (ub + 1) * 512] if is_sample else opoolp[:, ub * 512:(ub + 1) * 512]
                    R.dma("pool", DMA(dst, ptk[0:16, pb, :]), f"st_pool{pb}", reads=(f"ptk{pb}",))

        rs, rk = rms_stats(T, cq32, 4, sqb, "sqb", ("cq32",), 4, RMS_EPS, 0)
        for m in range(4):
            R.op("dve", STT(cqn[:, m, 0:T], cq32[:, m, 0:T], gq[:, m:m + 1], rs, ALU.mult, ALU.mult),
                 reads=("cq32", rk, "gq"), writes=("cqn",))
        R.dma("pool", DMA(cqsp[ti, :, 0:4 * T].rearrange("p (c t) -> p c t", c=4), cqn[:, :, 0:T]), "st_cq",
              reads=("cqn",), writes=(f"cqsp{ti}",))
        rs, rk = rms_stats(T, ckv32, 2, sqb, "sqb", ("ckv32",), 4, RMS_EPS, 0)
        for m in range(2):
            R.op("dve", STT(ckv32[:, m, 0:T], ckv32[:, m, 0:T], gkv[:, m:m + 1], rs, ALU.mult, ALU.mult),
                 reads=("ckv32", rk, "gkv"), writes=("ckv32",))
        R.op("act", ACTF(kvb_t[:, :, 0:T], ckv32[:, :, 0:T], AF.Copy), reads=("ckv32",), writes=("kvb",))
        R.op("act", ACTF(sqb[:, 0:2, 0:T], ckv32[:, :, 0:T], AF.Square), reads=("ckv32",), writes=("sqb",))
        R.op("act", ACTF(sqb[0:64, 2, 0:T], kr32[0:64, 0:T], AF.Square), reads=("kr32",), writes=("sqb",))
        R.op("pe", [MM(ps[5][:, 0:T], onesb[:], sqb[:, 0, 0:T], True, False),
                    MM(ps[5][:, 0:T], onesb[:], sqb[:, 1, 0:T], False, False),
                    MM(ps[5][:, 0:T], onesb[0:64, :], sqb[0:64, 2, 0:T], False, True)],
             reads=("sqb", "onesb"), writes=("ps5",))
        if not is_sample:
            R.op("dve", RED(nkt[:, 0:1], ps[5][:, 0:T], ALU.max), reads=("ps5",), writes=("nkt",))
            R.op("dve", TT(nkmax[:, 0:1], nkmax[:, 0:1], nkt[:, 0:1], ALU.max), reads=("nkt", "nkmax"), writes=("nkmax",))
        else:
            R.op("dve", RED(nkt[:, 0:4], ps[5][:, 0:T].rearrange("p (s t) -> p s t", s=4), ALU.max),
                 reads=("ps5",), writes=("nkt",))
        sub(15)
        if not is_sample:
            for blk in range(nblk):
                bnk = 6 + (blk % 2)
                R.op("pe", [TR(ps[bnk][:, 0:128], ckv32[:, 0, blk * 128:(blk + 1) * 128], id32[:]),
                            TR(ps[bnk][:, 128:256], ckv32[:, 1, blk * 128:(blk + 1) * 128], id32[:]),
                            TR(ps[bnk][:, 256:320], kr32[0:64, blk * 128:(blk + 1) * 128], id32[0:64, 0:64])],
                     reads=("ckv32", "kr32", "id32"), writes=(f"ps{bnk}",))
                R.op("act", ACTF(kvtok[:, blk, :], ps[bnk][:, 0:320], AF.Copy), reads=(f"ps{bnk}",), writes=("kvtok",))
                R.op("dve", CP(vtokb[:, blk, :], ps[bnk][:, 0:256]), reads=(f"ps{bnk}",), writes=("vtokb",))
            R.dma("pool", DMA(okvp[ti].rearrange("(b p) c -> p b c", p=128), kvtok[:, :, 0:256]), "st_kv", reads=("kvtok",))
            R.dma("pool", DMA(okrp[ti].rearrange("(b p) c -> p b c", p=128), kvtok[:, :, 256:320]), "st_kr", reads=("kvtok",))
            base = (ti % 2) * XBR
            xb_in_ap = xb_in_aps[ti // 2]
            R.dma("pool", DMA(xb_in_ap[base:base + 128, :].rearrange("p (c t) -> p c t", c=2), kvb[:, :, :]), "st_xk",
                  reads=("kvb",), writes=(f"xb_in_k{ti}",))
            R.dma("pool", DMA(xb_in_ap[base + 128:base + 256, :].rearrange("p (b c) -> p b c", b=4), vtokb[:, :, :]), "st_xv",
                  reads=("vtokb",), writes=(f"xb_in_v{ti}",))
            R.dma("pool", DMA(xb_in_ap[base + 256:base + 288, :].rearrange("r (s c) -> (r s) c", s=2), krb[0:64, :]), "st_xr",
                  reads=("krb",), writes=(f"xb_in_r{ti}",))
        else:
            for s in range(4):
                bnk = 6 + (s % 2)
                R.op("pe", [TR(ps[bnk][0:64, 0:128], ckv32[:, 0, s * 64:(s + 1) * 64], id32[:]),
                            TR(ps[bnk][0:64, 128:256], ckv32[:, 1, s * 64:(s + 1) * 64], id32[:]),
                            TR(ps[bnk][0:64, 256:320], kr32[0:64, s * 64:(s + 1) * 64], id32[0:64, 0:64])],
                     reads=("ckv32", "kr32", "id32"), writes=(f"ps{bnk}",))
                R.op("act", ACTF(kvtok[0:64, s, :], ps[bnk][0:64, 0:320], AF.Copy), reads=(f"ps{bnk}",), writes=("kvtok",))
                R.op("dve", CP(s_vtok[:, s, :], ps[bnk][0:64, 0:256]), reads=(f"ps{bnk}",), writes=("vtokb",))
            R.dma("pool", DMA(okvs.rearrange("(s p) c -> p s c", p=64), kvtok[0:64, :, 0:256]), "st_kv", reads=("kvtok",))
            R.dma("pool", DMA(okrs.rearrange("(s p) c -> p s c", p=64), kvtok[0:64, :, 256:320]), "st_kr", reads=("kvtok",))

    if STAGE == 0:
        return finish()
    for li in range(0 if KSAMPLE else min(NT, KNT)):
        try:
            phaseA(li, TP, xp[li], False)
        except _Stop:
            return finish()
        if STAGE == 1:
            return finish()
    if STAGE == 2:
        return finish()
    R.op("dve", MS(tmp[5][:, 0:128], 0.0), writes=("tmp5",))
    R.dma("pool", DMA(xf_in_ap[NT * 128:NT * 128 + 128, :], tmp[5][:, 0:128]), "st_xz", reads=("tmp5",), writes=("xf_in_z",))
    R.dma("pool", DMA(xf_in_ap[NT * 128 + 128:NT * 128 + 256, :], tmp[5][:, 0:128]), "st_xz2", reads=("tmp5",), writes=("xf_in_z2",))
    R.dma("pool", DMAS(xf_in_ap[NT * 128 + 128:NT * 128 + 256, 0:1], nkmax[:, 0:1]), "st_xn", reads=("nkmax", "xf_in_z2"), writes=("xf_in_n",))
    if KCORES == 8:
        for k in range(4):
            rk = tuple(f"xb_in_{t}{ti}" for t in "kvr" for ti in (2 * k, 2 * k + 1))
            R.coll(lambda g, k=k: g.collective_compute("AllGather", ALU.bypass, replica_groups=[[0, 1], [2, 3], [4, 5], [6, 7]],
                                                       ins=[xb_in[k].ap().opt()], outs=[xb_out[k].ap().opt()]), f"xb{k}",
                   reads=rk, writes=(f"xb_out{k}",))
        rk = tuple(f"xf_in_u{ti}_{ub}" for ti in range(NT) for ub in range(2)) + ("xf_in_z", "xf_in_n")
        R.coll(lambda g: g.collective_compute("AllGather", ALU.bypass, replica_groups=[[0, 1], [2, 3], [4, 5], [6, 7]],
                                              ins=[xf_in.ap().opt()], outs=[xf_out.ap().opt()]), "xf",
               reads=rk, writes=("xf_out",))
    if STAGE == 3:
        return finish()
    phaseA(NT, TS, xs, True)
    if STAGE == 4:
        return finish()
    tokA = R.lastw.copy()
    R.barrier()
    for k in ("xb_out0", "xb_out1", "xb_out2", "xb_out3", "xf_out"):
        if k in tokA:
            R.lastw[k] = tokA[k]

    cv = Carver()
    KT = cv.take([2, 8192], BF16)
    KR = cv.take([8192], BF16)
    VV = cv.take([64, 256], BF16)
    cqn1 = cv.take([4, TP], BF16)
    qn = cv.take([TP], BF16)
    qr = cv.take([8, TP], BF16)
    sq1 = cv.take([3, TP], BF16)
    sq1x = [sq1, cv.take([3, TP], BF16)]
    PT = [cv.take([TP], BF16), cv.take([TP], BF16)]
    olat = cv.take([2, TP], BF16)
    oab = [cv.take([TP], BF16), cv.take([TP], BF16)]
    maskb = cv.take([8, TP], BF16)
    qr32 = cv.take([TP], F32)
    rtab1 = cv.take([2 * TP], F32)
    ktok = cv.take([16, 64], BF16)
    qnrm = cv.take([8, TS], F32)
    oas = cv.take([8, TS], BF16)
    qlat = wsl[2][:, :].rearrange("p (h c t) -> p h c t", h=8, c=2)

    xb_out_aps = [t.ap() for t in xb_out]
    xf_out_ap = xf_out.ap()
    W0, W1 = wsl[0], wsl[1]
    R.dma("pool", DMA(wview(W0, 0, 4, 1024), w_uqn.rearrange("(k p) f -> p k f", p=128)), "w0", writes=("attw",))
    R.dma("pool", DMA(wview(W0, 4096, 4, 512), w_uqr.rearrange("(k p) f -> p k f", p=128)), "w0", writes=("attw",))
    R.dma("pool", DMA(wview(W0, 6144, 4, 512), w_uqrp.rearrange("(k p) f -> p k f", p=128)), "w0", writes=("attw",))
    R.dma("pool", DMA(W1[:, 0:2048], w_ukT[:, :]), "w1", writes=("attw1",))
    R.dma("pool", DMA(wview(W1, 2048, 2, 1024), w_uv.rearrange("(k p) f -> p k f", p=128)), "w1", writes=("attw1",))
    R.dma("pool", DMA(maskb[:, :, :], mask_d.rearrange("p (j t) -> p j t", j=8)), "mask", writes=("maskb",))
    wqn = wview(W0, 0, 4, 1024)
    wqr = wview(W0, 4096, 4, 512)
    wqrp = wview(W0, 6144, 4, 512)
    wuk = wview(W1, 0, 8, 256)
    wuv = wview(W1, 2048, 2, 1024)
    R.op("dve", MS(KR[64:65, :], 1.0), writes=("KRones",))

    def q_proj(ti, T, sample):
        R.dma("pool", DMA(cqn1[:, :, 0:T], cqsp[ti, :, 0:4 * T].rearrange("p (c t) -> p c t", c=4)), "ld_cq",
              reads=(f"cqsp{ti}",), writes=("cqn1",))
        R.dma("pool", DMA(rtab1[0:64, :], rope_d[ti, :, :]), "rtab1", writes=("rtab1",))

        def norm_step(h):
            sq = sq1x[h % 2]
            sk = f"sq1_{h % 2}"
            R.op("pe", [MM(ps[5][:, 0:T], onesb[:], sq[:, 0, 0:T], True, False),
                        MM(ps[5][:, 0:T], onesb[:], sq[:, 1, 0:T], False, False),
                        MM(ps[5][:, 0:T], onesb[0:64, :], sq[0:64, 2, 0:T], False, True)],
                 reads=(sk, "onesb"), writes=("ps5",))
            R.op("act", ACTF(tmp[3][64:65, 0:T], ps[5][64:65, 0:T], AF.Sqrt), reads=("ps5",), writes=("tmp3",))
            if not sample:
                R.op("dve", TSC(qr[64:65, h, 0:T], tmp[3][64:65, 0:T], negnk[64:65, 0:1], None, ALU.mult),
                     reads=("tmp3", "negnk"), writes=("qr",))
            else:
                R.op("act", ACTF(qnrm[64:65, h, 0:T], tmp[3][64:65, 0:T], AF.Copy), reads=("tmp3",), writes=("qnrm",))

        prev = None
        for h in range(8):
            sq = sq1x[h % 2]
            sk = f"sq1_{h % 2}"
            R.op("pe", [MM(ps[0][:, 0:T], wqn[:, k, h * 128:(h + 1) * 128], cqn1[:, k, 0:T], k == 0, k == 3) for k in range(4)],
                 reads=("cqn1", "attw"), writes=("ps0",))
            R.op("act", ACTF(qn[:, 0:T], ps[0][:, 0:T], AF.Copy), reads=("ps0",), writes=("qn",))
            R.op("pe", [MM(ps[3][0:64, 0:T], wqr[:, k, h * 64:(h + 1) * 64], cqn1[:, k, 0:T], k == 0, k == 3) for k in range(4)],
                 reads=("cqn1", "attw"), writes=("ps3",))
            R.op("pe", [MM(ps[4][0:64, 0:T], wqrp[:, k, h * 64:(h + 1) * 64], cqn1[:, k, 0:T], k == 0, k == 3) for k in range(4)],
                 reads=("cqn1", "attw"), writes=("ps4",))
            if prev is not None:
                norm_step(prev)
            for c in range(2):
                R.op("pe", MM(ps[1 + c][:, 0:T], wuk[:, h, c * 128:(c + 1) * 128], qn[:, 0:T], True, True),
                     reads=("qn", "attw1"), writes=(f"ps{1 + c}",))
                R.op("act", ACTF(qlat[:, h, c, 0:T], ps[1 + c][:, 0:T], AF.Copy), reads=(f"ps{1 + c}",), writes=("qlat",))
                R.op("act", ACTF(sq[:, c, 0:T], ps[1 + c][:, 0:T], AF.Square), reads=(f"ps{1 + c}",), writes=(sk,))
            t1, t2 = tmp[1][0:64, 0:T], tmp[2][0:64, 0:T]
            R.op("dve", TT(t1, ps[3][0:64, 0:T], rtab1[0:64, 0:T], ALU.mult), reads=("ps3", "rtab1"), writes=("tmp1",))
            R.op("dve", TT(t2, ps[4][0:64, 0:T], rtab1[0:64, TP:TP + T], ALU.mult), reads=("ps4", "rtab1"), writes=("tmp2",))
            R.op("dve", TT(qr32[0:64, 0:T], t1, t2, ALU.add), reads=("tmp1", "tmp2"), writes=("qr32",))
            R.op("act", ACTF(qr[0:64, h, 0:T], qr32[0:64, 0:T], AF.Copy), reads=("qr32",), writes=("qr",))
            R.op("act", ACTF(sq[0:64, 2, 0:T], qr32[0:64, 0:T], AF.Square), reads=("qr32",), writes=(sk,))
            prev = h
        norm_step(prev)

    def attend(nblocks, T, rhs_fn, kcount_fn, mask_fn, acc, tagK):
        o0, o1, lb = acc
        pend = None

        def pv(j, last):
            kc = kcount_fn(j)
            pt = PT[j % 2]
            R.op("pe", [MM(ps[o0][:, 0:T], VV[0:kc, j, 0:128], pt[0:kc, 0:T], j == 0, last),
                        MM(ps[o1][:, 0:T], VV[0:kc, j, 128:256], pt[0:kc, 0:T], j == 0, last),
                        MM(ps[lb][:, 0:T], onesb[0:kc, :], pt[0:kc, 0:T], j == 0, last)],
                 reads=(f"PT{j % 2}", "V", "onesb"), writes=(f"ps{o0}", f"ps{o1}", f"ps{lb}"))

        for j in range(nblocks):
            kc = kcount_fn(j)
            sbk = j % 2
            mk = mask_fn(j)
            fns = [MM(ps[sbk][0:kc, 0:T], KT[:, 0, j * 128:j * 128 + kc], rhs_fn(0), True, False),
                   MM(ps[sbk][0:kc, 0:T], KT[:, 1, j * 128:j * 128 + kc], rhs_fn(1), False, False),
                   MM(ps[sbk][0:kc, 0:T], KR[0:65, j * 128:j * 128 + kc], rhs_fn(2), False, mk is None)]
            rd = ["KT", "KR", "KRones", "qlat", "qr"]
            if mk is not None:
                fns.append(MM(ps[sbk][0:kc, 0:T], idb[:], mk, False, True))
                rd += ["maskb", "idb"]
            R.op("pe", fns, reads=tuple(rd), writes=(f"ps{sbk}",))
            R.op("act", ACTF(PT[sbk][0:kc, 0:T], ps[sbk][0:kc, 0:T], AF.Exp, scale=SCALE), reads=(f"ps{sbk}",), writes=(f"PT{sbk}",))
            if j == 2:
                flush()
            if pend is not None:
                pv(pend, False)
            pend = j
        pv(pend, True)

    def finish_head(T, acc, out_fn):
        o0, o1, lb = acc
        rinv = tmp[4][:, 0:T]
        R.op("dve", RCP(rinv, ps[lb][:, 0:T]), reads=(f"ps{lb}",), writes=("tmp4",))
        R.op("dve", TT(olat[:, 0, 0:T], ps[o0][:, 0:T], rinv, ALU.mult), reads=(f"ps{o0}", "tmp4"), writes=("olat",))
        R.op("dve", TT(olat[:, 1, 0:T], ps[o1][:, 0:T], rinv, ALU.mult), reads=(f"ps{o1}", "tmp4"), writes=("olat",))
        pending.append(out_fn)

    pending = []

    def flush():
        while pending:
            pending.pop(0)()

    q_proj(NT, TS, True)
    ACC_A, ACC_B = (2, 3, 4), (5, 6, 7)
    for s in range(4):
        R.dma("pool", DMA(VV[:, 0:16, :], ckvc[s].rearrange("(j p) c -> p j c", p=128)), "ld_vc", writes=("V",))
        R.dma("pool", DMA(ktok[:, :, :], ckrc[s].rearrange("(j p) c -> p j c", p=128)), "ld_kc", writes=("ktok",))
        R.op("act", ACTF(VV[0:64, 16, :], s_vtok[:, s, :], AF.Copy), reads=("vtokb", "V"), writes=("V",))
        gi = 0
        for jg in range(4):
            for c in range(2):
                bnk = 6 + (gi % 2)
                gi += 1
                R.op("pe", [TR(psb(bnk)[:, i * 128:(i + 1) * 128], VV[:, jg * 4 + i, c * 128:(c + 1) * 128], idb[:]) for i in range(4)],
                     reads=("V", "idb"), writes=(f"ps{bnk}",))
                R.op("dve", CP(KT[:, c, jg * 512:(jg + 1) * 512], psb(bnk)[:, 0:512]), reads=(f"ps{bnk}",), writes=("KT",))
            bnk = 6 + (gi % 2)
            gi += 1
            R.op("pe", [TR(psb(bnk)[0:64, i * 128:(i + 1) * 128], ktok[:, jg * 4 + i, :], idb[:]) for i in range(4)],
                 reads=("ktok", "idb"), writes=(f"ps{bnk}",))
            R.op("dve", CP(KR[0:64, jg * 512:(jg + 1) * 512], psb(bnk)[0:64, 0:512]), reads=(f"ps{bnk}",), writes=("KR",))
        R.op("act", ACTF(KT[:, :, 2048:2112], s_kvb[:, :, s * 64:(s + 1) * 64], AF.Copy), reads=("kvb", "KT"), writes=("KT",))
        R.op("act", ACTF(KR[0:64, 2048:2112], s_krb[0:64, s * 64:(s + 1) * 64], AF.Copy), reads=("krb", "KR"), writes=("KR",))
        for g5 in range(5):
            c0 = g5 * 512
            wd = 512 if g5 < 4 else 64
            R.op("act", ACTF(sq1[:, 0:2, 0:wd], KT[:, :, c0:c0 + wd], AF.Square), reads=("KT",), writes=("sq1_0",))
            R.op("act", ACTF(sq1[0:64, 2, 0:wd], KR[0:64, c0:c0 + wd], AF.Square), reads=("KR",), writes=("sq1_0",))
            R.op("pe", [MM(ps[5][:, 0:wd], onesb[:], sq1[:, 0, 0:wd], True, False),
                        MM(ps[5][:, 0:wd], onesb[:], sq1[:, 1, 0:wd], False, False),
                        MM(ps[5][:, 0:wd], onesb[0:64, :], sq1[0:64, 2, 0:wd], False, True)],
                 reads=("sq1_0", "onesb"), writes=("ps5",))
            R.op("dve", RED(nkt[:, g5:g5 + 1], ps[5][:, 0:wd], ALU.max), reads=("ps5",), writes=("nkt",))
        R.op("dve", RED(nkt[:, 7:8], nkt[:, 0:5], ALU.max), reads=("nkt",), writes=("nkt7",))
        R.op("act", ACTF(nkt[:, 7:8], nkt[:, 7:8], AF.Sqrt), reads=("nkt7",), writes=("nkt7",))
        R.op("dve", TSC(negnk[:, 1 + s:2 + s], nkt[:, 7:8], -1.03, None, ALU.mult), reads=("nkt7",), writes=("negnk",))
        R.op("dve", TSC(qr[64:65, :, s * 64:(s + 1) * 64], qnrm[64:65, :, s * 64:(s + 1) * 64], negnk[64:65, 1 + s:2 + s], None, ALU.mult),
             reads=("qnrm", "negnk", "qr"), writes=("qr",))
        acc = ACC_A
        attend(17, 512,
               lambda c, s=s: (qlat[:, :, c, s * 64:(s + 1) * 64] if c < 2 else qr[0:65, :, s * 64:(s + 1) * 64]),
               lambda j: 128 if j < 16 else 64, lambda j: None, acc, "s")

        def proj_s(s=s, acc=acc):
            fns = []
            for h in range(8):
                for c in range(2):
                    fns.append(MM(ps[5][:, h * 64:(h + 1) * 64], wuv[:, c, h * 128:(h + 1) * 128], olat[:, c, h * 64:(h + 1) * 64], c == 0, c == 1))
            R.op("pe", fns, reads=("olat", "attw1"), writes=("ps5",))
            R.op("act", ACTF(oas[:, :, s * 64:(s + 1) * 64], ps[5][:, 0:512].rearrange("p (h t) -> p h t", h=8), AF.Copy),
                 reads=("ps5",), writes=("oas",))
        finish_head(512, acc, proj_s)
        flush()
    R.dma("pool", DMA(oasp[NT, :, 0:8 * TS].rearrange("p (h t) -> p h t", h=8), oas[:, :, :]), "st_oas", reads=("oas",), writes=(f"oasp{NT}",))

    if STAGE == 5:
        return finish()
    if not KSAMPLE:
        for r in range(2):
            for li in range(NT):
                gt = 2 * li + r
                base = (r * 2 + (li % 2)) * XBR
                xb_out_ap = xb_out_aps[li // 2]
                xbk = f"xb_out{li // 2}"
                R.dma("pool", DMA(KT[:, :, gt * 512:(gt + 1) * 512], xb_out_ap[base:base + 128, :].rearrange("p (c t) -> p c t", c=2)),
                      "kvres", reads=(xbk,), writes=("KT",), join=True)
                R.dma("pool", DMA(VV[:, gt * 4:(gt + 1) * 4, :], xb_out_ap[base + 128:base + 256, :].rearrange("p (b c) -> p b c", b=4)),
                      "kvres", reads=(xbk,), writes=("V",), join=True)
                R.dma("pool", DMA(KR[0:64, gt * 512:(gt + 1) * 512], xb_out_ap[base + 256:base + 288, :].rearrange("r (s c) -> (r s) c", s=2)),
                      "kvres", reads=(xbk,), writes=("KR",), join=True)
        tk = R.lastw["KR"]
        R.lastw["KT"] = tk
        R.lastw["V"] = tk
        R.dma("pool", DMAS(nkt[:, 5:6], xf_out_ap[NT * 128 + 128:NT * 128 + 256, 0:1]), "ld_nk", reads=("xf_out",), writes=("nkt56",))
        R.dma("pool", DMAS(nkt[:, 6:7], xf_out_ap[XFR + NT * 128 + 128:XFR + NT * 128 + 256, 0:1]), "ld_nk", reads=("xf_out",), writes=("nkt56",))
        R.op("dve", TT(nkt[:, 7:8], nkt[:, 5:6], nkt[:, 6:7], ALU.max), reads=("nkt56", "nkt7"), writes=("nkt7",))
        R.op("act", ACTF(nkt[:, 7:8], nkt[:, 7:8], AF.Sqrt), reads=("nkt7",), writes=("nkt7",))
        R.op("dve", TSC(negnk[:, 0:1], nkt[:, 7:8], -1.03, None, ALU.mult), reads=("nkt7",), writes=("negnk",))

    for li in range(0 if KSAMPLE else NT):
        flush()
        q_proj(li, TP, False)
        nblocks = 8 * li + 8
        for h in range(8):
            acc = ACC_A if h % 2 == 0 else ACC_B
            attend(nblocks, TP,
                   lambda c, h=h: (qlat[:, h, c, :] if c < 2 else qr[0:65, h, :]),
                   lambda j: 128,
                   lambda j, li=li: (maskb[:, j - 8 * li, :] if j >= 8 * li else None), acc, "p")

            def proj_p(h=h, acc=acc, li=li):
                o0 = acc[0]
                R.op("pe", [MM(ps[o0][:, 0:TP], wuv[:, c, h * 128:(h + 1) * 128], olat[:, c, :], c == 0, c == 1) for c in range(2)],
                     reads=("olat", "attw1"), writes=(f"ps{o0}",))
                ob = oab[h % 2]
                R.op("act", ACTF(ob[:, :], ps[o0][:, 0:TP], AF.Copy), reads=(f"ps{o0}",), writes=(f"oab{h % 2}",))
                R.dma("pool", DMA(oasp[li, :, h * TP:(h + 1) * TP], ob[:, :]), f"st_oa{h % 2}", reads=(f"oab{h % 2}",), writes=(f"oasp{li}",))
            finish_head(TP, acc, proj_p)
    flush()

    R.barrier()

    cv = Carver()
    x32 = cv.take([16, TP], F32)
    xb = cv.take([16, TP], BF16)
    aT = cv.take([11, TP], BF16)
    lnsq = cv.take([8, TP], BF16)
    uext = cv.take([8, 528], F32)
    sA = cv.take([528], F32)
    sB = cv.take([528], F32)
    dT = cv.take([8, TP], BF16)
    h0 = cv.take([8, 16], F32)
    h1 = cv.take([8, 16], F32)
    ptab = cv.take([64], F32)
    ptok = cv.take([4, 256], F32)
    pT = cv.take([2, TP], BF16)
    ytok = [cv.take([D], F32), cv.take([D], F32)]
    wppb = cv.take([2, D], BF16)
    R.dma("pool", DMA(wppb[:, :, :], w_pp.rearrange("(k p) d -> p k d", p=128)), "ld_wpp", writes=("wppb",))

    def layer_norm2(T, lnidx, post_scale, pre=None):
        layer_norm(T, x32, xb, lnidx, "x32_", "xb_", post_scale, pre=pre)

    pool_ready = set()

    def pool_prepare(ti, T, is_sample):
        if ti in pool_ready:
            return []
        pool_ready.add(ti)
        nseq = 4 if is_sample else 1
        L = T // nseq
        E = 16 + L
        ue = uext[:, :, 0:nseq * E].rearrange("p c (s e) -> p c s e", s=nseq)
        for s in range(nseq):
            R.dma("pool", DMA(ue[:, :, s, 16:E], usp[ti, :, 0:8 * T].rearrange("p (c t) -> p c t", c=8)[:, :, s * L:(s + 1) * L]),
                  "ld_u", reads=(f"usp{ti}",), writes=("uext",))
        R.dma("pool", DMA(ptab[:, :], ptab_d[ti, :, :]), "ld_ptab", writes=("ptab",))
        if not is_sample:
            zrow = NT * 128
            r1row = XFR + (ti - 1) * 128 if ti > 0 else zrow
            R.dma("pool", DMA(h0[:, :, :], xf_out_ap[r1row:r1row + 128, :].rearrange("p (c t) -> p c t", c=8)), "ld_h0",
                  reads=("xf_out",), writes=("h0",))
            R.dma("pool", DMA(h1[:, :, :], xf_out_ap[ti * 128:(ti + 1) * 128, :].rearrange("p (c t) -> p c t", c=8)), "ld_h1",
                  reads=("xf_out",), writes=("h1",))
            R.op("dve", TSC(h0[:, :, :], h0[:, :, :], sel[:, 0:1], None, ALU.mult), reads=("h0", "sel"), writes=("h0",))
            R.op("dve", STT(ue[:, :, 0, 0:16], h1[:, :, :], sel[:, 1:2], h0[:, :, :], ALU.mult, ALU.add),
                 reads=("h0", "h1", "sel", "uext"), writes=("uext",))
        else:
            hv = histT.rearrange("p (c s t) -> p c s t", c=8, s=4)
            for c in range(8):
                R.dma("pool", DMA(ue[:, c, :, 0:16], hv[:, c, :, :]), "ld_u", reads=(), writes=("uext",))
        def pool_chunk(c):
            g = c // 2
            u_c = ue[:, c, :, :]
            cur, curk = u_c, "uext"
            bufs = [(sA, "sA"), (sB, "sB")]
            sh = 1
            for step in range(g + 1):
                dstt, dk = bufs[step % 2]
                dst = dstt[:, 0:nseq * E].rearrange("p (s e) -> p s e", s=nseq)
                lo = 2 * sh - 1
                R.op("dve", TT(dst[:, :, lo:E], cur[:, :, lo:E], cur[:, :, lo - sh:E - sh], ALU.add), reads=(curk,), writes=(dk,))
                cur, curk = dst, dk
                sh *= 2
            w = WINS[g]
            dv = dT[:, c, 0:T].rearrange("p (s l) -> p s l", s=nseq)
            R.op("dve", STT(dv, cur[:, :, 16:E], 1.0 / w, u_c[:, :, 16:E], ALU.mult, ALU.subtract), reads=(curk, "uext"), writes=(f"dT{c}",))
            if not is_sample:
                tf = tmp[5][:, 0:16]
                R.op("dve", TT(tf, cur[:, 0, 16:32], ptab[:, g * 16:(g + 1) * 16], ALU.mult), reads=(curk, "ptab"), writes=("tmp5",))
                R.op("dve", TT(dT[:, c, 0:16], tf, u_c[:, 0, 16:32], ALU.subtract), reads=("tmp5", "uext"), writes=(f"dT{c}",))

        return [(lambda c=c: pool_chunk(c)) for c in range(8)]

    def phaseB2(ti, T, p_tok, y_out, is_sample, nxt=None):
        nseq = 4 if is_sample else 1
        L = T // nseq
        E = 16 + L
        xkeys = tuple(f"x32_{c}" for c in range(NCH))
        R.dma("pool", DMA(x32[:, :, 0:T], x1sp[ti, :, 0:NCH * T].rearrange("p (c t) -> p c t", c=NCH)), "ld_x1",
              reads=(f"x1sp{ti}",), writes=xkeys)
        R.dma("pool", DMA(xb[:, 0:8, 0:T], oasp[ti, :, 0:8 * T].rearrange("p (h t) -> p h t", h=8)), "ld_oa",
              reads=(f"oasp{ti}",), writes=tuple(f"xb_{c}" for c in range(8)))
        for blk in range(T // 128):
            R.dma("pool", DMA(ptok[:, blk, :], p_tok[blk * 128:(blk + 1) * 128, :]), "ld_p", writes=("ptok",))
        for f_ in pool_prepare(ti, T, is_sample):
            f_()
        for blk in range(T // 128):
            bnk = blk % 2
            R.op("pe", [TR(ps[bnk][:, i * 128:(i + 1) * 128], ptok[:, blk, i * 128:(i + 1) * 128], id32[:]) for i in range(2)],
                 reads=("ptok", "id32"), writes=(f"ps{bnk}",))
            R.op("dve", CP(pT[:, :, blk * 128:(blk + 1) * 128], ps[bnk][:, 0:256].rearrange("p (a b) -> p a b", a=2)),
                 reads=(f"ps{bnk}",), writes=("pT",))
        slot, wk = wload([(0, w_pool.rearrange("(k p) d -> p k d", p=128))], bid=("wpool",))
        wp = wview(slot, 0, 8, 256)
        for g in range(4):
            for dc in range(2):
                bnk = (2 * g + dc) % 4
                R.op("pe", [MM(ps[bnk][:, 0:T], wp[:, 2 * g + cc, dc * 128:(dc + 1) * 128], dT[:, 2 * g + cc, 0:T], cc == 0, cc == 1) for cc in range(2)],
                     reads=(f"dT{2 * g}", f"dT{2 * g + 1}", wk), writes=(f"ps{bnk}",))
                R.op("dve", TSC(xb[:, 8 + 2 * g + dc, 0:T], ps[bnk][:, 0:T], psc[:, 2 * g + dc:2 * g + dc + 1], None, ALU.mult),
                     reads=(f"ps{bnk}", "psc"), writes=(f"xb_{8 + 2 * g + dc}",))
        xbkeys = tuple(f"xb_{c}" for c in range(NCH))
        wov = w_o.rearrange("(k p) f -> p k f", p=128)
        hk2 = StatsHook(T, x32, "x32_", 0, 1)
        for dg in range(4):
            slot, wk = wload([(0, wov[:, :, dg * 512:(dg + 1) * 512])], bid=("wo", dg))
            wv = wview(slot, 0, 16, 512)
            for dc in range(4):
                c = dg * 4 + dc
                bnk = 4 + dc
                R.op("pe", [MM(ps[bnk][:, 0:T], wv[:, k, dc * 128:(dc + 1) * 128], xb[:, k, 0:T], k == 0, k == 15) for k in range(16)],
                     reads=xbkeys + (wk,), writes=(f"ps{bnk}",))
                R.op("dve", STT(x32[:, c, 0:T], x32[:, c, 0:T], ALPHA, ps[bnk][:, 0:T], ALU.mult, ALU.add),
                     reads=(f"ps{bnk}", f"x32_{c}"), writes=(f"x32_{c}",))
                hk2.chunk(c)
        def dbg(k):
            if KSUB == k:
                R.dma("pool", DMA(x1sp[ti, :, 0:NCH * T].rearrange("p (c t) -> p c t", c=NCH), x32[:, :, 0:T]), "st_dbg", reads=xkeys)
                raise _Stop()
        dbg(20)
        layer_norm2(T, 1, ALPHA, pre=hk2)
        dbg(21)
        bg = pool_prepare(*nxt) if nxt is not None else []
        hk3 = StatsHook(T, x32, "x32_", 0, 1)
        ffn(T, x32, xb, aT, w_f2gu, w_f2d, "x32_", "xb_", "f2", hook=hk3, bgw=bg)
        layer_norm2(T, 2, ALPHA, pre=hk3)
        dbg(22)
        wpp, wkp = wppb, "wppb"
        wgv = w_pg.rearrange("(k p) f -> p k f", p=128)
        hk4 = StatsHook(T, x32, "x32_", 4, 5)
        for dg in range(4):
            slot, wk = wload([(0, wgv[:, :, dg * 512:(dg + 1) * 512])], bid=("wpg", dg))
            wv = wview(slot, 0, 16, 512)
            for dc in range(4):
                c = dg * 4 + dc
                par = c % 2
                bg, bp = 2 * par, 2 * par + 1
                R.op("pe", [MM(ps[bg][:, 0:T], wv[:, k, dc * 128:(dc + 1) * 128], xb[:, k, 0:T], k == 0, k == 15) for k in range(16)],
                     reads=xbkeys + (wk,), writes=(f"ps{bg}",))
                R.op("pe", [MM(ps[bp][:, 0:T], wpp[:, k, c * 128:(c + 1) * 128], pT[:, k, 0:T], k == 0, k == 1) for k in range(2)],
                     reads=("pT", wkp), writes=(f"ps{bp}",))
                tg = tmp[par][:, 0:T]
                R.op("act", ACTF(tg, ps[bg][:, 0:T], AF.Sigmoid), reads=(f"ps{bg}",), writes=(f"tmp{par}",))
                R.op("dve", TT(tg, tg, ps[bp][:, 0:T], ALU.mult), reads=(f"tmp{par}", f"ps{bp}"), writes=(f"tmp{par}",))
                R.op("dve", TT(x32[:, c, 0:T], x32[:, c, 0:T], tg, ALU.add), reads=(f"tmp{par}", f"x32_{c}"), writes=(f"x32_{c}",))
                hk4.chunk(c)
        dbg(23)
        layer_norm2(T, 3, None, pre=hk4)
        for blk in range(T // 128):
            yt = ytok[blk % 2]
            ytk = f"ytok{blk % 2}"
            for g in range(4):
                bnk = 4 + g
                R.op("pe", [TR(ps[bnk][:, i * 128:(i + 1) * 128], x32[:, 4 * g + i, blk * 128:(blk + 1) * 128], id32[:]) for i in range(4)],
                     reads=tuple(f"x32_{4 * g + i}" for i in range(4)) + ("id32",), writes=(f"ps{bnk}",))
                if g % 2 == 0:
                    R.op("act", ACTF(yt[:, g * 512:(g + 1) * 512], ps[bnk][:, :], AF.Copy), reads=(f"ps{bnk}",), writes=(ytk,))
                else:
                    R.op("dve", CP(yt[:, g * 512:(g + 1) * 512], ps[bnk][:, :]), reads=(f"ps{bnk}",), writes=(ytk,))
            R.dma("pool", DMA(y_out[blk * 128:(blk + 1) * 128, :], yt[:, :]), f"st_y{blk % 2}", reads=(ytk,))

    if STAGE == 6:
        return finish()
    order = [(NT, TS, pss, ys, True)] if KSAMPLE else \
        [(0, TP, pp[0], yp[0], False), (NT, TS, pss, ys, True)] + [(li, TP, pp[li], yp[li], False) for li in range(1, NT)]
    try:
        for i, (ti_, T_, p_, y_, smp_) in enumerate(order):
            nxt = (order[i + 1][0], order[i + 1][1], order[i + 1][4]) if i + 1 < len(order) else None
            phaseB2(ti_, T_, p_, y_, smp_, nxt=nxt)
            if STAGE == 7 and smp_:
                return finish()
    except _Stop:
        return finish()
    R.barrier()
    return nc, R, es


def emit(nc, R, es):
    sems = {}
    for name in R.semnames:
        sems[name] = es.enter_context(nc.semaphore(name))
    block = es.enter_context(nc.Block())

    def replay(eng, q):
        for it in q:
            if it[0] == "w":
                eng.wait_ge(sems[it[1]], it[2])
            else:
                ins = it[1](eng)
                if it[2] is not None:
                    if it[3] is None:
                        ins.then_inc(sems[it[2]])
                    else:
                        ins.then_inc(sems[it[2]], it[3])

    @block.tensor
    def _(e):
        replay(e, R.q["pe"])

    @block.scalar
    def _(e):
        replay(e, R.q["act"])

    @block.vector
    def _(e):
        replay(e, R.q["dve"])

    @block.gpsimd
    def _(e):
        replay(e, R.q["pool"])

    @block.sync
    def _(e):
        replay(e, R.q["sp"])


_CACHE = {}


def get_program():
    if "nc" not in _CACHE:
        nc, R, es = build_program()
        emit(nc, R, es)
        es.close()
        _CACHE["nc"] = nc
    return _CACHE["nc"]


def _rope_tab(pos):
    inv = (1.0 / (np.float32(10000.0) ** (np.arange(0, 64, 2, dtype=np.float32) / np.float32(64)))).astype(np.float32)
    ang = pos.astype(np.float32)[:, None] * inv[None, :]
    ang = np.concatenate([ang, ang], -1)
    cos = np.cos(ang).astype(np.float32).T
    sin = np.sin(ang).astype(np.float32).T
    sin_signed = sin.copy()
    sin_signed[0:32] *= -1.0
    return cos, sin_signed


def kernel(x_prompt, x_sample, cache_kv_latent, cache_k_rope, state_pool, p_prompt, p_sample,
           ffn1_gu, ffn1_down, ln1_g, ln1_b, w_in, g_q, g_kv, w_uq_nope, w_uq_rope, w_uk, w_uv,
           w_pool, pool_scale, w_o, ln2_g, ln2_b, ffn2_gu, ffn2_down, ln3_g, ln3_b,
           w_ple_gate, w_ple_proj, ln4_g, ln4_b):
    f32 = np.float32
    A = lambda a: np.ascontiguousarray(np.asarray(a, dtype=f32))
    x_prompt = np.asarray(x_prompt, f32); x_sample = np.asarray(x_sample, f32)
    p_prompt = np.asarray(p_prompt, f32); p_sample = np.asarray(p_sample, f32)
    cache_kv_latent = np.asarray(cache_kv_latent, f32); cache_k_rope = np.asarray(cache_k_rope, f32)
    state_pool = np.asarray(state_pool, f32)
    perm = (np.arange(64) + 32) % 64
    w_in0 = np.asarray(w_in, f32)[0]
    w_uqr0 = np.asarray(w_uq_rope, f32)[0]
    fm = lambda v, n: A(np.asarray(v, f32).reshape(n, 128).T)
    lnp = np.concatenate([fm(ln1_g[0], 16), fm(ln1_b[0], 16), fm(ln2_g[0], 16), fm(ln2_b[0], 16),
                          fm(ln3_g[0], 16), fm(ln3_b[0], 16), fm(ln4_g[0], 16), fm(ln4_b[0], 16)], axis=1)
    shared = {
        "ffn1_gu": A(ffn1_gu[0]), "ffn1_down": A(ffn1_down[0]), "ffn2_gu": A(ffn2_gu[0]), "ffn2_down": A(ffn2_down[0]),
        "w_in": A(w_in0), "w_in_perm": A(w_in0[:, 768:832][:, perm]),
        "w_uqn": A(np.asarray(w_uq_nope, f32)[0].reshape(512, 1024)),
        "w_uqr": A(w_uqr0.reshape(512, 512)), "w_uqr_perm": A(w_uqr0[:, :, perm].reshape(512, 512)),
        "w_ukT": A(np.asarray(w_uk, f32)[0].transpose(2, 1, 0).reshape(128, 2048)),
        "w_uv": A(np.asarray(w_uv, f32)[0].reshape(256, 1024)),
        "w_pool": A(np.asarray(w_pool, f32)[0].reshape(1024, 256)), "w_o": A(w_o[0]),
        "w_ple_gate": A(w_ple_gate[0]), "w_ple_proj": A(w_ple_proj[0]),
        "lnp": A(lnp), "gq": fm(g_q[0], 4), "gkv": fm(g_kv[0], 2), "pscale": fm(pool_scale[0], 8),
        "ident32": np.eye(128, dtype=f32), "identb": np.eye(128, dtype=f32).astype(ml_dtypes.bfloat16),
    }
    in_maps = []
    for c in range(8):
        b, h = c // 2, c % 2
        m = dict(shared)
        m["xp"] = A(x_prompt[b].reshape(16, TP, D)[h::2])
        m["pp"] = A(p_prompt[0, b].reshape(16, TP, 256)[h::2])
        m["xs"] = A(x_sample[4 * c:4 * c + 4].reshape(TS, D))
        m["ps"] = A(p_sample[0, 4 * c:4 * c + 4].reshape(TS, 256))
        m["ckvc"] = A(cache_kv_latent[0, 4 * c:4 * c + 4])
        m["ckrc"] = A(cache_k_rope[0, 4 * c:4 * c + 4])
        hist = state_pool[0, 4 * c:4 * c + 4]
        hp = np.zeros((4, 16, 1024), f32)
        hp[:, 1:16] = hist
        m["histT"] = A(hp.reshape(4, 16, 8, 128).transpose(3, 2, 0, 1).reshape(128, 8 * 4 * 16))
        rt = np.zeros((NT + 1, 64, 2 * TP), f32)
        ptb = np.zeros((NT + 1, 128, 64), f32)
        for li in range(NT):
            gt = 2 * li + h
            cos, sn = _rope_tab(gt * TP + np.arange(TP))
            rt[li, :, 0:TP] = cos
            rt[li, :, TP:] = sn
            for g, w in enumerate(WINS):
                pos = gt * TP + np.arange(16)
                ptb[li, :, g * 16:(g + 1) * 16] = (1.0 / np.minimum(w, pos + 1).astype(f32))[None, :]
        cos, sn = _rope_tab(2048 + (np.arange(TS) % 64))
        rt[NT, :, 0:TS] = cos
        rt[NT, :, TP:TP + TS] = sn
        for g, w in enumerate(WINS):
            ptb[NT, :, g * 16:(g + 1) * 16] = 1.0 / w
        m["ropetab"] = rt
        m["pooltab"] = ptb
        mk = np.zeros((128, 8, TP), f32)
        kk = np.arange(128)[:, None]
        qq = np.arange(TP)[None, :]
        for jj in range(8):
            key_chunk = jj * 2 + kk // 64
            q_chunk = h * 8 + qq // 64
            mk[:, jj, :] = np.where(key_chunk <= q_chunk, 0.0, NEG)
        m["maskb"] = A(mk.reshape(128, 8 * TP)).astype(ml_dtypes.bfloat16)
        sel = np.zeros((128, 2), f32)
        sel[:, h] = 1.0
        m["sel"] = sel
        in_maps.append(m)

    nc = get_program()
    res = run_bass_kernel_spmd(nc, in_maps[:KCORES], core_ids=list(range(KCORES)))
    outs = list(res.results) + [res.results[0]] * (8 - KCORES)
    y_p = np.zeros((4, 8192, D), f32); y_s = np.zeros((32, 64, D), f32)
    kv_p = np.zeros((1, 4, 8192, 256), f32); kr_p = np.zeros((1, 4, 8192, 64), f32); pool_p = np.zeros((1, 4, 15, 1024), f32)
    kv_s = np.zeros((1, 32, 64, 256), f32); kr_s = np.zeros((1, 32, 64, 64), f32); pool_s = np.zeros((1, 32, 15, 1024), f32)
    for c in range(8):
        b, h = c // 2, c % 2
        o = outs[c]
        y_p[b].reshape(16, TP, D)[h::2] = np.asarray(o["yp"], f32)
        kv_p[0, b].reshape(16, TP, 256)[h::2] = np.asarray(o["okvp"], f32)
        kr_p[0, b].reshape(16, TP, 64)[h::2] = np.asarray(o["okrp"], f32)
        if h == 1:
            pool_p[0, b] = np.asarray(o["opoolp"], f32)[1:16]
        y_s[4 * c:4 * c + 4] = np.asarray(o["ys"], f32).reshape(4, 64, D)
        kv_s[0, 4 * c:4 * c + 4] = np.asarray(o["okvs"], f32).reshape(4, 64, 256)
        kr_s[0, 4 * c:4 * c + 4] = np.asarray(o["okrs"], f32).reshape(4, 64, 64)
        pool_s[0, 4 * c:4 * c + 4] = np.asarray(o["opools"], f32).reshape(16, 4, 1024).transpose(1, 0, 2)[:, 1:16]
    return (y_p, y_s, kv_p, kr_p, pool_p, kv_s, kr_s, pool_s)
```

```python
import contextlib
import numpy as np
import ml_dtypes
import concourse.bass as bass
import concourse.mybir as mybir
from concourse.bass_utils import run_bass_kernel_spmd

F32 = mybir.dt.float32
BF16 = mybir.dt.bfloat16
AF = mybir.ActivationFunctionType
ALU = mybir.AluOpType
AX = mybir.AxisListType

D = 2048
DFF = 5632
NCH = 16
ALPHA = 2.0 ** 0.25
SCALE = 192.0 ** -0.5
LN_EPS = 1e-5
RMS_EPS = 1e-6
NT = 8
TP = 512
TS = 256
NEG = -30000.0
WINS = (2, 4, 8, 16)
SLABS = [(0, 11), (11, 11), (22, 11), (33, 11)]


class Rec:
    ENG = ["pe", "act", "dve", "pool", "sp"]

    def __init__(self):
        self.q = {e: [] for e in self.ENG}
        self.cnt = {e: 0 for e in self.ENG}
        self.epoch = {e: 0 for e in self.ENG}
        self.seen = {e: {} for e in self.ENG}
        self.lastw = {}
        self.readers = {}
        self.chan = {}
        self.semnames = []

    def _sem(self, name):
        if name not in self.semnames:
            self.semnames.append(name)
        return name

    def _tok_eng(self, e):
        if self.cnt[e] >= 20000:
            self.epoch[e] += 1
            self.cnt[e] = 0
        self.cnt[e] += 1
        return (self._sem(f"e_{e}_{self.epoch[e]}"), self.cnt[e], e)

    def _deps(self, reads, writes):
        toks = []
        for k in reads:
            t = self.lastw.get(k)
            if t is not None:
                toks.append(t)
        for k in writes:
            t = self.lastw.get(k)
            if t is not None:
                toks.append(t)
            toks.extend(self.readers.get(k, {}).values())
        return toks

    def _commit(self, tok, reads, writes):
        for k in reads:
            d = self.readers.setdefault(k, {})
            o = d.get(tok[0])
            if o is None or o[1] < tok[1]:
                d[tok[0]] = tok
        for k in writes:
            self.lastw[k] = tok
            self.readers[k] = {}

    def _emit_waits(self, e, toks):
        for (name, val, src) in toks:
            if src == e and e == "pe":
                continue
            if self.seen[e].get(name, 0) >= val:
                continue
            self.seen[e][name] = val
            self.q[e].append(("w", name, val))

    @staticmethod
    def _excl(reads, writes):
        extra = tuple(k for k in reads if k.startswith("ps") and k[2:].isdigit() and k not in writes)
        return tuple(reads), tuple(writes) + extra

    def op(self, e, fns, reads=(), writes=()):
        if callable(fns):
            fns = [fns]
        reads, writes = self._excl(reads, writes)
        self._emit_waits(e, self._deps(reads, writes))
        tok = self._tok_eng(e)
        for f in fns[:-1]:
            self.q[e].append(("i", f, None, 0))
        self.q[e].append(("i", fns[-1], tok[0], 1))
        self._commit(tok, reads, writes)
        return tok

    def dma(self, e, fn, chan, reads=(), writes=(), join=False):
        toks = self._deps(reads, writes)
        if join:
            toks = [t for t in toks if t[0] != f"d_{chan}"]
        self._emit_waits(e, toks)
        n = self.chan.get(chan, 0) + 1
        self.chan[chan] = n
        tok = (self._sem(f"d_{chan}"), 16 * n, "dma")
        self.q[e].append(("i", fn, tok[0], 16))
        self._commit(tok, reads, writes)
        return tok

    def coll(self, fn, chan, reads=(), writes=()):
        self._emit_waits("pool", self._deps(reads, writes))
        tok = (self._sem(f"c_{chan}"), 1, "cc")
        self.q["pool"].append(("i", fn, tok[0], None))
        self._commit(tok, reads, writes)
        return tok

    def all_tokens(self):
        toks = []
        for e in self.ENG:
            for ep in range(self.epoch[e] + 1):
                name = f"e_{e}_{ep}"
                if name in self.semnames:
                    toks.append((name, self.cnt[e] if ep == self.epoch[e] else 20000, e))
        for ch, n in self.chan.items():
            toks.append((f"d_{ch}", 16 * n, "dma"))
        return toks

    def barrier(self):
        toks = self.all_tokens()
        for e in self.ENG:
            self._emit_waits(e, [t for t in toks if not (t[2] == e)])
        self.lastw = {}
        self.readers = {}


def MM(out, l, r, st, sp):
    return lambda e: e.matmul(out, lhsT=l, rhs=r, start=st, stop=sp)


def TR(out, in_, ident):
    return lambda e: e.transpose(out, in_, ident)


def ACTF(out, in_, func, scale=None, bias=None):
    def f(e):
        kw = {}
        if scale is not None:
            kw["scale"] = scale
        if bias is not None:
            kw["bias"] = bias
        return e.activation(out=out, in_=in_, func=func, **kw)
    return f


def TT(out, a, b, op):
    return lambda e: e.tensor_tensor(out=out, in0=a, in1=b, op=op)


def TSC(out, a, s1, s2, op0, op1=None):
    def f(e):
        if op1 is None:
            return e.tensor_scalar(out=out, in0=a, scalar1=s1, scalar2=None, op0=op0)
        return e.tensor_scalar(out=out, in0=a, scalar1=s1, scalar2=s2, op0=op0, op1=op1)
    return f


def STT(out, a, s, b, op0, op1):
    return lambda e: e.scalar_tensor_tensor(out=out, in0=a, scalar=s, in1=b, op0=op0, op1=op1)


def CP(out, in_):
    return lambda e: e.tensor_copy(out=out, in_=in_)


def MS(ap, v):
    return lambda e: e.memset(ap, v)


def RCP(out, in_):
    return lambda e: e.reciprocal(out=out, in_=in_)


def RED(out, in_, op):
    return lambda e: e.tensor_reduce(out=out, in_=in_, axis=AX.X, op=op)


def DMA(out, in_):
    return lambda e: e.dma_start(out=out, in_=in_)


def DMAS(out, in_):
    return lambda e: e.dma_start(out=out, in_=in_, allow_slow_non_contiguous=True)


import os
STAGE = int(os.environ.get("KSTAGE", "99"))
KSUB = int(os.environ.get("KSUB", "99"))
KCORES = int(os.environ.get("KCORES", "8"))
KSAMPLE = int(os.environ.get("KSAMPLE", "0"))
KNT = int(os.environ.get("KNT", "8"))


class _Stop(Exception):
    pass


def sub(k):
    if KSUB == k:
        raise _Stop()


def build_program():
    nc = bass.Bass("TRN2", target_bir_lowering=False)
    R = Rec()
    es = contextlib.ExitStack()

    def finish():
        R.barrier()
        return nc, R, es

    def din(name, shape, dt=F32):
        return nc.dram_tensor(name, list(shape), dt, kind="ExternalInput").ap()

    def dout(name, shape, dt=F32):
        return nc.dram_tensor(name, list(shape), dt, kind="ExternalOutput").ap()

    def dscr(name, shape, dt=F32):
        return nc.dram_tensor(name, list(shape), dt).ap()

    xp = din("xp", [NT, TP, D]); xs = din("xs", [TS, D])
    pp = din("pp", [NT, TP, 256]); pss = din("ps", [TS, 256])
    ckvc = din("ckvc", [4, 2048, 256]); ckrc = din("ckrc", [4, 2048, 64])
    histT = din("histT", [128, 8 * 4 * 16])
    w_f1gu = din("ffn1_gu", [D, 2 * DFF]); w_f1d = din("ffn1_down", [DFF, D])
    w_f2gu = din("ffn2_gu", [D, 2 * DFF]); w_f2d = din("ffn2_down", [DFF, D])
    w_in = din("w_in", [D, 1856]); w_inp = din("w_in_perm", [D, 64])
    w_uqn = din("w_uqn", [512, 1024]); w_uqr = din("w_uqr", [512, 512]); w_uqrp = din("w_uqr_perm", [512, 512])
    w_ukT = din("w_ukT", [128, 8 * 256]); w_uv = din("w_uv", [256, 1024])
    w_pool = din("w_pool", [4 * 256, 256]); w_o = din("w_o", [D, D])
    w_pg = din("w_ple_gate", [D, D]); w_pp = din("w_ple_proj", [256, D])
    lnp_d = din("lnp", [128, 8 * 16]); gq_d = din("gq", [128, 4]); gkv_d = din("gkv", [128, 2]); psc_d = din("pscale", [128, 8])
    id32_d = din("ident32", [128, 128]); idb_d = din("identb", [128, 128], BF16)
    rope_d = din("ropetab", [NT + 1, 64, 2 * TP])
    mask_d = din("maskb", [128, 8 * TP], BF16)
    ptab_d = din("pooltab", [NT + 1, 128, 64])
    sel_d = din("sel", [128, 2])

    yp = dout("yp", [NT, TP, D]); ys = dout("ys", [TS, D])
    okvp = dout("okvp", [NT, TP, 256]); okrp = dout("okrp", [NT, TP, 64]); opoolp = dout("opoolp", [16, 1024])
    okvs = dout("okvs", [TS, 256]); okrs = dout("okrs", [TS, 64]); opools = dout("opools", [16, 4 * 1024])

    x1sp = dscr("x1sp", [NT + 1, 128, NCH * TP])
    cqsp = dscr("cqsp", [NT + 1, 128, 4 * TP], BF16)
    usp = dscr("usp", [NT + 1, 128, 8 * TP])
    oasp = dscr("oasp", [NT + 1, 128, 8 * TP], BF16)
    XBR = 288
    xb_in = [nc.dram_tensor(f"xb_in{k}", [2 * XBR, 1024], BF16) for k in range(4)]
    xb_out = [nc.dram_tensor(f"xb_out{k}", [4 * XBR, 1024], BF16) for k in range(4)]
    XFR = NT * 128 + 256
    xf_in = nc.dram_tensor("xf_in", [XFR, 128], F32)
    xf_out = nc.dram_tensor("xf_out", [2 * XFR, 128], F32)

    def sb(name, shape, dt=F32):
        return es.enter_context(nc.sbuf_tensor("sb_" + name, list(shape), dt))

    id32 = sb("id32", [128, 128]); idb = sb("idb", [128, 128], BF16); onesb = sb("onesb", [128, 128], BF16)
    lnp = sb("lnp", [128, 8 * 16]); gq = sb("gq", [128, 4]); gkv = sb("gkv", [128, 2]); psc = sb("psc", [128, 8])
    sel = sb("sel", [128, 2])
    lnpa = sb("lnpa", [128, 4 * 16])
    nkmax = sb("nkmax", [128, 4]); nkt = sb("nkt", [128, 8]); negnk = sb("negnk", [128, 8])
    wsl = [sb(f"wsl{i}", [128, 8192], BF16) for i in range(3)]
    tmp = [sb(f"tmp{i}", [128, 512]) for i in range(6)]
    UB = 136 * 1024
    U = sb("U", [128, UB // 4])
    ps = [es.enter_context(nc.psum_tensor(f"psum{i}", [128, 512], F32)) for i in range(8)]

    class Carver:
        def __init__(self):
            self.off = 0

        def take(self, shape_free, dt):
            n = int(np.prod(shape_free))
            nbytes = n * (4 if dt == F32 else 2)
            nbytes4 = (nbytes + 31) // 32 * 32
            a = self.off // 4
            self.off += nbytes4
            assert self.off <= UB, f"U overflow {self.off} > {UB}"
            v = U[:, a:a + nbytes4 // 4]
            if dt != F32:
                v = v.bitcast(BF16)
            v = v[:, 0:n]
            if len(shape_free) == 2:
                v = v.rearrange("p (a b) -> p a b", a=shape_free[0])
            elif len(shape_free) == 3:
                v = v.rearrange("p (a b c) -> p a b c", a=shape_free[0], b=shape_free[1])
            elif len(shape_free) == 4:
                v = v.rearrange("p (a b c d) -> p a b c d", a=shape_free[0], b=shape_free[1], c=shape_free[2])
            return v

    def psb(i):
        return ps[i][:].bitcast(BF16)

    wstate = {"i": 0}
    wcache = {}
    NWBLK = 100
    wscr = dscr("wscr", [NWBLK, 128, 8192], BF16)

    def wload(parts, bid=None):
        s = wstate["i"] % 3
        wstate["i"] += 1
        key = f"wsl{s}"
        used = max(off + src.shape[1] * src.shape[2] for (off, src) in parts)
        if bid is not None and bid in wcache:
            idx = wcache[bid]
            R.dma("sp", DMA(wsl[s][:, 0:used], wscr[idx, :, 0:used]), f"wc{s}", reads=(f"wscr{idx}",), writes=(key,))
            return wsl[s], key
        for (off, src) in parts:
            a, b = src.shape[1], src.shape[2]
            dst = wsl[s][:, off:off + a * b].rearrange("p (a b) -> p a b", a=a)
            R.dma("pool", DMA(dst, src), f"w{s}", reads=(), writes=(key,), join=True)
        if bid is not None:
            idx = len(wcache)
            assert idx < NWBLK
            wcache[bid] = idx
            R.dma("sp", DMA(wscr[idx, :, 0:used], wsl[s][:, 0:used]), f"wb{s}", reads=(key,), writes=(f"wscr{idx}",))
        return wsl[s], key

    def wview(slot, off, a, b):
        return slot[:, off:off + a * b].rearrange("p (a b) -> p a b", a=a)

    for (dst, src, k) in [(id32, id32_d, "id32"), (idb, idb_d, "idb"), (lnp, lnp_d, "lnp"), (gq, gq_d, "gq"),
                          (gkv, gkv_d, "gkv"), (psc, psc_d, "psc"), (sel, sel_d, "sel")]:
        R.dma("pool", DMA(dst[:], src[:, :]), "const", writes=(k,))
    R.op("dve", MS(onesb[:], 1.0), writes=("onesb",))
    R.op("dve", MS(nkmax[:], 0.0), writes=("nkmax",))
    R.barrier()
    for i in range(4):
        R.op("dve", TSC(lnpa[:, i * 16:(i + 1) * 16], lnp[:, (2 * i + 1) * 16:(2 * i + 2) * 16], ALPHA, None, ALU.mult),
             reads=("lnp",), writes=("lnpa",))
    R.barrier()

    class StatsHook:
        def __init__(self, T, x32, xk, mean_bank, ex2_bank):
            self.T, self.x32, self.xk, self.mb, self.eb = T, x32, xk, mean_bank, ex2_bank
            self.q = []

        def chunk(self, c):
            T = self.T
            i = c % 4
            R.op("act", ACTF(lnsq[:, i, 0:T], self.x32[:, c, 0:T], AF.Copy), reads=(f"{self.xk}{c}",), writes=(f"lnsqA{i}",))
            R.op("act", ACTF(lnsq[:, 4 + i, 0:T], self.x32[:, c, 0:T], AF.Square), reads=(f"{self.xk}{c}",), writes=(f"lnsqB{i}",))

            def pe(c=c, i=i):
                R.op("pe", MM(ps[self.mb][:, 0:T], onesb[:], lnsq[:, i, 0:T], c == 0, c == 15), reads=(f"lnsqA{i}", "onesb"), writes=(f"ps{self.mb}",))
                R.op("pe", MM(ps[self.eb][:, 0:T], onesb[:], lnsq[:, 4 + i, 0:T], c == 0, c == 15), reads=(f"lnsqB{i}", "onesb"), writes=(f"ps{self.eb}",))
            self.q.append(pe)
            while len(self.q) > 2:
                self.q.pop(0)()

        def flush(self):
            while self.q:
                self.q.pop(0)()

    def layer_norm(T, x32, xb, lnidx, xk, xbk, post_scale=None, pre=None, write_xb=True):
        g = lnp[:, (2 * lnidx) * 16:(2 * lnidx + 1) * 16]
        b = lnp[:, (2 * lnidx + 1) * 16:(2 * lnidx + 2) * 16]
        mb, eb = (pre.mb, pre.eb) if pre is not None else (0, 1)
        mean_ps, ex2_ps = ps[mb][:, 0:T], ps[eb][:, 0:T]
        if pre is not None:
            pre.flush()
        for hh in range(0 if pre is not None else 4):
            cs = slice(hh * 4, hh * 4 + 4)
            ls = slice((hh % 2) * 4, (hh % 2) * 4 + 4)
            lk = f"lnsq{hh % 2}"
            keys = tuple(f"{xk}{c}" for c in range(hh * 4, hh * 4 + 4))
            bkeys = tuple(f"{xbk}{c}" for c in range(hh * 4, hh * 4 + 4))
            R.op("dve", CP(xb[:, cs, 0:T], x32[:, cs, 0:T]), reads=keys, writes=bkeys)
            R.op("act", ACTF(lnsq[:, ls, 0:T], x32[:, cs, 0:T], AF.Square), reads=keys, writes=(lk,))
            fns = []
            for c in range(4):
                cc = hh * 4 + c
                fns.append(MM(mean_ps, onesb[:], xb[:, cc, 0:T], cc == 0, cc == 15))
            R.op("pe", fns, reads=bkeys + ("onesb",), writes=("ps0",))
            fns = []
            for c in range(4):
                cc = hh * 4 + c
                fns.append(MM(ex2_ps, onesb[:], lnsq[:, (hh % 2) * 4 + c, 0:T], cc == 0, cc == 15))
            R.op("pe", fns, reads=(lk, "onesb"), writes=("ps1",))
        mean, rstd, nmr, t3 = tmp[0][:, 0:T], tmp[1][:, 0:T], tmp[2][:, 0:T], tmp[3][:, 0:T]
        R.op("act", ACTF(mean, mean_ps, AF.Copy, scale=1.0 / D), reads=(f"ps{mb}",), writes=("tmp0",))
        R.op("dve", TT(t3, mean, mean, ALU.mult), reads=("tmp0",), writes=("tmp3",))
        R.op("dve", STT(rstd, ex2_ps, 1.0 / D, t3, ALU.mult, ALU.subtract), reads=(f"ps{eb}", "tmp3"), writes=("tmp1",))
        R.op("dve", TSC(rstd, rstd, LN_EPS, None, ALU.add), reads=("tmp1",), writes=("tmp1",))
        R.op("act", ACTF(rstd, rstd, AF.Sqrt), reads=("tmp1",), writes=("tmp1",))
        R.op("dve", RCP(rstd, rstd), reads=("tmp1",), writes=("tmp1",))
        R.op("dve", STT(nmr, mean, -1.0, rstd, ALU.mult, ALU.mult), reads=("tmp0", "tmp1"), writes=("tmp2",))
        ba = lnpa[:, lnidx * 16:(lnidx + 1) * 16]
        for c in range(NCH):
            k, bk = f"{xk}{c}", f"{xbk}{c}"
            ta = tmp[4 + (c % 2)][:, 0:T]
            tk = f"tmp{4 + (c % 2)}"
            R.op("dve", STT(ta, x32[:, c, 0:T], g[:, c:c + 1], rstd, ALU.mult, ALU.mult), reads=(k, "tmp1", "lnp"), writes=(tk,))
            R.op("dve", STT(ta, nmr, g[:, c:c + 1], ta, ALU.mult, ALU.add), reads=(tk, "tmp2", "lnp"), writes=(tk,))
            if write_xb:
                R.op("act", ACTF(xb[:, c, 0:T], ta, AF.Identity, bias=b[:, c:c + 1]), reads=(tk, "lnp"), writes=(bk,))
            if post_scale is None:
                R.op("act", ACTF(x32[:, c, 0:T], ta, AF.Identity, bias=b[:, c:c + 1]), reads=(tk, "lnp"), writes=(k,))
            else:
                R.op("act", ACTF(x32[:, c, 0:T], ta, AF.Identity, scale=post_scale, bias=ba[:, c:c + 1]), reads=(tk, "lnpa"), writes=(k,))

    def ffn(T, x32, xb, aT, wgu, wd, xk, xbk, wname, hook=None, bgw=None):
        wguv = wgu.rearrange("(k p) f -> p k f", p=128)
        wdv = wd.rearrange("(k p) d -> p k d", p=128)
        xbkeys = tuple(f"{xbk}{c}" for c in range(NCH))
        fcount = 0
        for (f0, nf) in SLABS:
            fl = 0
            while fl < nf:
                n = min(2, nf - fl)
                fa = f0 + fl
                slot, wk = wload([(0, wguv[:, :, fa * 128:(fa + n) * 128]),
                                  (16 * n * 128, wguv[:, :, DFF + fa * 128:DFF + (fa + n) * 128])], bid=(wname, "gu", fa))
                gv = wview(slot, 0, 16, n * 128)
                uv = wview(slot, 16 * n * 128, 16, n * 128)
                for i in range(n):
                    par = fcount % 2
                    fcount += 1
                    bg, bu = 2 * par, 2 * par + 1
                    R.op("pe", [MM(ps[bg][:, 0:T], gv[:, k, i * 128:(i + 1) * 128], xb[:, k, 0:T], k == 0, k == 15)
                                for k in range(16)], reads=xbkeys + (wk,), writes=(f"ps{bg}",))
                    R.op("pe", [MM(ps[bu][:, 0:T], uv[:, k, i * 128:(i + 1) * 128], xb[:, k, 0:T], k == 0, k == 15)
                                for k in range(16)], reads=xbkeys + (wk,), writes=(f"ps{bu}",))
                    ts_ = tmp[par][:, 0:T]
                    R.op("act", ACTF(ts_, ps[bg][:, 0:T], AF.Silu), reads=(f"ps{bg}",), writes=(f"tmp{par}",))
                    R.op("dve", TT(aT[:, fl + i, 0:T], ts_, ps[bu][:, 0:T], ALU.mult),
                         reads=(f"tmp{par}", f"ps{bu}"), writes=(f"aT{fl + i}",))
                    if bgw and fcount % 3 == 0:
                        bgw.pop(0)()
                fl += n
            akeys = tuple(f"aT{i}" for i in range(nf))
            for dg in range(4):
                slot, wk = wload([(0, wdv[:, f0:f0 + nf, dg * 512:(dg + 1) * 512])], bid=(wname, "d", f0, dg))
                dv = wview(slot, 0, nf, 512)
                for dc in range(4):
                    bnk = 4 + dc
                    c = dg * 4 + dc
                    R.op("pe", [MM(ps[bnk][:, 0:T], dv[:, k, dc * 128:(dc + 1) * 128], aT[:, k, 0:T], k == 0, k == nf - 1)
                                for k in range(nf)], reads=akeys + (wk,), writes=(f"ps{bnk}",))
                    R.op("dve", STT(x32[:, c, 0:T], ps[bnk][:, 0:T], 0.5, x32[:, c, 0:T], ALU.mult, ALU.add),
                         reads=(f"ps{bnk}", f"{xk}{c}"), writes=(f"{xk}{c}",))
                    if hook is not None and f0 == SLABS[-1][0]:
                        hook.chunk(c)
        while bgw:
            bgw.pop(0)()

    def load_transposed(T, src_tok, x32, xb, xtok, xk, xbk, scale):
        for blk in range(T // 128):
            xt = xtok[blk % 2]
            xtk = f"xtok{blk % 2}"
            R.dma("pool", DMA(xt[:], src_tok[blk * 128:(blk + 1) * 128, :]), xtk, writes=(xtk,))
            if KSUB == 9:
                continue
            for g in range(4):
                bnk = (blk * 4 + g) % 4
                R.op("pe", [TR(ps[bnk][:, i * 128:(i + 1) * 128], xt[:, (4 * g + i) * 128:(4 * g + i + 1) * 128], id32[:])
                            for i in range(4)], reads=(xtk, "id32"), writes=(f"ps{bnk}",))
                if KSUB == 8:
                    continue
                pv = ps[bnk][:].rearrange("p (a b) -> p a b", a=4)
                keys = tuple(f"{xk}{4 * g + i}" for i in range(4))
                bkeys = tuple(f"{xbk}{4 * g + i}" for i in range(4))
                if KSUB != 6:
                    R.op("act", ACTF(x32[:, 4 * g:4 * g + 4, blk * 128:(blk + 1) * 128], pv, AF.Copy, scale=scale),
                         reads=(f"ps{bnk}",), writes=keys)
                if KSUB != 7:
                    R.op("dve", CP(xb[:, 4 * g:4 * g + 4, blk * 128:(blk + 1) * 128], pv), reads=(f"ps{bnk}",), writes=bkeys)

    def rms_stats(T, src32, nchunk, sq, sqk, srck, bank, eps, rstd_tmp):
        R.op("act", ACTF(sq[:, 0:nchunk, 0:T], src32[:, 0:nchunk, 0:T], AF.Square), reads=srck, writes=(sqk,))
        R.op("pe", [MM(ps[bank][:, 0:T], onesb[:], sq[:, c, 0:T], c == 0, c == nchunk - 1) for c in range(nchunk)],
             reads=(sqk, "onesb"), writes=(f"ps{bank}",))
        rs = tmp[rstd_tmp][:, 0:T]
        tk = f"tmp{rstd_tmp}"
        R.op("dve", TSC(rs, ps[bank][:, 0:T], 1.0 / (128 * nchunk), eps, ALU.mult, ALU.add), reads=(f"ps{bank}",), writes=(tk,))
        R.op("act", ACTF(rs, rs, AF.Sqrt), reads=(tk,), writes=(tk,))
        R.op("dve", RCP(rs, rs), reads=(tk,), writes=(tk,))
        return rs, tk

    cv = Carver()
    x32 = cv.take([16, TP], F32)
    xb = cv.take([16, TP], BF16)
    aT = cv.take([11, TP], BF16)
    lnsq = cv.take([8, TP], BF16)
    xtok = [cv.take([D], F32), cv.take([D], F32)]
    cq32 = cv.take([4, TP], F32)
    cqn = cv.take([4, TP], BF16)
    ckv32 = cv.take([2, TP], F32)
    kvb = cv.take([2, TP], BF16)
    sqb = cv.take([4, TP], BF16)
    kr32 = cv.take([TP], F32)
    krb = cv.take([TP], BF16)
    rtab = cv.take([2 * TP], F32)
    kvtok = cv.take([4, 320], F32)
    vtokb = cv.take([4, 256], BF16)
    ust = cv.take([4, TP], F32)
    ptk = cv.take([2, 512], F32)
    s_kvb = sb("s_kvb", [128, 2, TS], BF16)
    s_krb = sb("s_krb", [64, TS], BF16)
    s_vtok = sb("s_vtok", [64, 4, 256], BF16)

    xb_in_aps = [t.ap() for t in xb_in]
    xf_in_ap = xf_in.ap()

    def phaseA(ti, T, src_tok, is_sample):
        nblk = T // 128
        load_transposed(T, src_tok, x32, xb, xtok, "x32_", "xb_", ALPHA)
        sub(6)
        sub(7)
        sub(8)
        sub(9)
        sub(10)
        hk = StatsHook(T, x32, "x32_", 0, 1)
        ffn(T, x32, xb, aT, w_f1gu, w_f1d, "x32_", "xb_", "f1", hook=hk)
        sub(11)
        layer_norm(T, x32, xb, 0, "x32_", "xb_", pre=hk)
        sub(12)
        xkeys = tuple(f"x32_{c}" for c in range(NCH))
        xbkeys = tuple(f"xb_{c}" for c in range(NCH))
        R.dma("pool", DMA(x1sp[ti, :, 0:NCH * T].rearrange("p (c t) -> p c t", c=NCH), x32[:, :, 0:T]), "st_x1",
              reads=xkeys, writes=(f"x1sp{ti}",))
        sub(13)
        winv = w_in.rearrange("(k p) f -> p k f", p=128)
        slot, wk = wload([(0, winv[:, :, 0:512])], bid=("win", "cq"))
        wv = wview(slot, 0, 16, 512)
        for m in range(4):
            R.op("pe", [MM(ps[m][:, 0:T], wv[:, k, m * 128:(m + 1) * 128], xb[:, k, 0:T], k == 0, k == 15) for k in range(16)],
                 reads=xbkeys + (wk,), writes=(f"ps{m}",))
            R.op("act", ACTF(cq32[:, m, 0:T], ps[m][:, 0:T], AF.Copy), reads=(f"ps{m}",), writes=("cq32",))
        sub(14)
        winpv = w_inp.rearrange("(k p) f -> p k f", p=128)
        slot, wk = wload([(0, winv[:, :, 512:832]), (16 * 320, winpv[:, :, 0:64])], bid=("win", "kv"))
        wv = wview(slot, 0, 16, 320)
        wvp = wview(slot, 16 * 320, 16, 64)
        R.dma("pool", DMA(rtab[0:64, :], rope_d[ti, :, :]), "rtab", writes=("rtab",))
        for m in range(2):
            R.op("pe", [MM(ps[m][:, 0:T], wv[:, k, m * 128:(m + 1) * 128], xb[:, k, 0:T], k == 0, k == 15) for k in range(16)],
                 reads=xbkeys + (wk,), writes=(f"ps{m}",))
            R.op("act", ACTF(ckv32[:, m, 0:T], ps[m][:, 0:T], AF.Copy), reads=(f"ps{m}",), writes=("ckv32",))
        R.op("pe", [MM(ps[2][0:64, 0:T], wv[:, k, 256:320], xb[:, k, 0:T], k == 0, k == 15) for k in range(16)],
             reads=xbkeys + (wk,), writes=("ps2",))
        R.op("pe", [MM(ps[3][0:64, 0:T], wvp[:, k, 0:64], xb[:, k, 0:T], k == 0, k == 15) for k in range(16)],
             reads=xbkeys + (wk,), writes=("ps3",))
        kvb_t = s_kvb if is_sample else kvb
        krb_t = s_krb if is_sample else krb
        t1, t2 = tmp[1][0:64, 0:T], tmp[2][0:64, 0:T]
        R.op("dve", TT(t1, ps[2][0:64, 0:T], rtab[0:64, 0:T], ALU.mult), reads=("ps2", "rtab"), writes=("tmp1",))
        R.op("dve", TT(t2, ps[3][0:64, 0:T], rtab[0:64, TP:TP + T], ALU.mult), reads=("ps3", "rtab"), writes=("tmp2",))
        R.op("dve", TT(kr32[0:64, 0:T], t1, t2, ALU.add), reads=("tmp1", "tmp2"), writes=("kr32",))
        R.op("act", ACTF(krb_t[0:64, 0:T], kr32[0:64, 0:T], AF.Copy), reads=("kr32",), writes=("krb",))
        sub(16)
        for ub in range(2):
            slot, wk = wload([(0, winv[:, :, 832 + ub * 512:832 + (ub + 1) * 512])], bid=("win", "u", ub))
            wv = wview(slot, 0, 16, 512)
            for m in range(4):
                R.op("pe", [MM(ps[m][:, 0:T], wv[:, k, m * 128:(m + 1) * 128], xb[:, k, 0:T], k == 0, k == 15) for k in range(16)],
                     reads=xbkeys + (wk,), writes=(f"ps{m}",))
                R.op("act", ACTF(ust[:, m, 0:T], ps[m][:, 0:T], AF.Copy), reads=(f"ps{m}",), writes=("ust",))
            R.dma("pool", DMA(usp[ti, :, ub * 4 * T:(ub + 1) * 4 * T].rearrange("p (c t) -> p c t", c=4), ust[:, :, 0:T]), "st_u",
                  reads=("ust",), writes=(f"usp{ti}",))
            if not is_sample:
                R.dma("pool", DMA(xf_in_ap[ti * 128:(ti + 1) * 128, ub * 64:(ub + 1) * 64].rearrange("p (c t) -> p c t", c=4),
                                ust[:, :, T - 16:T]), "st_xu", reads=("ust",), writes=(f"xf_in_u{ti}_{ub}",))
            if (not is_sample and ti == NT - 1) or is_sample:
                nseq = 4 if is_sample else 1
                L = T // nseq
                for s in range(nseq):
                    pb = (ub * nseq + s) % 2
                    R.op("pe", [TR(ps[5][0:16, m * 128:(m + 1) * 128], ust[:, m, s * L + L - 16:s * L + L], id32[:]) for m in range(4)],
                         reads=("ust", "id32"), writes=("ps5",))
                    R.op("act", ACTF(ptk[0:16, pb, :], ps[5][0:16, 0:512], AF.Copy), reads=("ps5",), writes=(f"ptk{pb}",))
                    dst = opools[:, s * 1024 + ub * 512:s * 1024 + (ub + 1) * 512] if is_sample else opoolp[:, ub * 512:(ub + 1) * 512]
                    R.dma("pool", DMA(dst, ptk[0:16, pb, :]), f"st_pool{pb}", reads=(f"ptk{pb}",))

        rs, rk = rms_stats(T, cq32, 4, sqb, "sqb", ("cq32",), 4, RMS_EPS, 0)
        for m in range(4):
            R.op("dve", STT(cqn[:, m, 0:T], cq32[:, m, 0:T], gq[:, m:m + 1], rs, ALU.mult, ALU.mult),
                 reads=("cq32", rk, "gq"), writes=("cqn",))
        R.dma("pool", DMA(cqsp[ti, :, 0:4 * T].rearrange("p (c t) -> p c t", c=4), cqn[:, :, 0:T]), "st_cq",
              reads=("cqn",), writes=(f"cqsp{ti}",))
        rs, rk = rms_stats(T, ckv32, 2, sqb, "sqb", ("ckv32",), 4, RMS_EPS, 0)
        for m in range(2):
            R.op("dve", STT(ckv32[:, m, 0:T], ckv32[:, m, 0:T], gkv[:, m:m + 1], rs, ALU.mult, ALU.mult),
                 reads=("ckv32", rk, "gkv"), writes=("ckv32",))
        R.op("act", ACTF(kvb_t[:, :, 0:T], ckv32[:, :, 0:T], AF.Copy), reads=("ckv32",), writes=("kvb",))
        R.op("act", ACTF(sqb[:, 0:2, 0:T], ckv32[:, :, 0:T], AF.Square), reads=("ckv32",), writes=("sqb",))
        R.op("act", ACTF(sqb[0:64, 2, 0:T], kr32[0:64, 0:T], AF.Square), reads=("kr32",), writes=("sqb",))
        R.op("pe", [MM(ps[5][:, 0:T], onesb[:], sqb[:, 0, 0:T], True, False),
                    MM(ps[5][:, 0:T], onesb[:], sqb[:, 1, 0:T], False, False),
                    MM(ps[5][:, 0:T], onesb[0:64, :], sqb[0:64, 2, 0:T], False, True)],
             reads=("sqb", "onesb"), writes=("ps5",))
        if not is_sample:
            R.op("dve", RED(nkt[:, 0:1], ps[5][:, 0:T], ALU.max), reads=("ps5",), writes=("nkt",))
            R.op("dve", TT(nkmax[:, 0:1], nkmax[:, 0:1], nkt[:, 0:1], ALU.max), reads=("nkt", "nkmax"), writes=("nkmax",))
        else:
            R.op("dve", RED(nkt[:, 0:4], ps[5][:, 0:T].rearrange("p (s t) -> p s t", s=4), ALU.max),
                 reads=("ps5",), writes=("nkt",))
        sub(15)
        if not is_sample:
            for blk in range(nblk):
                bnk = 6 + (blk % 2)
                R.op("pe", [TR(ps[bnk][:, 0:128], ckv32[:, 0, blk * 128:(blk + 1) * 128], id32[:]),
                            TR(ps[bnk][:, 128:256], ckv32[:, 1, blk * 128:(blk + 1) * 128], id32[:]),
                            TR(ps[bnk][:, 256:320], kr32[0:64, blk * 128:(blk + 1) * 128], id32[0:64, 0:64])],
                     reads=("ckv32", "kr32", "id32"), writes=(f"ps{bnk}",))
                R.op("act", ACTF(kvtok[:, blk, :], ps[bnk][:, 0:320], AF.Copy), reads=(f"ps{bnk}",), writes=("kvtok",))
                R.op("dve", CP(vtokb[:, blk, :], ps[bnk][:, 0:256]), reads=(f"ps{bnk}",), writes=("vtokb",))
            R.dma("pool", DMA(okvp[ti].rearrange("(b p) c -> p b c", p=128), kvtok[:, :, 0:256]), "st_kv", reads=("kvtok",))
            R.dma("pool", DMA(okrp[ti].rearrange("(b p) c -> p b c", p=128), kvtok[:, :, 256:320]), "st_kr", reads=("kvtok",))
            base = (ti % 2) * XBR
            xb_in_ap = xb_in_aps[ti // 2]
            R.dma("pool", DMA(xb_in_ap[base:base + 128, :].rearrange("p (c t) -> p c t", c=2), kvb[:, :, :]), "st_xk",
                  reads=("kvb",), writes=(f"xb_in_k{ti}",))
            R.dma("pool", DMA(xb_in_ap[base + 128:base + 256, :].rearrange("p (b c) -> p b c", b=4), vtokb[:, :, :]), "st_xv",
                  reads=("vtokb",), writes=(f"xb_in_v{ti}",))
            R.dma("pool", DMA(xb_in_ap[base + 256:base + 288, :].rearrange("r (s c) -> (r s) c", s=2), krb[0:64, :]), "st_xr",
                  reads=("krb",), writes=(f"xb_in_r{ti}",))
        else:
            for s in range(4):
                bnk = 6 + (s % 2)
                R.op("pe", [TR(ps[bnk][0:64, 0:128], ckv32[:, 0, s * 64:(s + 1) * 64], id32[:]),
                            TR(ps[bnk][0:64, 128:256], ckv32[:, 1, s * 64:(s + 1) * 64], id32[:]),
                            TR(ps[bnk][0:64, 256:320], kr32[0:64, s * 64:(s + 1) * 64], id32[0:64, 0:64])],
                     reads=("ckv32", "kr32", "id32"), writes=(f"ps{bnk}",))
                R.op("act", ACTF(kvtok[0:64, s, :], ps[bnk][0:64, 0:320], AF.Copy), reads=(f"ps{bnk}",), writes=("kvtok",))
                R.op("dve", CP(s_vtok[:, s, :], ps[bnk][0:64, 0:256]), reads=(f"ps{bnk}",), writes=("vtokb",))
            R.dma("pool", DMA(okvs.rearrange("(s p) c -> p s c", p=64), kvtok[0:64, :, 0:256]), "st_kv", reads=("kvtok",))
            R.dma("pool", DMA(okrs.rearrange("(s p) c -> p s c", p=64), kvtok[0:64, :, 256:320]), "st_kr", reads=("kvtok",))

    if STAGE == 0:
        return finish()
    for li in range(0 if KSAMPLE else min(NT, KNT)):
        try:
            phaseA(li, TP, xp[li], False)
        except _Stop:
            return finish()
        if STAGE == 1:
            return finish()
    if STAGE == 2:
        return finish()
    R.op("dve", MS(tmp[5][:, 0:128], 0.0), writes=("tmp5",))
    R.dma("pool", DMA(xf_in_ap[NT * 128:NT * 128 + 128, :], tmp[5][:, 0:128]), "st_xz", reads=("tmp5",), writes=("xf_in_z",))
    R.dma("pool", DMA(xf_in_ap[NT * 128 + 128:NT * 128 + 256, :], tmp[5][:, 0:128]), "st_xz2", reads=("tmp5",), writes=("xf_in_z2",))
    R.dma("pool", DMAS(xf_in_ap[NT * 128 + 128:NT * 128 + 256, 0:1], nkmax[:, 0:1]), "st_xn", reads=("nkmax", "xf_in_z2"), writes=("xf_in_n",))
    if KCORES == 8:
        for k in range(4):
            rk = tuple(f"xb_in_{t}{ti}" for t in "kvr" for ti in (2 * k, 2 * k + 1))
            R.coll(lambda g, k=k: g.collective_compute("AllGather", ALU.bypass, replica_groups=[[0, 1], [2, 3], [4, 5], [6, 7]],
                                                       ins=[xb_in[k].ap().opt()], outs=[xb_out[k].ap().opt()]), f"xb{k}",
                   reads=rk, writes=(f"xb_out{k}",))
        rk = tuple(f"xf_in_u{ti}_{ub}" for ti in range(NT) for ub in range(2)) + ("xf_in_z", "xf_in_n")
        R.coll(lambda g: g.collective_compute("AllGather", ALU.bypass, replica_groups=[[0, 1], [2, 3], [4, 5], [6, 7]],
                                              ins=[xf_in.ap().opt()], outs=[xf_out.ap().opt()]), "xf",
               reads=rk, writes=("xf_out",))
    if STAGE == 3:
        return finish()
    phaseA(NT, TS, xs, True)
    if STAGE == 4:
        return finish()
    tokA = R.lastw.copy()
    R.barrier()
    for k in ("xb_out0", "xb_out1", "xb_out2", "xb_out3", "xf_out"):
        if k in tokA:
            R.lastw[k] = tokA[k]

    cv = Carver()
    KT = cv.take([2, 8192], BF16)
    KR = cv.take([8192], BF16)
    VV = cv.take([64, 256], BF16)
    cqn1 = cv.take([4, TP], BF16)
    qn = cv.take([TP], BF16)
    qr = cv.take([8, TP], BF16)
    sq1 = cv.take([3, TP], BF16)
    sq1x = [sq1, cv.take([3, TP], BF16)]
    PT = [cv.take([TP], BF16), cv.take([TP], BF16)]
    olat = cv.take([2, TP], BF16)
    oab = [cv.take([TP], BF16), cv.take([TP], BF16)]
    maskb = cv.take([8, TP], BF16)
    qr32 = cv.take([TP], F32)
    rtab1 = cv.take([2 * TP], F32)
    ktok = cv.take([16, 64], BF16)
    qnrm = cv.take([8, TS], F32)
    oas = cv.take([8, TS], BF16)
    qlat = wsl[2][:, :].rearrange("p (h c t) -> p h c t", h=8, c=2)

    xb_out_aps = [t.ap() for t in xb_out]
    xf_out_ap = xf_out.ap()
    W0, W1 = wsl[0], wsl[1]
    R.dma("pool", DMA(wview(W0, 0, 4, 1024), w_uqn.rearrange("(k p) f -> p k f", p=128)), "w0", writes=("attw",))
    R.dma("pool", DMA(wview(W0, 4096, 4, 512), w_uqr.rearrange("(k p) f -> p k f", p=128)), "w0", writes=("attw",))
    R.dma("pool", DMA(wview(W0, 6144, 4, 512), w_uqrp.rearrange("(k p) f -> p k f", p=128)), "w0", writes=("attw",))
    R.dma("pool", DMA(W1[:, 0:2048], w_ukT[:, :]), "w1", writes=("attw1",))
    R.dma("pool", DMA(wview(W1, 2048, 2, 1024), w_uv.rearrange("(k p) f -> p k f", p=128)), "w1", writes=("attw1",))
    R.dma("pool", DMA(maskb[:, :, :], mask_d.rearrange("p (j t) -> p j t", j=8)), "mask", writes=("maskb",))
    wqn = wview(W0, 0, 4, 1024)
    wqr = wview(W0, 4096, 4, 512)
    wqrp = wview(W0, 6144, 4, 512)
    wuk = wview(W1, 0, 8, 256)
    wuv = wview(W1, 2048, 2, 1024)
    R.op("dve", MS(KR[64:65, :], 1.0), writes=("KRones",))

    def q_proj(ti, T, sample):
        R.dma("pool", DMA(cqn1[:, :, 0:T], cqsp[ti, :, 0:4 * T].rearrange("p (c t) -> p c t", c=4)), "ld_cq",
              reads=(f"cqsp{ti}",), writes=("cqn1",))
        R.dma("pool", DMA(rtab1[0:64, :], rope_d[ti, :, :]), "rtab1", writes=("rtab1",))

        def norm_step(h):
            sq = sq1x[h % 2]
            sk = f"sq1_{h % 2}"
            R.op("pe", [MM(ps[5][:, 0:T], onesb[:], sq[:, 0, 0:T], True, False),
                        MM(ps[5][:, 0:T], onesb[:], sq[:, 1, 0:T], False, False),
                        MM(ps[5][:, 0:T], onesb[0:64, :], sq[0:64, 2, 0:T], False, True)],
                 reads=(sk, "onesb"), writes=("ps5",))
            R.op("act", ACTF(tmp[3][64:65, 0:T], ps[5][64:65, 0:T], AF.Sqrt), reads=("ps5",), writes=("tmp3",))
            if not sample:
                R.op("dve", TSC(qr[64:65, h, 0:T], tmp[3][64:65, 0:T], negnk[64:65, 0:1], None, ALU.mult),
                     reads=("tmp3", "negnk"), writes=("qr",))
            else:
                R.op("act", ACTF(qnrm[64:65, h, 0:T], tmp[3][64:65, 0:T], AF.Copy), reads=("tmp3",), writes=("qnrm",))

        prev = None
        for h in range(8):
            sq = sq1x[h % 2]
            sk = f"sq1_{h % 2}"
            R.op("pe", [MM(ps[0][:, 0:T], wqn[:, k, h * 128:(h + 1) * 128], cqn1[:, k, 0:T], k == 0, k == 3) for k in range(4)],
                 reads=("cqn1", "attw"), writes=("ps0",))
            R.op("act", ACTF(qn[:, 0:T], ps[0][:, 0:T], AF.Copy), reads=("ps0",), writes=("qn",))
            R.op("pe", [MM(ps[3][0:64, 0:T], wqr[:, k, h * 64:(h + 1) * 64], cqn1[:, k, 0:T], k == 0, k == 3) for k in range(4)],
                 reads=("cqn1", "attw"), writes=("ps3",))
            R.op("pe", [MM(ps[4][0:64, 0:T], wqrp[:, k, h * 64:(h + 1) * 64], cqn1[:, k, 0:T], k == 0, k == 3) for k in range(4)],
                 reads=("cqn1", "attw"), writes=("ps4",))
            if prev is not None:
                norm_step(prev)
            for c in range(2):
                R.op("pe", MM(ps[1 + c][:, 0:T], wuk[:, h, c * 128:(c + 1) * 128], qn[:, 0:T], True, True),
                     reads=("qn", "attw1"), writes=(f"ps{1 + c}",))
                R.op("act", ACTF(qlat[:, h, c, 0:T], ps[1 + c][:, 0:T], AF.Copy), reads=(f"ps{1 + c}",), writes=("qlat",))
                R.op("act", ACTF(sq[:, c, 0:T], ps[1 + c][:, 0:T], AF.Square), reads=(f"ps{1 + c}",), writes=(sk,))
            t1, t2 = tmp[1][0:64, 0:T], tmp[2][0:64, 0:T]
            R.op("dve", TT(t1, ps[3][0:64, 0:T], rtab1[0:64, 0:T], ALU.mult), reads=("ps3", "rtab1"), writes=("tmp1",))
            R.op("dve", TT(t2, ps[4][0:64, 0:T], rtab1[0:64, TP:TP + T], ALU.mult), reads=("ps4", "rtab1"), writes=("tmp2",))
            R.op("dve", TT(qr32[0:64, 0:T], t1, t2, ALU.add), reads=("tmp1", "tmp2"), writes=("qr32",))
            R.op("act", ACTF(qr[0:64, h, 0:T], qr32[0:64, 0:T], AF.Copy), reads=("qr32",), writes=("qr",))
            R.op("act", ACTF(sq[0:64, 2, 0:T], qr32[0:64, 0:T], AF.Square), reads=("qr32",), writes=(sk,))
            prev = h
        norm_step(prev)

    def attend(nblocks, T, rhs_fn, kcount_fn, mask_fn, acc, tagK):
        o0, o1, lb = acc
        pend = None

        def pv(j, last):
            kc = kcount_fn(j)
            pt = PT[j % 2]
            R.op("pe", [MM(ps[o0][:, 0:T], VV[0:kc, j, 0:128], pt[0:kc, 0:T], j == 0, last),
                        MM(ps[o1][:, 0:T], VV[0:kc, j, 128:256], pt[0:kc, 0:T], j == 0, last),
                        MM(ps[lb][:, 0:T], onesb[0:kc, :], pt[0:kc, 0:T], j == 0, last)],
                 reads=(f"PT{j % 2}", "V", "onesb"), writes=(f"ps{o0}", f"ps{o1}", f"ps{lb}"))

        for j in range(nblocks):
            kc = kcount_fn(j)
            sbk = j % 2
            mk = mask_fn(j)
            fns = [MM(ps[sbk][0:kc, 0:T], KT[:, 0, j * 128:j * 128 + kc], rhs_fn(0), True, False),
                   MM(ps[sbk][0:kc, 0:T], KT[:, 1, j * 128:j * 128 + kc], rhs_fn(1), False, False),
                   MM(ps[sbk][0:kc, 0:T], KR[0:65, j * 128:j * 128 + kc], rhs_fn(2), False, mk is None)]
            rd = ["KT", "KR", "KRones", "qlat", "qr"]
            if mk is not None:
                fns.append(MM(ps[sbk][0:kc, 0:T], idb[:], mk, False, True))
                rd += ["maskb", "idb"]
            R.op("pe", fns, reads=tuple(rd), writes=(f"ps{sbk}",))
            R.op("act", ACTF(PT[sbk][0:kc, 0:T], ps[sbk][0:kc, 0:T], AF.Exp, scale=SCALE), reads=(f"ps{sbk}",), writes=(f"PT{sbk}",))
            if j == 2:
                flush()
            if pend is not None:
                pv(pend, False)
            pend = j
        pv(pend, True)

    def finish_head(T, acc, out_fn):
        o0, o1, lb = acc
        rinv = tmp[4][:, 0:T]
        R.op("dve", RCP(rinv, ps[lb][:, 0:T]), reads=(f"ps{lb}",), writes=("tmp4",))
        R.op("dve", TT(olat[:, 0, 0:T], ps[o0][:, 0:T], rinv, ALU.mult), reads=(f"ps{o0}", "tmp4"), writes=("olat",))
        R.op("dve", TT(olat[:, 1, 0:T], ps[o1][:, 0:T], rinv, ALU.mult), reads=(f"ps{o1}", "tmp4"), writes=("olat",))
        pending.append(out_fn)

    pending = []

    def flush():
        while pending:
            pending.pop(0)()

    q_proj(NT, TS, True)
    ACC_A, ACC_B = (2, 3, 4), (5, 6, 7)
    for s in range(4):
        R.dma("pool", DMA(VV[:, 0:16, :], ckvc[s].rearrange("(j p) c -> p j c", p=128)), "ld_vc", writes=("V",))
        R.dma("pool", DMA(ktok[:, :, :], ckrc[s].rearrange("(j p) c -> p j c", p=128)), "ld_kc", writes=("ktok",))
        R.op("act", ACTF(VV[0:64, 16, :], s_vtok[:, s, :], AF.Copy), reads=("vtokb", "V"), writes=("V",))
        gi = 0
        for jg in range(4):
            for c in range(2):
                bnk = 6 + (gi % 2)
                gi += 1
                R.op("pe", [TR(psb(bnk)[:, i * 128:(i + 1) * 128], VV[:, jg * 4 + i, c * 128:(c + 1) * 128], idb[:]) for i in range(4)],
                     reads=("V", "idb"), writes=(f"ps{bnk}",))
                R.op("dve", CP(KT[:, c, jg * 512:(jg + 1) * 512], psb(bnk)[:, 0:512]), reads=(f"ps{bnk}",), writes=("KT",))
            bnk = 6 + (gi % 2)
            gi += 1
            R.op("pe", [TR(psb(bnk)[0:64, i * 128:(i + 1) * 128], ktok[:, jg * 4 + i, :], idb[:]) for i in range(4)],
                 reads=("ktok", "idb"), writes=(f"ps{bnk}",))
            R.op("dve", CP(KR[0:64, jg * 512:(jg + 1) * 512], psb(bnk)[0:64, 0:512]), reads=(f"ps{bnk}",), writes=("KR",))
        R.op("act", ACTF(KT[:, :, 2048:2112], s_kvb[:, :, s * 64:(s + 1) * 64], AF.Copy), reads=("kvb", "KT"), writes=("KT",))
        R.op("act", ACTF(KR[0:64, 2048:2112], s_krb[0:64, s * 64:(s + 1) * 64], AF.Copy), reads=("krb", "KR"), writes=("KR",))
        for g5 in range(5):
            c0 = g5 * 512
            wd = 512 if g5 < 4 else 64
            R.op("act", ACTF(sq1[:, 0:2, 0:wd], KT[:, :, c0:c0 + wd], AF.Square), reads=("KT",), writes=("sq1_0",))
            R.op("act", ACTF(sq1[0:64, 2, 0:wd], KR[0:64, c0:c0 + wd], AF.Square), reads=("KR",), writes=("sq1_0",))
            R.op("pe", [MM(ps[5][:, 0:wd], onesb[:], sq1[:, 0, 0:wd], True, False),
                        MM(ps[5][:, 0:wd], onesb[:], sq1[:, 1, 0:wd], False, False),
                        MM(ps[5][:, 0:wd], onesb[0:64, :], sq1[0:64, 2, 0:wd], False, True)],
                 reads=("sq1_0", "onesb"), writes=("ps5",))
            R.op("dve", RED(nkt[:, g5:g5 + 1], ps[5][:, 0:wd], ALU.max), reads=("ps5",), writes=("nkt",))
        R.op("dve", RED(nkt[:, 7:8], nkt[:, 0:5], ALU.max), reads=("nkt",), writes=("nkt7",))
        R.op("act", ACTF(nkt[:, 7:8], nkt[:, 7:8], AF.Sqrt), reads=("nkt7",), writes=("nkt7",))
        R.op("dve", TSC(negnk[:, 1 + s:2 + s], nkt[:, 7:8], -1.03, None, ALU.mult), reads=("nkt7",), writes=("negnk",))
        R.op("dve", TSC(qr[64:65, :, s * 64:(s + 1) * 64], qnrm[64:65, :, s * 64:(s + 1) * 64], negnk[64:65, 1 + s:2 + s], None, ALU.mult),
             reads=("qnrm", "negnk", "qr"), writes=("qr",))
        acc = ACC_A
        attend(17, 512,
               lambda c, s=s: (qlat[:, :, c, s * 64:(s + 1) * 64] if c < 2 else qr[0:65, :, s * 64:(s + 1) * 64]),
               lambda j: 128 if j < 16 else 64, lambda j: None, acc, "s")

        def proj_s(s=s, acc=acc):
            fns = []
            for h in range(8):
                for c in range(2):
                    fns.append(MM(ps[5][:, h * 64:(h + 1) * 64], wuv[:, c, h * 128:(h + 1) * 128], olat[:, c, h * 64:(h + 1) * 64], c == 0, c == 1))
            R.op("pe", fns, reads=("olat", "attw1"), writes=("ps5",))
            R.op("act", ACTF(oas[:, :, s * 64:(s + 1) * 64], ps[5][:, 0:512].rearrange("p (h t) -> p h t", h=8), AF.Copy),
                 reads=("ps5",), writes=("oas",))
        finish_head(512, acc, proj_s)
        flush()
    R.dma("pool", DMA(oasp[NT, :, 0:8 * TS].rearrange("p (h t) -> p h t", h=8), oas[:, :, :]), "st_oas", reads=("oas",), writes=(f"oasp{NT}",))

    if STAGE == 5:
        return finish()
    if not KSAMPLE:
        for r in range(2):
            for li in range(NT):
                gt = 2 * li + r
                base = (r * 2 + (li % 2)) * XBR
                xb_out_ap = xb_out_aps[li // 2]
                xbk = f"xb_out{li // 2}"
                R.dma("pool", DMA(KT[:, :, gt * 512:(gt + 1) * 512], xb_out_ap[base:base + 128, :].rearrange("p (c t) -> p c t", c=2)),
                      "kvres", reads=(xbk,), writes=("KT",), join=True)
                R.dma("pool", DMA(VV[:, gt * 4:(gt + 1) * 4, :], xb_out_ap[base + 128:base + 256, :].rearrange("p (b c) -> p b c", b=4)),
                      "kvres", reads=(xbk,), writes=("V",), join=True)
                R.dma("pool", DMA(KR[0:64, gt * 512:(gt + 1) * 512], xb_out_ap[base + 256:base + 288, :].rearrange("r (s c) -> (r s) c", s=2)),
                      "kvres", reads=(xbk,), writes=("KR",), join=True)
        tk = R.lastw["KR"]
        R.lastw["KT"] = tk
        R.lastw["V"] = tk
        R.dma("pool", DMAS(nkt[:, 5:6], xf_out_ap[NT * 128 + 128:NT * 128 + 256, 0:1]), "ld_nk", reads=("xf_out",), writes=("nkt56",))
        R.dma("pool", DMAS(nkt[:, 6:7], xf_out_ap[XFR + NT * 128 + 128:XFR + NT * 128 + 256, 0:1]), "ld_nk", reads=("xf_out",), writes=("nkt56",))
        R.op("dve", TT(nkt[:, 7:8], nkt[:, 5:6], nkt[:, 6:7], ALU.max), reads=("nkt56", "nkt7"), writes=("nkt7",))
        R.op("act", ACTF(nkt[:, 7:8], nkt[:, 7:8], AF.Sqrt), reads=("nkt7",), writes=("nkt7",))
        R.op("dve", TSC(negnk[:, 0:1], nkt[:, 7:8], -1.03, None, ALU.mult), reads=("nkt7",), writes=("negnk",))

    for li in range(0 if KSAMPLE else NT):
        flush()
        q_proj(li, TP, False)
        nblocks = 8 * li + 8
        for h in range(8):
            acc = ACC_A if h % 2 == 0 else ACC_B
            attend(nblocks, TP,
                   lambda c, h=h: (qlat[:, h, c, :] if c < 2 else qr[0:65, h, :]),
                   lambda j: 128,
                   lambda j, li=li: (maskb[:, j - 8 * li, :] if j >= 8 * li else None), acc, "p")

            def proj_p(h=h, acc=acc, li=li):
                o0 = acc[0]
                R.op("pe", [MM(ps[o0][:, 0:TP], wuv[:, c, h * 128:(h + 1) * 128], olat[:, c, :], c == 0, c == 1) for c in range(2)],
                     reads=("olat", "attw1"), writes=(f"ps{o0}",))
                ob = oab[h % 2]
                R.op("act", ACTF(ob[:, :], ps[o0][:, 0:TP], AF.Copy), reads=(f"ps{o0}",), writes=(f"oab{h % 2}",))
                R.dma("pool", DMA(oasp[li, :, h * TP:(h + 1) * TP], ob[:, :]), f"st_oa{h % 2}", reads=(f"oab{h % 2}",), writes=(f"oasp{li}",))
            finish_head(TP, acc, proj_p)
    flush()

    R.barrier()

    cv = Carver()
    x32 = cv.take([16, TP], F32)
    xb = cv.take([16, TP], BF16)
    aT = cv.take([11, TP], BF16)
    lnsq = cv.take([8, TP], BF16)
    uext = cv.take([8, 528], F32)
    sA = cv.take([528], F32)
    sB = cv.take([528], F32)
    dT = cv.take([8, TP], BF16)
    h0 = cv.take([8, 16], F32)
    h1 = cv.take([8, 16], F32)
    ptab = cv.take([64], F32)
    ptok = cv.take([4, 256], F32)
    pT = cv.take([2, TP], BF16)
    ytok = [cv.take([D], F32), cv.take([D], F32)]
    wppb = cv.take([2, D], BF16)
    R.dma("pool", DMA(wppb[:, :, :], w_pp.rearrange("(k p) d -> p k d", p=128)), "ld_wpp", writes=("wppb",))

    def layer_norm2(T, lnidx, post_scale, pre=None, write_xb=True):
        layer_norm(T, x32, xb, lnidx, "x32_", "xb_", post_scale, pre=pre, write_xb=write_xb)

    pool_ready = set()
    oa_ready = set()

    def pool_prepare(ti, T, is_sample):
        if ti in pool_ready:
            return []
        pool_ready.add(ti)
        nseq = 4 if is_sample else 1
        L = T // nseq
        E = 16 + L
        ue = uext[:, :, 0:nseq * E].rearrange("p c (s e) -> p c s e", s=nseq)
        for s in range(nseq):
            R.dma("pool", DMA(ue[:, :, s, 16:E], usp[ti, :, 0:8 * T].rearrange("p (c t) -> p c t", c=8)[:, :, s * L:(s + 1) * L]),
                  "ld_u", reads=(f"usp{ti}",), writes=("uext",))
        R.dma("pool", DMA(ptab[:, :], ptab_d[ti, :, :]), "ld_ptab", writes=("ptab",))
        if not is_sample:
            zrow = NT * 128
            r1row = XFR + (ti - 1) * 128 if ti > 0 else zrow
            R.dma("pool", DMA(h0[:, :, :], xf_out_ap[r1row:r1row + 128, :].rearrange("p (c t) -> p c t", c=8)), "ld_h0",
                  reads=("xf_out",), writes=("h0",))
            R.dma("pool", DMA(h1[:, :, :], xf_out_ap[ti * 128:(ti + 1) * 128, :].rearrange("p (c t) -> p c t", c=8)), "ld_h1",
                  reads=("xf_out",), writes=("h1",))
            R.op("dve", TSC(h0[:, :, :], h0[:, :, :], sel[:, 0:1], None, ALU.mult), reads=("h0", "sel"), writes=("h0",))
            R.op("dve", STT(ue[:, :, 0, 0:16], h1[:, :, :], sel[:, 1:2], h0[:, :, :], ALU.mult, ALU.add),
                 reads=("h0", "h1", "sel", "uext"), writes=("uext",))
        else:
            hv = histT.rearrange("p (c s t) -> p c s t", c=8, s=4)
            for c in range(8):
                R.dma("pool", DMA(ue[:, c, :, 0:16], hv[:, c, :, :]), "ld_u", reads=(), writes=("uext",))
        def pool_chunk(c):
            g = c // 2
            u_c = ue[:, c, :, :]
            cur, curk = u_c, "uext"
            bufs = [(sA, "sA"), (sB, "sB")]
            sh = 1
            for step in range(g + 1):
                dstt, dk = bufs[step % 2]
                dst = dstt[:, 0:nseq * E].rearrange("p (s e) -> p s e", s=nseq)
                lo = 2 * sh - 1
                R.op("dve", TT(dst[:, :, lo:E], cur[:, :, lo:E], cur[:, :, lo - sh:E - sh], ALU.add), reads=(curk,), writes=(dk,))
                cur, curk = dst, dk
                sh *= 2
            w = WINS[g]
            dv = dT[:, c, 0:T].rearrange("p (s l) -> p s l", s=nseq)
            R.op("dve", STT(dv, cur[:, :, 16:E], 1.0 / w, u_c[:, :, 16:E], ALU.mult, ALU.subtract), reads=(curk, "uext"), writes=(f"dT{c}",))
            if not is_sample:
                tf = tmp[5][:, 0:16]
                R.op("dve", TT(tf, cur[:, 0, 16:32], ptab[:, g * 16:(g + 1) * 16], ALU.mult), reads=(curk, "ptab"), writes=("tmp5",))
                R.op("dve", TT(dT[:, c, 0:16], tf, u_c[:, 0, 16:32], ALU.subtract), reads=("tmp5", "uext"), writes=(f"dT{c}",))

        return [(lambda c=c: pool_chunk(c)) for c in range(8)]

    def phaseB2(ti, T, p_tok, y_out, is_sample, nxt=None):
        nseq = 4 if is_sample else 1
        L = T // nseq
        E = 16 + L
        xkeys = tuple(f"x32_{c}" for c in range(NCH))
        R.dma("pool", DMA(x32[:, :, 0:T], x1sp[ti, :, 0:NCH * T].rearrange("p (c t) -> p c t", c=NCH)), "ld_x1",
              reads=(f"x1sp{ti}",), writes=xkeys)
        if ti not in oa_ready:
            R.dma("pool", DMA(xb[:, 0:8, 0:T], oasp[ti, :, 0:8 * T].rearrange("p (h t) -> p h t", h=8)), "ld_oa",
                  reads=(f"oasp{ti}",), writes=tuple(f"xb_{c}" for c in range(8)))
        for blk in range(T // 128):
            R.dma("pool", DMA(ptok[:, blk, :], p_tok[blk * 128:(blk + 1) * 128, :]), "ld_p", writes=("ptok",))
        for f_ in pool_prepare(ti, T, is_sample):
            f_()
        for blk in range(T // 128):
            bnk = blk % 2
            R.op("pe", [TR(ps[bnk][:, i * 128:(i + 1) * 128], ptok[:, blk, i * 128:(i + 1) * 128], id32[:]) for i in range(2)],
                 reads=("ptok", "id32"), writes=(f"ps{bnk}",))
            R.op("dve", CP(pT[:, :, blk * 128:(blk + 1) * 128], ps[bnk][:, 0:256].rearrange("p (a b) -> p a b", a=2)),
                 reads=(f"ps{bnk}",), writes=("pT",))
        slot, wk = wload([(0, w_pool.rearrange("(k p) d -> p k d", p=128))], bid=("wpool",))
        wp = wview(slot, 0, 8, 256)
        for g in range(4):
            for dc in range(2):
                bnk = (2 * g + dc) % 4
                R.op("pe", [MM(ps[bnk][:, 0:T], wp[:, 2 * g + cc, dc * 128:(dc + 1) * 128], dT[:, 2 * g + cc, 0:T], cc == 0, cc == 1) for cc in range(2)],
                     reads=(f"dT{2 * g}", f"dT{2 * g + 1}", wk), writes=(f"ps{bnk}",))
                R.op("dve", TSC(xb[:, 8 + 2 * g + dc, 0:T], ps[bnk][:, 0:T], psc[:, 2 * g + dc:2 * g + dc + 1], None, ALU.mult),
                     reads=(f"ps{bnk}", "psc"), writes=(f"xb_{8 + 2 * g + dc}",))
        xbkeys = tuple(f"xb_{c}" for c in range(NCH))
        wov = w_o.rearrange("(k p) f -> p k f", p=128)
        hk2 = StatsHook(T, x32, "x32_", 0, 1)
        for dg in range(4):
            slot, wk = wload([(0, wov[:, :, dg * 512:(dg + 1) * 512])], bid=("wo", dg))
            wv = wview(slot, 0, 16, 512)
            for dc in range(4):
                c = dg * 4 + dc
                bnk = 4 + dc
                R.op("pe", [MM(ps[bnk][:, 0:T], wv[:, k, dc * 128:(dc + 1) * 128], xb[:, k, 0:T], k == 0, k == 15) for k in range(16)],
                     reads=xbkeys + (wk,), writes=(f"ps{bnk}",))
                R.op("dve", STT(x32[:, c, 0:T], x32[:, c, 0:T], ALPHA, ps[bnk][:, 0:T], ALU.mult, ALU.add),
                     reads=(f"ps{bnk}", f"x32_{c}"), writes=(f"x32_{c}",))
                hk2.chunk(c)
        def dbg(k):
            if KSUB == k:
                R.dma("pool", DMA(x1sp[ti, :, 0:NCH * T].rearrange("p (c t) -> p c t", c=NCH), x32[:, :, 0:T]), "st_dbg", reads=xkeys)
                raise _Stop()
        dbg(20)
        layer_norm2(T, 1, ALPHA, pre=hk2)
        dbg(21)
        bg = pool_prepare(*nxt) if nxt is not None else []
        hk3 = StatsHook(T, x32, "x32_", 0, 1)
        ffn(T, x32, xb, aT, w_f2gu, w_f2d, "x32_", "xb_", "f2", hook=hk3, bgw=bg)
        layer_norm2(T, 2, ALPHA, pre=hk3)
        dbg(22)
        wpp, wkp = wppb, "wppb"
        wgv = w_pg.rearrange("(k p) f -> p k f", p=128)
        hk4 = StatsHook(T, x32, "x32_", 4, 5)
        for dg in range(4):
            slot, wk = wload([(0, wgv[:, :, dg * 512:(dg + 1) * 512])], bid=("wpg", dg))
            wv = wview(slot, 0, 16, 512)
            for dc in range(4):
                c = dg * 4 + dc
                par = c % 2
                bg, bp = 2 * par, 2 * par + 1
                R.op("pe", [MM(ps[bg][:, 0:T], wv[:, k, dc * 128:(dc + 1) * 128], xb[:, k, 0:T], k == 0, k == 15) for k in range(16)],
                     reads=xbkeys + (wk,), writes=(f"ps{bg}",))
                R.op("pe", [MM(ps[bp][:, 0:T], wpp[:, k, c * 128:(c + 1) * 128], pT[:, k, 0:T], k == 0, k == 1) for k in range(2)],
                     reads=("pT", wkp), writes=(f"ps{bp}",))
                tg = tmp[par][:, 0:T]
                R.op("act", ACTF(tg, ps[bg][:, 0:T], AF.Sigmoid), reads=(f"ps{bg}",), writes=(f"tmp{par}",))
                R.op("dve", TT(tg, tg, ps[bp][:, 0:T], ALU.mult), reads=(f"tmp{par}", f"ps{bp}"), writes=(f"tmp{par}",))
                R.op("dve", TT(x32[:, c, 0:T], x32[:, c, 0:T], tg, ALU.add), reads=(f"tmp{par}", f"x32_{c}"), writes=(f"x32_{c}",))
                hk4.chunk(c)
        dbg(23)
        if nxt is not None:
            nti, nT = nxt[0], nxt[1]
            R.dma("pool", DMA(xb[:, 0:8, 0:nT], oasp[nti, :, 0:8 * nT].rearrange("p (h t) -> p h t", h=8)), "ld_oa",
                  reads=(f"oasp{nti}",), writes=tuple(f"xb_{c}" for c in range(8)))
            oa_ready.add(nti)
        layer_norm2(T, 3, None, pre=hk4, write_xb=False)
        for blk in range(T // 128):
            yt = ytok[blk % 2]
            ytk = f"ytok{blk % 2}"
            for g in range(4):
                bnk = 4 + g
                R.op("pe", [TR(ps[bnk][:, i * 128:(i + 1) * 128], x32[:, 4 * g + i, blk * 128:(blk + 1) * 128], id32[:]) for i in range(4)],
                     reads=tuple(f"x32_{4 * g + i}" for i in range(4)) + ("id32",), writes=(f"ps{bnk}",))
                if g % 2 == 0:
                    R.op("act", ACTF(yt[:, g * 512:(g + 1) * 512], ps[bnk][:, :], AF.Copy), reads=(f"ps{bnk}",), writes=(ytk,))
                else:
                    R.op("dve", CP(yt[:, g * 512:(g + 1) * 512], ps[bnk][:, :]), reads=(f"ps{bnk}",), writes=(ytk,))
            R.dma("pool", DMA(y_out[blk * 128:(blk + 1) * 128, :], yt[:, :]), f"st_y{blk % 2}", reads=(ytk,))

    if STAGE == 6:
        return finish()
    order = [(NT, TS, pss, ys, True)] if KSAMPLE else \
        [(0, TP, pp[0], yp[0], False), (NT, TS, pss, ys, True)] + [(li, TP, pp[li], yp[li], False) for li in range(1, NT)]
    try:
        for i, (ti_, T_, p_, y_, smp_) in enumerate(order):
            nxt = (order[i + 1][0], order[i + 1][1], order[i + 1][4]) if i + 1 < len(order) else None
            phaseB2(ti_, T_, p_, y_, smp_, nxt=nxt)
            if STAGE == 7 and smp_:
                return finish()
    except _Stop:
        return finish()
    R.barrier()
    return nc, R, es


def emit(nc, R, es):
    sems = {}
    for name in R.semnames:
        sems[name] = es.enter_context(nc.semaphore(name))
    block = es.enter_context(nc.Block())

    def replay(eng, q):
        for it in q:
            if it[0] == "w":
                eng.wait_ge(sems[it[1]], it[2])
            else:
                ins = it[1](eng)
                if it[2] is not None:
                    if it[3] is None:
                        ins.then_inc(sems[it[2]])
                    else:
                        ins.then_inc(sems[it[2]], it[3])

    @block.tensor
    def _(e):
        replay(e, R.q["pe"])

    @block.scalar
    def _(e):
        replay(e, R.q["act"])

    @block.vector
    def _(e):
        replay(e, R.q["dve"])

    @block.gpsimd
    def _(e):
        replay(e, R.q["pool"])

    @block.sync
    def _(e):
        replay(e, R.q["sp"])


_CACHE = {}


def get_program():
    if "nc" not in _CACHE:
        nc, R, es = build_program()
        emit(nc, R, es)
        es.close()
        _CACHE["nc"] = nc
    return _CACHE["nc"]


def _rope_tab(pos):
    inv = (1.0 / (np.float32(10000.0) ** (np.arange(0, 64, 2, dtype=np.float32) / np.float32(64)))).astype(np.float32)
    ang = pos.astype(np.float32)[:, None] * inv[None, :]
    ang = np.concatenate([ang, ang], -1)
    cos = np.cos(ang).astype(np.float32).T
    sin = np.sin(ang).astype(np.float32).T
    sin_signed = sin.copy()
    sin_signed[0:32] *= -1.0
    return cos, sin_signed


def kernel(x_prompt, x_sample, cache_kv_latent, cache_k_rope, state_pool, p_prompt, p_sample,
           ffn1_gu, ffn1_down, ln1_g, ln1_b, w_in, g_q, g_kv, w_uq_nope, w_uq_rope, w_uk, w_uv,
           w_pool, pool_scale, w_o, ln2_g, ln2_b, ffn2_gu, ffn2_down, ln3_g, ln3_b,
           w_ple_gate, w_ple_proj, ln4_g, ln4_b):
    f32 = np.float32
    A = lambda a: np.ascontiguousarray(np.asarray(a, dtype=f32))
    x_prompt = np.asarray(x_prompt, f32); x_sample = np.asarray(x_sample, f32)
    p_prompt = np.asarray(p_prompt, f32); p_sample = np.asarray(p_sample, f32)
    cache_kv_latent = np.asarray(cache_kv_latent, f32); cache_k_rope = np.asarray(cache_k_rope, f32)
    state_pool = np.asarray(state_pool, f32)
    perm = (np.arange(64) + 32) % 64
    w_in0 = np.asarray(w_in, f32)[0]
    w_uqr0 = np.asarray(w_uq_rope, f32)[0]
    fm = lambda v, n: A(np.asarray(v, f32).reshape(n, 128).T)
    lnp = np.concatenate([fm(ln1_g[0], 16), fm(ln1_b[0], 16), fm(ln2_g[0], 16), fm(ln2_b[0], 16),
                          fm(ln3_g[0], 16), fm(ln3_b[0], 16), fm(ln4_g[0], 16), fm(ln4_b[0], 16)], axis=1)
    shared = {
        "ffn1_gu": A(ffn1_gu[0]), "ffn1_down": A(ffn1_down[0]), "ffn2_gu": A(ffn2_gu[0]), "ffn2_down": A(ffn2_down[0]),
        "w_in": A(w_in0), "w_in_perm": A(w_in0[:, 768:832][:, perm]),
        "w_uqn": A(np.asarray(w_uq_nope, f32)[0].reshape(512, 1024)),
        "w_uqr": A(w_uqr0.reshape(512, 512)), "w_uqr_perm": A(w_uqr0[:, :, perm].reshape(512, 512)),
        "w_ukT": A(np.asarray(w_uk, f32)[0].transpose(2, 1, 0).reshape(128, 2048)),
        "w_uv": A(np.asarray(w_uv, f32)[0].reshape(256, 1024)),
        "w_pool": A(np.asarray(w_pool, f32)[0].reshape(1024, 256)), "w_o": A(w_o[0]),
        "w_ple_gate": A(w_ple_gate[0]), "w_ple_proj": A(w_ple_proj[0]),
        "lnp": A(lnp), "gq": fm(g_q[0], 4), "gkv": fm(g_kv[0], 2), "pscale": fm(pool_scale[0], 8),
        "ident32": np.eye(128, dtype=f32), "identb": np.eye(128, dtype=f32).astype(ml_dtypes.bfloat16),
    }
    in_maps = []
    for c in range(8):
        b, h = c // 2, c % 2
        m = dict(shared)
        m["xp"] = A(x_prompt[b].reshape(16, TP, D)[h::2])
        m["pp"] = A(p_prompt[0, b].reshape(16, TP, 256)[h::2])
        m["xs"] = A(x_sample[4 * c:4 * c + 4].reshape(TS, D))
        m["ps"] = A(p_sample[0, 4 * c:4 * c + 4].reshape(TS, 256))
        m["ckvc"] = A(cache_kv_latent[0, 4 * c:4 * c + 4])
        m["ckrc"] = A(cache_k_rope[0, 4 * c:4 * c + 4])
        hist = state_pool[0, 4 * c:4 * c + 4]
        hp = np.zeros((4, 16, 1024), f32)
        hp[:, 1:16] = hist
        m["histT"] = A(hp.reshape(4, 16, 8, 128).transpose(3, 2, 0, 1).reshape(128, 8 * 4 * 16))
        rt = np.zeros((NT + 1, 64, 2 * TP), f32)
        ptb = np.zeros((NT + 1, 128, 64), f32)
        for li in range(NT):
            gt = 2 * li + h
            cos, sn = _rope_tab(gt * TP + np.arange(TP))
            rt[li, :, 0:TP] = cos
            rt[li, :, TP:] = sn
            for g, w in enumerate(WINS):
                pos = gt * TP + np.arange(16)
                ptb[li, :, g * 16:(g + 1) * 16] = (1.0 / np.minimum(w, pos + 1).astype(f32))[None, :]
        cos, sn = _rope_tab(2048 + (np.arange(TS) % 64))
        rt[NT, :, 0:TS] = cos
        rt[NT, :, TP:TP + TS] = sn
        for g, w in enumerate(WINS):
            ptb[NT, :, g * 16:(g + 1) * 16] = 1.0 / w
        m["ropetab"] = rt
        m["pooltab"] = ptb
        mk = np.zeros((128, 8, TP), f32)
        kk = np.arange(128)[:, None]
        qq = np.arange(TP)[None, :]
        for jj in range(8):
            key_chunk = jj * 2 + kk // 64
            q_chunk = h * 8 + qq // 64
            mk[:, jj, :] = np.where(key_chunk <= q_chunk, 0.0, NEG)
        m["maskb"] = A(mk.reshape(128, 8 * TP)).astype(ml_dtypes.bfloat16)
        sel = np.zeros((128, 2), f32)
        sel[:, h] = 1.0
        m["sel"] = sel
        in_maps.append(m)

    nc = get_program()
    res = run_bass_kernel_spmd(nc, in_maps[:KCORES], core_ids=list(range(KCORES)))
    outs = list(res.results) + [res.results[0]] * (8 - KCORES)
    y_p = np.zeros((4, 8192, D), f32); y_s = np.zeros((32, 64, D), f32)
    kv_p = np.zeros((1, 4, 8192, 256), f32); kr_p = np.zeros((1, 4, 8192, 64), f32); pool_p = np.zeros((1, 4, 15, 1024), f32)
    kv_s = np.zeros((1, 32, 64, 256), f32); kr_s = np.zeros((1, 32, 64, 64), f32); pool_s = np.zeros((1, 32, 15, 1024), f32)
    for c in range(8):
        b, h = c // 2, c % 2
        o = outs[c]
        y_p[b].reshape(16, TP, D)[h::2] = np.asarray(o["yp"], f32)
        kv_p[0, b].reshape(16, TP, 256)[h::2] = np.asarray(o["okvp"], f32)
        kr_p[0, b].reshape(16, TP, 64)[h::2] = np.asarray(o["okrp"], f32)
        if h == 1:
            pool_p[0, b] = np.asarray(o["opoolp"], f32)[1:16]
        y_s[4 * c:4 * c + 4] = np.asarray(o["ys"], f32).reshape(4, 64, D)
        kv_s[0, 4 * c:4 * c + 4] = np.asarray(o["okvs"], f32).reshape(4, 64, 256)
        kr_s[0, 4 * c:4 * c + 4] = np.asarray(o["okrs"], f32).reshape(4, 64, 64)
        pool_s[0, 4 * c:4 * c + 4] = np.asarray(o["opools"], f32).reshape(16, 4, 1024).transpose(1, 0, 2)[:, 1:16]
    return (y_p, y_s, kv_p, kr_p, pool_p, kv_s, kr_s, pool_s)
```

```python
import contextlib
import numpy as np
import ml_dtypes
import concourse.bass as bass
import concourse.mybir as mybir
from concourse.bass_utils import run_bass_kernel_spmd

F32 = mybir.dt.float32
BF16 = mybir.dt.bfloat16
AF = mybir.ActivationFunctionType
ALU = mybir.AluOpType
AX = mybir.AxisListType

D = 2048
DFF = 5632
NCH = 16
ALPHA = 2.0 ** 0.25
SCALE = 192.0 ** -0.5
LN_EPS = 1e-5
RMS_EPS = 1e-6
NT = 8
TP = 512
TS = 256
NEG = -30000.0
WINS = (2, 4, 8, 16)
SLABS = [(0, 11), (11, 11), (22, 11), (33, 11)]


class Rec:
    ENG = ["pe", "act", "dve", "pool", "sp"]

    def __init__(self):
        self.q = {e: [] for e in self.ENG}
        self.cnt = {e: 0 for e in self.ENG}
        self.epoch = {e: 0 for e in self.ENG}
        self.seen = {e: {} for e in self.ENG}
        self.lastw = {}
        self.readers = {}
        self.chan = {}
        self.semnames = []

    def _sem(self, name):
        if name not in self.semnames:
            self.semnames.append(name)
        return name

    def _tok_eng(self, e):
        if self.cnt[e] >= 20000:
            self.epoch[e] += 1
            self.cnt[e] = 0
        self.cnt[e] += 1
        return (self._sem(f"e_{e}_{self.epoch[e]}"), self.cnt[e], e)

    def _deps(self, reads, writes):
        toks = []
        for k in reads:
            t = self.lastw.get(k)
            if t is not None:
                toks.append(t)
        for k in writes:
            t = self.lastw.get(k)
            if t is not None:
                toks.append(t)
            toks.extend(self.readers.get(k, {}).values())
        return toks

    def _commit(self, tok, reads, writes):
        for k in reads:
            d = self.readers.setdefault(k, {})
            o = d.get(tok[0])
            if o is None or o[1] < tok[1]:
                d[tok[0]] = tok
        for k in writes:
            self.lastw[k] = tok
            self.readers[k] = {}

    def _emit_waits(self, e, toks):
        for (name, val, src) in toks:
            if src == e and e == "pe":
                continue
            if self.seen[e].get(name, 0) >= val:
                continue
            self.seen[e][name] = val
            self.q[e].append(("w", name, val))

    @staticmethod
    def _excl(reads, writes):
        extra = tuple(k for k in reads if k.startswith("ps") and k[2:].isdigit() and k not in writes)
        return tuple(reads), tuple(writes) + extra

    def op(self, e, fns, reads=(), writes=()):
        if callable(fns):
            fns = [fns]
        reads, writes = self._excl(reads, writes)
        self._emit_waits(e, self._deps(reads, writes))
        tok = self._tok_eng(e)
        for f in fns[:-1]:
            self.q[e].append(("i", f, None, 0))
        self.q[e].append(("i", fns[-1], tok[0], 1))
        self._commit(tok, reads, writes)
        return tok

    def dma(self, e, fn, chan, reads=(), writes=(), join=False):
        toks = self._deps(reads, writes)
        if join:
            toks = [t for t in toks if t[0] != f"d_{chan}"]
        self._emit_waits(e, toks)
        n = self.chan.get(chan, 0) + 1
        self.chan[chan] = n
        tok = (self._sem(f"d_{chan}"), 16 * n, "dma")
        self.q[e].append(("i", fn, tok[0], 16))
        self._commit(tok, reads, writes)
        return tok

    def coll(self, fn, chan, reads=(), writes=()):
        self._emit_waits("pool", self._deps(reads, writes))
        tok = (self._sem(f"c_{chan}"), 1, "cc")
        self.q["pool"].append(("i", fn, tok[0], None))
        self._commit(tok, reads, writes)
        return tok

    def all_tokens(self):
        toks = []
        for e in self.ENG:
            for ep in range(self.epoch[e] + 1):
                name = f"e_{e}_{ep}"
                if name in self.semnames:
                    toks.append((name, self.cnt[e] if ep == self.epoch[e] else 20000, e))
        for ch, n in self.chan.items():
            toks.append((f"d_{ch}", 16 * n, "dma"))
        return toks

    def barrier(self):
        toks = self.all_tokens()
        for e in self.ENG:
            self._emit_waits(e, [t for t in toks if not (t[2] == e)])
        self.lastw = {}
        self.readers = {}


def MM(out, l, r, st, sp):
    return lambda e: e.matmul(out, lhsT=l, rhs=r, start=st, stop=sp)


def TR(out, in_, ident):
    return lambda e: e.transpose(out, in_, ident)


def ACTF(out, in_, func, scale=None, bias=None):
    def f(e):
        kw = {}
        if scale is not None:
            kw["scale"] = scale
        if bias is not None:
            kw["bias"] = bias
        return e.activation(out=out, in_=in_, func=func, **kw)
    return f


def TT(out, a, b, op):
    return lambda e: e.tensor_tensor(out=out, in0=a, in1=b, op=op)


def TSC(out, a, s1, s2, op0, op1=None):
    def f(e):
        if op1 is None:
            return e.tensor_scalar(out=out, in0=a, scalar1=s1, scalar2=None, op0=op0)
        return e.tensor_scalar(out=out, in0=a, scalar1=s1, scalar2=s2, op0=op0, op1=op1)
    return f


def STT(out, a, s, b, op0, op1):
    return lambda e: e.scalar_tensor_tensor(out=out, in0=a, scalar=s, in1=b, op0=op0, op1=op1)


def CP(out, in_):
    return lambda e: e.tensor_copy(out=out, in_=in_)


def MS(ap, v):
    return lambda e: e.memset(ap, v)


def RCP(out, in_):
    return lambda e: e.reciprocal(out=out, in_=in_)


def RED(out, in_, op):
    return lambda e: e.tensor_reduce(out=out, in_=in_, axis=AX.X, op=op)


def DMA(out, in_):
    return lambda e: e.dma_start(out=out, in_=in_)


def DMAS(out, in_):
    return lambda e: e.dma_start(out=out, in_=in_, allow_slow_non_contiguous=True)


import os
STAGE = int(os.environ.get("KSTAGE", "99"))
KSUB = int(os.environ.get("KSUB", "99"))
KCORES = int(os.environ.get("KCORES", "8"))
KSAMPLE = int(os.environ.get("KSAMPLE", "0"))
KNT = int(os.environ.get("KNT", "8"))


class _Stop(Exception):
    pass


def sub(k):
    if KSUB == k:
        raise _Stop()


def build_program():
    nc = bass.Bass("TRN2", target_bir_lowering=False)
    R = Rec()
    es = contextlib.ExitStack()

    def finish():
        R.barrier()
        return nc, R, es

    def din(name, shape, dt=F32):
        return nc.dram_tensor(name, list(shape), dt, kind="ExternalInput").ap()

    def dout(name, shape, dt=F32):
        return nc.dram_tensor(name, list(shape), dt, kind="ExternalOutput").ap()

    def dscr(name, shape, dt=F32):
        return nc.dram_tensor(name, list(shape), dt).ap()

    xp = din("xp", [NT, TP, D]); xs = din("xs", [TS, D])
    pp = din("pp", [NT, TP, 256]); pss = din("ps", [TS, 256])
    ckvc = din("ckvc", [4, 2048, 256]); ckrc = din("ckrc", [4, 2048, 64])
    histT = din("histT", [128, 8 * 4 * 16])
    w_f1gu = din("ffn1_gu", [D, 2 * DFF]); w_f1d = din("ffn1_down", [DFF, D])
    w_f2gu = din("ffn2_gu", [D, 2 * DFF]); w_f2d = din("ffn2_down", [DFF, D])
    w_in = din("w_in", [D, 1856]); w_inp = din("w_in_perm", [D, 64])
    w_uqn = din("w_uqn", [512, 1024]); w_uqr = din("w_uqr", [512, 512]); w_uqrp = din("w_uqr_perm", [512, 512])
    w_ukT = din("w_ukT", [128, 8 * 256]); w_uv = din("w_uv", [256, 1024])
    w_pool = din("w_pool", [4 * 256, 256]); w_o = din("w_o", [D, D])
    w_pg = din("w_ple_gate", [D, D]); w_pp = din("w_ple_proj", [256, D])
    lnp_d = din("lnp", [128, 8 * 16]); gq_d = din("gq", [128, 4]); gkv_d = din("gkv", [128, 2]); psc_d = din("pscale", [128, 8])
    id32_d = din("ident32", [128, 128]); idb_d = din("identb", [128, 128], BF16)
    rope_d = din("ropetab", [NT + 1, 64, 2 * TP])
    mask_d = din("maskb", [128, 8 * TP], BF16)
    ptab_d = din("pooltab", [NT + 1, 128, 64])
    sel_d = din("sel", [128, 2])

    yp = dout("yp", [NT, TP, D]); ys = dout("ys", [TS, D])
    okvp = dout("okvp", [NT, TP, 256]); okrp = dout("okrp", [NT, TP, 64]); opoolp = dout("opoolp", [16, 1024])
    okvs = dout("okvs", [TS, 256]); okrs = dout("okrs", [TS, 64]); opools = dout("opools", [16, 4 * 1024])

    x1sp = dscr("x1sp", [NT + 1, 128, NCH * TP])
    cqsp = dscr("cqsp", [NT + 1, 128, 4 * TP], BF16)
    usp = dscr("usp", [NT + 1, 128, 8 * TP])
    oasp = dscr("oasp", [NT + 1, 128, 8 * TP], BF16)
    XBR = 288
    xb_in = [nc.dram_tensor(f"xb_in{k}", [2 * XBR, 1024], BF16) for k in range(4)]
    xb_out = [nc.dram_tensor(f"xb_out{k}", [4 * XBR, 1024], BF16) for k in range(4)]
    XFR = NT * 128 + 256
    xf_in = nc.dram_tensor("xf_in", [XFR, 128], F32)
    xf_out = nc.dram_tensor("xf_out", [2 * XFR, 128], F32)

    def sb(name, shape, dt=F32):
        return es.enter_context(nc.sbuf_tensor("sb_" + name, list(shape), dt))

    id32 = sb("id32", [128, 128]); idb = sb("idb", [128, 128], BF16); onesb = sb("onesb", [128, 128], BF16)
    lnp = sb("lnp", [128, 8 * 16]); gq = sb("gq", [128, 4]); gkv = sb("gkv", [128, 2]); psc = sb("psc", [128, 8])
    sel = sb("sel", [128, 2])
    lnpa = sb("lnpa", [128, 4 * 16])
    nkmax = sb("nkmax", [128, 4]); nkt = sb("nkt", [128, 8]); negnk = sb("negnk", [128, 8])
    wsl = [sb(f"wsl{i}", [128, 8192], BF16) for i in range(3)]
    tmp = [sb(f"tmp{i}", [128, 512]) for i in range(6)]
    UB = 136 * 1024
    U = sb("U", [128, UB // 4])
    ps = [es.enter_context(nc.psum_tensor(f"psum{i}", [128, 512], F32)) for i in range(8)]

    class Carver:
        def __init__(self):
            self.off = 0

        def take(self, shape_free, dt):
            n = int(np.prod(shape_free))
            nbytes = n * (4 if dt == F32 else 2)
            nbytes4 = (nbytes + 31) // 32 * 32
            a = self.off // 4
            self.off += nbytes4
            assert self.off <= UB, f"U overflow {self.off} > {UB}"
            v = U[:, a:a + nbytes4 // 4]
            if dt != F32:
                v = v.bitcast(BF16)
            v = v[:, 0:n]
            if len(shape_free) == 2:
                v = v.rearrange("p (a b) -> p a b", a=shape_free[0])
            elif len(shape_free) == 3:
                v = v.rearrange("p (a b c) -> p a b c", a=shape_free[0], b=shape_free[1])
            elif len(shape_free) == 4:
                v = v.rearrange("p (a b c d) -> p a b c d", a=shape_free[0], b=shape_free[1], c=shape_free[2])
            return v

    def psb(i):
        return ps[i][:].bitcast(BF16)

    wstate = {"i": 0}
    wcache = {}
    NWBLK = 100
    wscr = dscr("wscr", [NWBLK, 128, 8192], BF16)

    def wload(parts, bid=None):
        s = wstate["i"] % 3
        wstate["i"] += 1
        key = f"wsl{s}"
        used = max(off + src.shape[1] * src.shape[2] for (off, src) in parts)
        if bid is not None and bid in wcache:
            idx = wcache[bid]
            R.dma("sp", DMA(wsl[s][:, 0:used], wscr[idx, :, 0:used]), f"wc{s}", reads=(f"wscr{idx}",), writes=(key,))
            return wsl[s], key
        for (off, src) in parts:
            a, b = src.shape[1], src.shape[2]
            dst = wsl[s][:, off:off + a * b].rearrange("p (a b) -> p a b", a=a)
            R.dma("pool", DMA(dst, src), f"w{s}", reads=(), writes=(key,), join=True)
        if bid is not None:
            idx = len(wcache)
            assert idx < NWBLK
            wcache[bid] = idx
            R.dma("sp", DMA(wscr[idx, :, 0:used], wsl[s][:, 0:used]), f"wb{s}", reads=(key,), writes=(f"wscr{idx}",))
        return wsl[s], key

    def wview(slot, off, a, b):
        return slot[:, off:off + a * b].rearrange("p (a b) -> p a b", a=a)

    for (dst, src, k) in [(id32, id32_d, "id32"), (idb, idb_d, "idb"), (lnp, lnp_d, "lnp"), (gq, gq_d, "gq"),
                          (gkv, gkv_d, "gkv"), (psc, psc_d, "psc"), (sel, sel_d, "sel")]:
        R.dma("pool", DMA(dst[:], src[:, :]), "const", writes=(k,))
    R.op("dve", MS(onesb[:], 1.0), writes=("onesb",))
    R.op("dve", MS(nkmax[:], 0.0), writes=("nkmax",))
    R.barrier()
    for i in range(4):
        R.op("dve", TSC(lnpa[:, i * 16:(i + 1) * 16], lnp[:, (2 * i + 1) * 16:(2 * i + 2) * 16], ALPHA, None, ALU.mult),
             reads=("lnp",), writes=("lnpa",))
    R.barrier()

    class StatsHook:
        def __init__(self, T, x32, xk, mean_bank, ex2_bank):
            self.T, self.x32, self.xk, self.mb, self.eb = T, x32, xk, mean_bank, ex2_bank
            self.q = []

        def chunk(self, c):
            T = self.T
            i = c % 4
            R.op("act", ACTF(lnsq[:, i, 0:T], self.x32[:, c, 0:T], AF.Copy), reads=(f"{self.xk}{c}",), writes=(f"lnsqA{i}",))
            R.op("act", ACTF(lnsq[:, 4 + i, 0:T], self.x32[:, c, 0:T], AF.Square), reads=(f"{self.xk}{c}",), writes=(f"lnsqB{i}",))

            def pe(c=c, i=i):
                R.op("pe", MM(ps[self.mb][:, 0:T], onesb[:], lnsq[:, i, 0:T], c == 0, c == 15), reads=(f"lnsqA{i}", "onesb"), writes=(f"ps{self.mb}",))
                R.op("pe", MM(ps[self.eb][:, 0:T], onesb[:], lnsq[:, 4 + i, 0:T], c == 0, c == 15), reads=(f"lnsqB{i}", "onesb"), writes=(f"ps{self.eb}",))
            self.q.append(pe)
            while len(self.q) > 2:
                self.q.pop(0)()

        def flush(self):
            while self.q:
                self.q.pop(0)()

    def layer_norm(T, x32, xb, lnidx, xk, xbk, post_scale=None, pre=None, write_xb=True):
        g = lnp[:, (2 * lnidx) * 16:(2 * lnidx + 1) * 16]
        b = lnp[:, (2 * lnidx + 1) * 16:(2 * lnidx + 2) * 16]
        mb, eb = (pre.mb, pre.eb) if pre is not None else (0, 1)
        mean_ps, ex2_ps = ps[mb][:, 0:T], ps[eb][:, 0:T]
        if pre is not None:
            pre.flush()
        for hh in range(0 if pre is not None else 4):
            cs = slice(hh * 4, hh * 4 + 4)
            ls = slice((hh % 2) * 4, (hh % 2) * 4 + 4)
            lk = f"lnsq{hh % 2}"
            keys = tuple(f"{xk}{c}" for c in range(hh * 4, hh * 4 + 4))
            bkeys = tuple(f"{xbk}{c}" for c in range(hh * 4, hh * 4 + 4))
            R.op("dve", CP(xb[:, cs, 0:T], x32[:, cs, 0:T]), reads=keys, writes=bkeys)
            R.op("act", ACTF(lnsq[:, ls, 0:T], x32[:, cs, 0:T], AF.Square), reads=keys, writes=(lk,))
            fns = []
            for c in range(4):
                cc = hh * 4 + c
                fns.append(MM(mean_ps, onesb[:], xb[:, cc, 0:T], cc == 0, cc == 15))
            R.op("pe", fns, reads=bkeys + ("onesb",), writes=("ps0",))
            fns = []
            for c in range(4):
                cc = hh * 4 + c
                fns.append(MM(ex2_ps, onesb[:], lnsq[:, (hh % 2) * 4 + c, 0:T], cc == 0, cc == 15))
            R.op("pe", fns, reads=(lk, "onesb"), writes=("ps1",))
        mean, rstd, nmr, t3 = tmp[0][:, 0:T], tmp[1][:, 0:T], tmp[2][:, 0:T], tmp[3][:, 0:T]
        R.op("act", ACTF(mean, mean_ps, AF.Copy, scale=1.0 / D), reads=(f"ps{mb}",), writes=("tmp0",))
        R.op("dve", TT(t3, mean, mean, ALU.mult), reads=("tmp0",), writes=("tmp3",))
        R.op("dve", STT(rstd, ex2_ps, 1.0 / D, t3, ALU.mult, ALU.subtract), reads=(f"ps{eb}", "tmp3"), writes=("tmp1",))
        R.op("dve", TSC(rstd, rstd, LN_EPS, None, ALU.add), reads=("tmp1",), writes=("tmp1",))
        R.op("act", ACTF(rstd, rstd, AF.Sqrt), reads=("tmp1",), writes=("tmp1",))
        R.op("dve", RCP(rstd, rstd), reads=("tmp1",), writes=("tmp1",))
        R.op("dve", STT(nmr, mean, -1.0, rstd, ALU.mult, ALU.mult), reads=("tmp0", "tmp1"), writes=("tmp2",))
        ba = lnpa[:, lnidx * 16:(lnidx + 1) * 16]
        for c in range(NCH):
            k, bk = f"{xk}{c}", f"{xbk}{c}"
            ta = tmp[4 + (c % 2)][:, 0:T]
            tk = f"tmp{4 + (c % 2)}"
            R.op("dve", STT(ta, x32[:, c, 0:T], g[:, c:c + 1], rstd, ALU.mult, ALU.mult), reads=(k, "tmp1", "lnp"), writes=(tk,))
            R.op("dve", STT(ta, nmr, g[:, c:c + 1], ta, ALU.mult, ALU.add), reads=(tk, "tmp2", "lnp"), writes=(tk,))
            if write_xb:
                R.op("act", ACTF(xb[:, c, 0:T], ta, AF.Identity, bias=b[:, c:c + 1]), reads=(tk, "lnp"), writes=(bk,))
            if post_scale is None:
                R.op("act", ACTF(x32[:, c, 0:T], ta, AF.Identity, bias=b[:, c:c + 1]), reads=(tk, "lnp"), writes=(k,))
            else:
                R.op("act", ACTF(x32[:, c, 0:T], ta, AF.Identity, scale=post_scale, bias=ba[:, c:c + 1]), reads=(tk, "lnpa"), writes=(k,))

    def ffn(T, x32, xb, aT, wgu, wd, xk, xbk, wname, hook=None, bgw=None):
        wguv = wgu.rearrange("(k p) f -> p k f", p=128)
        wdv = wd.rearrange("(k p) d -> p k d", p=128)
        xbkeys = tuple(f"{xbk}{c}" for c in range(NCH))
        fcount = 0
        for (f0, nf) in SLABS:
            fl = 0
            while fl < nf:
                n = min(2, nf - fl)
                fa = f0 + fl
                slot, wk = wload([(0, wguv[:, :, fa * 128:(fa + n) * 128]),
                                  (16 * n * 128, wguv[:, :, DFF + fa * 128:DFF + (fa + n) * 128])], bid=(wname, "gu", fa))
                gv = wview(slot, 0, 16, n * 128)
                uv = wview(slot, 16 * n * 128, 16, n * 128)
                for i in range(n):
                    par = fcount % 2
                    fcount += 1
                    bg, bu = 2 * par, 2 * par + 1
                    R.op("pe", [MM(ps[bg][:, 0:T], gv[:, k, i * 128:(i + 1) * 128], xb[:, k, 0:T], k == 0, k == 15)
                                for k in range(16)], reads=xbkeys + (wk,), writes=(f"ps{bg}",))
                    R.op("pe", [MM(ps[bu][:, 0:T], uv[:, k, i * 128:(i + 1) * 128], xb[:, k, 0:T], k == 0, k == 15)
                                for k in range(16)], reads=xbkeys + (wk,), writes=(f"ps{bu}",))
                    ts_ = tmp[par][:, 0:T]
                    R.op("act", ACTF(ts_, ps[bg][:, 0:T], AF.Silu), reads=(f"ps{bg}",), writes=(f"tmp{par}",))
                    R.op("dve", TT(aT[:, fl + i, 0:T], ts_, ps[bu][:, 0:T], ALU.mult),
                         reads=(f"tmp{par}", f"ps{bu}"), writes=(f"aT{fl + i}",))
                    if bgw and fcount % 3 == 0:
                        bgw.pop(0)()
                fl += n
            akeys = tuple(f"aT{i}" for i in range(nf))
            for dg in range(4):
                slot, wk = wload([(0, wdv[:, f0:f0 + nf, dg * 512:(dg + 1) * 512])], bid=(wname, "d", f0, dg))
                dv = wview(slot, 0, nf, 512)
                for dc in range(4):
                    bnk = 4 + dc
                    c = dg * 4 + dc
                    R.op("pe", [MM(ps[bnk][:, 0:T], dv[:, k, dc * 128:(dc + 1) * 128], aT[:, k, 0:T], k == 0, k == nf - 1)
                                for k in range(nf)], reads=akeys + (wk,), writes=(f"ps{bnk}",))
                    R.op("dve", STT(x32[:, c, 0:T], ps[bnk][:, 0:T], 0.5, x32[:, c, 0:T], ALU.mult, ALU.add),
                         reads=(f"ps{bnk}", f"{xk}{c}"), writes=(f"{xk}{c}",))
                    if hook is not None and f0 == SLABS[-1][0]:
                        hook.chunk(c)
        while bgw:
            bgw.pop(0)()

    def load_transposed(T, src_tok, x32, xb, xtok, xk, xbk, scale):
        for blk in range(T // 128):
            xt = xtok[blk % 2]
            xtk = f"xtok{blk % 2}"
            R.dma("pool", DMA(xt[:], src_tok[blk * 128:(blk + 1) * 128, :]), xtk, writes=(xtk,))
            if KSUB == 9:
                continue
            for g in range(4):
                bnk = (blk * 4 + g) % 4
                R.op("pe", [TR(ps[bnk][:, i * 128:(i + 1) * 128], xt[:, (4 * g + i) * 128:(4 * g + i + 1) * 128], id32[:])
                            for i in range(4)], reads=(xtk, "id32"), writes=(f"ps{bnk}",))
                if KSUB == 8:
                    continue
                pv = ps[bnk][:].rearrange("p (a b) -> p a b", a=4)
                keys = tuple(f"{xk}{4 * g + i}" for i in range(4))
                bkeys = tuple(f"{xbk}{4 * g + i}" for i in range(4))
                if KSUB != 6:
                    R.op("act", ACTF(x32[:, 4 * g:4 * g + 4, blk * 128:(blk + 1) * 128], pv, AF.Copy, scale=scale),
                         reads=(f"ps{bnk}",), writes=keys)
                if KSUB != 7:
                    R.op("dve", CP(xb[:, 4 * g:4 * g + 4, blk * 128:(blk + 1) * 128], pv), reads=(f"ps{bnk}",), writes=bkeys)

    def rms_stats(T, src32, nchunk, sq, sqk, srck, bank, eps, rstd_tmp):
        R.op("act", ACTF(sq[:, 0:nchunk, 0:T], src32[:, 0:nchunk, 0:T], AF.Square), reads=srck, writes=(sqk,))
        R.op("pe", [MM(ps[bank][:, 0:T], onesb[:], sq[:, c, 0:T], c == 0, c == nchunk - 1) for c in range(nchunk)],
             reads=(sqk, "onesb"), writes=(f"ps{bank}",))
        rs = tmp[rstd_tmp][:, 0:T]
        tk = f"tmp{rstd_tmp}"
        R.op("dve", TSC(rs, ps[bank][:, 0:T], 1.0 / (128 * nchunk), eps, ALU.mult, ALU.add), reads=(f"ps{bank}",), writes=(tk,))
        R.op("act", ACTF(rs, rs, AF.Sqrt), reads=(tk,), writes=(tk,))
        R.op("dve", RCP(rs, rs), reads=(tk,), writes=(tk,))
        return rs, tk

    cv = Carver()
    x32 = cv.take([16, TP], F32)
    xb = cv.take([16, TP], BF16)
    aT = cv.take([11, TP], BF16)
    lnsq = cv.take([8, TP], BF16)
    xtok = [cv.take([D], F32), cv.take([D], F32)]
    cq32 = cv.take([4, TP], F32)
    cqn = cv.take([4, TP], BF16)
    ckv32 = cv.take([2, TP], F32)
    kvb = cv.take([2, TP], BF16)
    sqb = cv.take([4, TP], BF16)
    kr32 = cv.take([TP], F32)
    krb = cv.take([TP], BF16)
    rtab = cv.take([2 * TP], F32)
    kvtok = cv.take([4, 320], F32)
    vtokb = cv.take([4, 256], BF16)
    ust = cv.take([4, TP], F32)
    ptk = cv.take([2, 512], F32)
    s_kvb = sb("s_kvb", [128, 2, TS], BF16)
    s_krb = sb("s_krb", [64, TS], BF16)
    s_vtok = sb("s_vtok", [64, 4, 256], BF16)

    xb_in_aps = [t.ap() for t in xb_in]
    xf_in_ap = xf_in.ap()

    def phaseA(ti, T, src_tok, is_sample):
        nblk = T // 128
        load_transposed(T, src_tok, x32, xb, xtok, "x32_", "xb_", ALPHA)
        sub(6)
        sub(7)
        sub(8)
        sub(9)
        sub(10)
        hk = StatsHook(T, x32, "x32_", 0, 1)
        ffn(T, x32, xb, aT, w_f1gu, w_f1d, "x32_", "xb_", "f1", hook=hk)
        sub(11)
        layer_norm(T, x32, xb, 0, "x32_", "xb_", pre=hk)
        sub(12)
        xkeys = tuple(f"x32_{c}" for c in range(NCH))
        xbkeys = tuple(f"xb_{c}" for c in range(NCH))
        R.dma("pool", DMA(x1sp[ti, :, 0:NCH * T].rearrange("p (c t) -> p c t", c=NCH), x32[:, :, 0:T]), "st_x1",
              reads=xkeys, writes=(f"x1sp{ti}",))
        sub(13)
        winv = w_in.rearrange("(k p) f -> p k f", p=128)
        slot, wk = wload([(0, winv[:, :, 0:512])], bid=("win", "cq"))
        wv = wview(slot, 0, 16, 512)
        for m in range(4):
            R.op("pe", [MM(ps[m][:, 0:T], wv[:, k, m * 128:(m + 1) * 128], xb[:, k, 0:T], k == 0, k == 15) for k in range(16)],
                 reads=xbkeys + (wk,), writes=(f"ps{m}",))
            R.op("act", ACTF(cq32[:, m, 0:T], ps[m][:, 0:T], AF.Copy), reads=(f"ps{m}",), writes=("cq32",))
        sub(14)
        winpv = w_inp.rearrange("(k p) f -> p k f", p=128)
        slot, wk = wload([(0, winv[:, :, 512:832]), (16 * 320, winpv[:, :, 0:64])], bid=("win", "kv"))
        wv = wview(slot, 0, 16, 320)
        wvp = wview(slot, 16 * 320, 16, 64)
        R.dma("pool", DMA(rtab[0:64, :], rope_d[ti, :, :]), "rtab", writes=("rtab",))
        for m in range(2):
            R.op("pe", [MM(ps[m][:, 0:T], wv[:, k, m * 128:(m + 1) * 128], xb[:, k, 0:T], k == 0, k == 15) for k in range(16)],
                 reads=xbkeys + (wk,), writes=(f"ps{m}",))
            R.op("act", ACTF(ckv32[:, m, 0:T], ps[m][:, 0:T], AF.Copy), reads=(f"ps{m}",), writes=("ckv32",))
        R.op("pe", [MM(ps[2][0:64, 0:T], wv[:, k, 256:320], xb[:, k, 0:T], k == 0, k == 15) for k in range(16)],
             reads=xbkeys + (wk,), writes=("ps2",))
        R.op("pe", [MM(ps[3][0:64, 0:T], wvp[:, k, 0:64], xb[:, k, 0:T], k == 0, k == 15) for k in range(16)],
             reads=xbkeys + (wk,), writes=("ps3",))
        kvb_t = s_kvb if is_sample else kvb
        krb_t = s_krb if is_sample else krb
        t1, t2 = tmp[1][0:64, 0:T], tmp[2][0:64, 0:T]
        R.op("dve", TT(t1, ps[2][0:64, 0:T], rtab[0:64, 0:T], ALU.mult), reads=("ps2", "rtab"), writes=("tmp1",))
        R.op("dve", TT(t2, ps[3][0:64, 0:T], rtab[0:64, TP:TP + T], ALU.mult), reads=("ps3", "rtab"), writes=("tmp2",))
        R.op("dve", TT(kr32[0:64, 0:T], t1, t2, ALU.add), reads=("tmp1", "tmp2"), writes=("kr32",))
        R.op("act", ACTF(krb_t[0:64, 0:T], kr32[0:64, 0:T], AF.Copy), reads=("kr32",), writes=("krb",))
        sub(16)
        for ub in range(2):
            slot, wk = wload([(0, winv[:, :, 832 + ub * 512:832 + (ub + 1) * 512])], bid=("win", "u", ub))
            wv = wview(slot, 0, 16, 512)
            for m in range(4):
                R.op("pe", [MM(ps[m][:, 0:T], wv[:, k, m * 128:(m + 1) * 128], xb[:, k, 0:T], k == 0, k == 15) for k in range(16)],
                     reads=xbkeys + (wk,), writes=(f"ps{m}",))
                R.op("act", ACTF(ust[:, m, 0:T], ps[m][:, 0:T], AF.Copy), reads=(f"ps{m}",), writes=("ust",))
            R.dma("pool", DMA(usp[ti, :, ub * 4 * T:(ub + 1) * 4 * T].rearrange("p (c t) -> p c t", c=4), ust[:, :, 0:T]), "st_u",
                  reads=("ust",), writes=(f"usp{ti}",))
            if not is_sample:
                R.dma("pool", DMA(xf_in_ap[ti * 128:(ti + 1) * 128, ub * 64:(ub + 1) * 64].rearrange("p (c t) -> p c t", c=4),
                                ust[:, :, T - 16:T]), "st_xu", reads=("ust",), writes=(f"xf_in_u{ti}_{ub}",))
            if (not is_sample and ti == NT - 1) or is_sample:
                nseq = 4 if is_sample else 1
                L = T // nseq
                for s in range(nseq):
                    pb = (ub * nseq + s) % 2
                    R.op("pe", [TR(ps[5][0:16, m * 128:(m + 1) * 128], ust[:, m, s * L + L - 16:s * L + L], id32[:]) for m in range(4)],
                         reads=("ust", "id32"), writes=("ps5",))
                    R.op("act", ACTF(ptk[0:16, pb, :], ps[5][0:16, 0:512], AF.Copy), reads=("ps5",), writes=(f"ptk{pb}",))
                    dst = opools[:, s * 1024 + ub * 512:s * 1024 + (ub + 1) * 512] if is_sample else opoolp[:, ub * 512:(ub + 1) * 512]
                    R.dma("pool", DMA(dst, ptk[0:16, pb, :]), f"st_pool{pb}", reads=(f"ptk{pb}",))

        rs, rk = rms_stats(T, cq32, 4, sqb, "sqb", ("cq32",), 4, RMS_EPS, 0)
        for m in range(4):
            R.op("dve", STT(cqn[:, m, 0:T], cq32[:, m, 0:T], gq[:, m:m + 1], rs, ALU.mult, ALU.mult),
                 reads=("cq32", rk, "gq"), writes=("cqn",))
        R.dma("pool", DMA(cqsp[ti, :, 0:4 * T].rearrange("p (c t) -> p c t", c=4), cqn[:, :, 0:T]), "st_cq",
              reads=("cqn",), writes=(f"cqsp{ti}",))
        rs, rk = rms_stats(T, ckv32, 2, sqb, "sqb", ("ckv32",), 4, RMS_EPS, 0)
        for m in range(2):
            R.op("dve", STT(ckv32[:, m, 0:T], ckv32[:, m, 0:T], gkv[:, m:m + 1], rs, ALU.mult, ALU.mult),
                 reads=("ckv32", rk, "gkv"), writes=("ckv32",))
        R.op("act", ACTF(kvb_t[:, :, 0:T], ckv32[:, :, 0:T], AF.Copy), reads=("ckv32",), writes=("kvb",))
        R.op("act", ACTF(sqb[:, 0:2, 0:T], ckv32[:, :, 0:T], AF.Square), reads=("ckv32",), writes=("sqb",))
        R.op("act", ACTF(sqb[0:64, 2, 0:T], kr32[0:64, 0:T], AF.Square), reads=("kr32",), writes=("sqb",))
        R.op("pe", [MM(ps[5][:, 0:T], onesb[:], sqb[:, 0, 0:T], True, False),
                    MM(ps[5][:, 0:T], onesb[:], sqb[:, 1, 0:T], False, False),
                    MM(ps[5][:, 0:T], onesb[0:64, :], sqb[0:64, 2, 0:T], False, True)],
             reads=("sqb", "onesb"), writes=("ps5",))
        if not is_sample:
            R.op("dve", RED(nkt[:, 0:1], ps[5][:, 0:T], ALU.max), reads=("ps5",), writes=("nkt",))
            R.op("dve", TT(nkmax[:, 0:1], nkmax[:, 0:1], nkt[:, 0:1], ALU.max), reads=("nkt", "nkmax"), writes=("nkmax",))
        else:
            R.op("dve", RED(nkt[:, 0:4], ps[5][:, 0:T].rearrange("p (s t) -> p s t", s=4), ALU.max),
                 reads=("ps5",), writes=("nkt",))
        sub(15)
        if not is_sample:
            for blk in range(nblk):
                bnk = 6 + (blk % 2)
                R.op("pe", [TR(ps[bnk][:, 0:128], ckv32[:, 0, blk * 128:(blk + 1) * 128], id32[:]),
                            TR(ps[bnk][:, 128:256], ckv32[:, 1, blk * 128:(blk + 1) * 128], id32[:]),
                            TR(ps[bnk][:, 256:320], kr32[0:64, blk * 128:(blk + 1) * 128], id32[0:64, 0:64])],
                     reads=("ckv32", "kr32", "id32"), writes=(f"ps{bnk}",))
                R.op("act", ACTF(kvtok[:, blk, :], ps[bnk][:, 0:320], AF.Copy), reads=(f"ps{bnk}",), writes=("kvtok",))
                R.op("dve", CP(vtokb[:, blk, :], ps[bnk][:, 0:256]), reads=(f"ps{bnk}",), writes=("vtokb",))
            R.dma("pool", DMA(okvp[ti].rearrange("(b p) c -> p b c", p=128), kvtok[:, :, 0:256]), "st_kv", reads=("kvtok",))
            R.dma("pool", DMA(okrp[ti].rearrange("(b p) c -> p b c", p=128), kvtok[:, :, 256:320]), "st_kr", reads=("kvtok",))
            base = (ti % 2) * XBR
            xb_in_ap = xb_in_aps[ti // 2]
            R.dma("pool", DMA(xb_in_ap[base:base + 128, :].rearrange("p (c t) -> p c t", c=2), kvb[:, :, :]), "st_xk",
                  reads=("kvb",), writes=(f"xb_in_k{ti}",))
            R.dma("pool", DMA(xb_in_ap[base + 128:base + 256, :].rearrange("p (b c) -> p b c", b=4), vtokb[:, :, :]), "st_xv",
                  reads=("vtokb",), writes=(f"xb_in_v{ti}",))
            R.dma("pool", DMA(xb_in_ap[base + 256:base + 288, :].rearrange("r (s c) -> (r s) c", s=2), krb[0:64, :]), "st_xr",
                  reads=("krb",), writes=(f"xb_in_r{ti}",))
        else:
            for s in range(4):
                bnk = 6 + (s % 2)
                R.op("pe", [TR(ps[bnk][0:64, 0:128], ckv32[:, 0, s * 64:(s + 1) * 64], id32[:]),
                            TR(ps[bnk][0:64, 128:256], ckv32[:, 1, s * 64:(s + 1) * 64], id32[:]),
                            TR(ps[bnk][0:64, 256:320], kr32[0:64, s * 64:(s + 1) * 64], id32[0:64, 0:64])],
                     reads=("ckv32", "kr32", "id32"), writes=(f"ps{bnk}",))
                R.op("act", ACTF(kvtok[0:64, s, :], ps[bnk][0:64, 0:320], AF.Copy), reads=(f"ps{bnk}",), writes=("kvtok",))
                R.op("dve", CP(s_vtok[:, s, :], ps[bnk][0:64, 0:256]), reads=(f"ps{bnk}",), writes=("vtokb",))
            R.dma("pool", DMA(okvs.rearrange("(s p) c -> p s c", p=64), kvtok[0:64, :, 0:256]), "st_kv", reads=("kvtok",))
            R.dma("pool", DMA(okrs.rearrange("(s p) c -> p s c", p=64), kvtok[0:64, :, 256:320]), "st_kr", reads=("kvtok",))

    if STAGE == 0:
        return finish()
    for li in range(0 if KSAMPLE else min(NT, KNT)):
        try:
            phaseA(li, TP, xp[li], False)
        except _Stop:
            return finish()
        if STAGE == 1:
            return finish()
    if STAGE == 2:
        return finish()
    R.op("dve", MS(tmp[5][:, 0:128], 0.0), writes=("tmp5",))
    R.dma("pool", DMA(xf_in_ap[NT * 128:NT * 128 + 128, :], tmp[5][:, 0:128]), "st_xz", reads=("tmp5",), writes=("xf_in_z",))
    R.dma("pool", DMA(xf_in_ap[NT * 128 + 128:NT * 128 + 256, :], tmp[5][:, 0:128]), "st_xz2", reads=("tmp5",), writes=("xf_in_z2",))
    R.dma("pool", DMAS(xf_in_ap[NT * 128 + 128:NT * 128 + 256, 0:1], nkmax[:, 0:1]), "st_xn", reads=("nkmax", "xf_in_z2"), writes=("xf_in_n",))
    if KCORES == 8:
        for k in range(4):
            rk = tuple(f"xb_in_{t}{ti}" for t in "kvr" for ti in (2 * k, 2 * k + 1))
            R.coll(lambda g, k=k: g.collective_compute("AllGather", ALU.bypass, replica_groups=[[0, 1], [2, 3], [4, 5], [6, 7]],
                                                       ins=[xb_in[k].ap().opt()], outs=[xb_out[k].ap().opt()]), f"xb{k}",
                   reads=rk, writes=(f"xb_out{k}",))
        rk = tuple(f"xf_in_u{ti}_{ub}" for ti in range(NT) for ub in range(2)) + ("xf_in_z", "xf_in_n")
        R.coll(lambda g: g.collective_compute("AllGather", ALU.bypass, replica_groups=[[0, 1], [2, 3], [4, 5], [6, 7]],
                                              ins=[xf_in.ap().opt()], outs=[xf_out.ap().opt()]), "xf",
               reads=rk, writes=("xf_out",))
    if STAGE == 3:
        return finish()
    phaseA(NT, TS, xs, True)
    if STAGE == 4:
        return finish()
    tokA = R.lastw.copy()
    R.barrier()
    for k in ("xb_out0", "xb_out1", "xb_out2", "xb_out3", "xf_out"):
        if k in tokA:
            R.lastw[k] = tokA[k]

    cv = Carver()
    KT = cv.take([2, 8192], BF16)
    KR = cv.take([8192], BF16)
    VV = cv.take([64, 256], BF16)
    cqn1 = cv.take([4, TP], BF16)
    qn = cv.take([TP], BF16)
    qr = cv.take([8, TP], BF16)
    sq1 = cv.take([3, TP], BF16)
    sq1x = [sq1, cv.take([3, TP], BF16)]
    PT = [cv.take([TP], BF16), cv.take([TP], BF16)]
    olat = cv.take([2, TP], BF16)
    oab = [cv.take([TP], BF16), cv.take([TP], BF16)]
    maskb = cv.take([8, TP], BF16)
    qr32 = cv.take([TP], F32)
    rtab1 = cv.take([2 * TP], F32)
    ktok = cv.take([16, 64], BF16)
    qnrm = cv.take([8, TS], F32)
    oas = cv.take([8, TS], BF16)
    qlat = wsl[2][:, :].rearrange("p (h c t) -> p h c t", h=8, c=2)

    xb_out_aps = [t.ap() for t in xb_out]
    xf_out_ap = xf_out.ap()
    W0, W1 = wsl[0], wsl[1]
    R.dma("pool", DMA(wview(W0, 0, 4, 1024), w_uqn.rearrange("(k p) f -> p k f", p=128)), "w0", writes=("attw",))
    R.dma("pool", DMA(wview(W0, 4096, 4, 512), w_uqr.rearrange("(k p) f -> p k f", p=128)), "w0", writes=("attw",))
    R.dma("pool", DMA(wview(W0, 6144, 4, 512), w_uqrp.rearrange("(k p) f -> p k f", p=128)), "w0", writes=("attw",))
    R.dma("pool", DMA(W1[:, 0:2048], w_ukT[:, :]), "w1", writes=("attw1",))
    R.dma("pool", DMA(wview(W1, 2048, 2, 1024), w_uv.rearrange("(k p) f -> p k f", p=128)), "w1", writes=("attw1",))
    R.dma("pool", DMA(maskb[:, :, :], mask_d.rearrange("p (j t) -> p j t", j=8)), "mask", writes=("maskb",))
    wqn = wview(W0, 0, 4, 1024)
    wqr = wview(W0, 4096, 4, 512)
    wqrp = wview(W0, 6144, 4, 512)
    wuk = wview(W1, 0, 8, 256)
    wuv = wview(W1, 2048, 2, 1024)
    R.op("dve", MS(KR[64:65, :], 1.0), writes=("KRones",))

    def q_proj(ti, T, sample):
        R.dma("pool", DMA(cqn1[:, :, 0:T], cqsp[ti, :, 0:4 * T].rearrange("p (c t) -> p c t", c=4)), "ld_cq",
              reads=(f"cqsp{ti}",), writes=("cqn1",))
        R.dma("pool", DMA(rtab1[0:64, :], rope_d[ti, :, :]), "rtab1", writes=("rtab1",))

        def norm_step(h):
            sq = sq1x[h % 2]
            sk = f"sq1_{h % 2}"
            R.op("pe", [MM(ps[5][:, 0:T], onesb[:], sq[:, 0, 0:T], True, False),
                        MM(ps[5][:, 0:T], onesb[:], sq[:, 1, 0:T], False, False),
                        MM(ps[5][:, 0:T], onesb[0:64, :], sq[0:64, 2, 0:T], False, True)],
                 reads=(sk, "onesb"), writes=("ps5",))
            R.op("act", ACTF(tmp[3][64:65, 0:T], ps[5][64:65, 0:T], AF.Sqrt), reads=("ps5",), writes=("tmp3",))
            if not sample:
                R.op("dve", TSC(qr[64:65, h, 0:T], tmp[3][64:65, 0:T], negnk[64:65, 0:1], None, ALU.mult),
                     reads=("tmp3", "negnk"), writes=("qr",))
            else:
                R.op("act", ACTF(qnrm[64:65, h, 0:T], tmp[3][64:65, 0:T], AF.Copy), reads=("tmp3",), writes=("qnrm",))

        prev = None
        for h in range(8):
            sq = sq1x[h % 2]
            sk = f"sq1_{h % 2}"
            R.op("pe", [MM(ps[0][:, 0:T], wqn[:, k, h * 128:(h + 1) * 128], cqn1[:, k, 0:T], k == 0, k == 3) for k in range(4)],
                 reads=("cqn1", "attw"), writes=("ps0",))
            R.op("act", ACTF(qn[:, 0:T], ps[0][:, 0:T], AF.Copy), reads=("ps0",), writes=("qn",))
            R.op("pe", [MM(ps[3][0:64, 0:T], wqr[:, k, h * 64:(h + 1) * 64], cqn1[:, k, 0:T], k == 0, k == 3) for k in range(4)],
                 reads=("cqn1", "attw"), writes=("ps3",))
            R.op("pe", [MM(ps[4][0:64, 0:T], wqrp[:, k, h * 64:(h + 1) * 64], cqn1[:, k, 0:T], k == 0, k == 3) for k in range(4)],
                 reads=("cqn1", "attw"), writes=("ps4",))
            if prev is not None:
                norm_step(prev)
            for c in range(2):
                R.op("pe", MM(ps[1 + c][:, 0:T], wuk[:, h, c * 128:(c + 1) * 128], qn[:, 0:T], True, True),
                     reads=("qn", "attw1"), writes=(f"ps{1 + c}",))
                R.op("act", ACTF(qlat[:, h, c, 0:T], ps[1 + c][:, 0:T], AF.Copy), reads=(f"ps{1 + c}",), writes=("qlat",))
                R.op("act", ACTF(sq[:, c, 0:T], ps[1 + c][:, 0:T], AF.Square), reads=(f"ps{1 + c}",), writes=(sk,))
            t1, t2 = tmp[1][0:64, 0:T], tmp[2][0:64, 0:T]
            R.op("dve", TT(t1, ps[3][0:64, 0:T], rtab1[0:64, 0:T], ALU.mult), reads=("ps3", "rtab1"), writes=("tmp1",))
            R.op("dve", TT(t2, ps[4][0:64, 0:T], rtab1[0:64, TP:TP + T], ALU.mult), reads=("ps4", "rtab1"), writes=("tmp2",))
            R.op("dve", TT(qr32[0:64, 0:T], t1, t2, ALU.add), reads=("tmp1", "tmp2"), writes=("qr32",))
            R.op("act", ACTF(qr[0:64, h, 0:T], qr32[0:64, 0:T], AF.Copy), reads=("qr32",), writes=("qr",))
            R.op("act", ACTF(sq[0:64, 2, 0:T], qr32[0:64, 0:T], AF.Square), reads=("qr32",), writes=(sk,))
            prev = h
        norm_step(prev)

    def attend(nblocks, T, rhs_fn, kcount_fn, mask_fn, acc, tagK):
        o0, o1, lb = acc
        pend = None

        def pv(j, last):
            kc = kcount_fn(j)
            pt = PT[j % 2]
            R.op("pe", [MM(ps[o0][:, 0:T], VV[0:kc, j, 0:128], pt[0:kc, 0:T], j == 0, last),
                        MM(ps[o1][:, 0:T], VV[0:kc, j, 128:256], pt[0:kc, 0:T], j == 0, last),
                        MM(ps[lb][:, 0:T], onesb[0:kc, :], pt[0:kc, 0:T], j == 0, last)],
                 reads=(f"PT{j % 2}", "V", "onesb"), writes=(f"ps{o0}", f"ps{o1}", f"ps{lb}"))

        for j in range(nblocks):
            kc = kcount_fn(j)
            sbk = j % 2
            mk = mask_fn(j)
            fns = [MM(ps[sbk][0:kc, 0:T], KT[:, 0, j * 128:j * 128 + kc], rhs_fn(0), True, False),
                   MM(ps[sbk][0:kc, 0:T], KT[:, 1, j * 128:j * 128 + kc], rhs_fn(1), False, False),
                   MM(ps[sbk][0:kc, 0:T], KR[0:65, j * 128:j * 128 + kc], rhs_fn(2), False, mk is None)]
            rd = ["KT", "KR", "KRones", "qlat", "qr"]
            if mk is not None:
                fns.append(MM(ps[sbk][0:kc, 0:T], idb[:], mk, False, True))
                rd += ["maskb", "idb"]
            R.op("pe", fns, reads=tuple(rd), writes=(f"ps{sbk}",))
            R.op("act", ACTF(PT[sbk][0:kc, 0:T], ps[sbk][0:kc, 0:T], AF.Exp, scale=SCALE), reads=(f"ps{sbk}",), writes=(f"PT{sbk}",))
            if j == 2:
                flush()
            if pend is not None:
                pv(pend, False)
            pend = j
        pv(pend, True)

    def finish_head(T, acc, out_fn):
        o0, o1, lb = acc
        rinv = tmp[4][:, 0:T]
        R.op("dve", RCP(rinv, ps[lb][:, 0:T]), reads=(f"ps{lb}",), writes=("tmp4",))
        R.op("dve", TT(olat[:, 0, 0:T], ps[o0][:, 0:T], rinv, ALU.mult), reads=(f"ps{o0}", "tmp4"), writes=("olat",))
        R.op("dve", TT(olat[:, 1, 0:T], ps[o1][:, 0:T], rinv, ALU.mult), reads=(f"ps{o1}", "tmp4"), writes=("olat",))
        pending.append(out_fn)

    pending = []

    def flush():
        while pending:
            pending.pop(0)()

    q_proj(NT, TS, True)
    ACC_A, ACC_B = (2, 3, 4), (5, 6, 7)
    for s in range(4):
        R.dma("pool", DMA(VV[:, 0:16, :], ckvc[s].rearrange("(j p) c -> p j c", p=128)), "ld_vc", writes=("V",))
        R.dma("pool", DMA(ktok[:, :, :], ckrc[s].rearrange("(j p) c -> p j c", p=128)), "ld_kc", writes=("ktok",))
        R.op("act", ACTF(VV[0:64, 16, :], s_vtok[:, s, :], AF.Copy), reads=("vtokb", "V"), writes=("V",))
        gi = 0
        for jg in range(4):
            for c in range(2):
                bnk = 6 + (gi % 2)
                gi += 1
                R.op("pe", [TR(psb(bnk)[:, i * 128:(i + 1) * 128], VV[:, jg * 4 + i, c * 128:(c + 1) * 128], idb[:]) for i in range(4)],
                     reads=("V", "idb"), writes=(f"ps{bnk}",))
                R.op("dve", CP(KT[:, c, jg * 512:(jg + 1) * 512], psb(bnk)[:, 0:512]), reads=(f"ps{bnk}",), writes=("KT",))
            bnk = 6 + (gi % 2)
            gi += 1
            R.op("pe", [TR(psb(bnk)[0:64, i * 128:(i + 1) * 128], ktok[:, jg * 4 + i, :], idb[:]) for i in range(4)],
                 reads=("ktok", "idb"), writes=(f"ps{bnk}",))
            R.op("dve", CP(KR[0:64, jg * 512:(jg + 1) * 512], psb(bnk)[0:64, 0:512]), reads=(f"ps{bnk}",), writes=("KR",))
        R.op("act", ACTF(KT[:, :, 2048:2112], s_kvb[:, :, s * 64:(s + 1) * 64], AF.Copy), reads=("kvb", "KT"), writes=("KT",))
        R.op("act", ACTF(KR[0:64, 2048:2112], s_krb[0:64, s * 64:(s + 1) * 64], AF.Copy), reads=("krb", "KR"), writes=("KR",))
        for g5 in range(5):
            c0 = g5 * 512
            wd = 512 if g5 < 4 else 64
            R.op("act", ACTF(sq1[:, 0:2, 0:wd], KT[:, :, c0:c0 + wd], AF.Square), reads=("KT",), writes=("sq1_0",))
            R.op("act", ACTF(sq1[0:64, 2, 0:wd], KR[0:64, c0:c0 + wd], AF.Square), reads=("KR",), writes=("sq1_0",))
            R.op("pe", [MM(ps[5][:, 0:wd], onesb[:], sq1[:, 0, 0:wd], True, False),
                        MM(ps[5][:, 0:wd], onesb[:], sq1[:, 1, 0:wd], False, False),
                        MM(ps[5][:, 0:wd], onesb[0:64, :], sq1[0:64, 2, 0:wd], False, True)],
                 reads=("sq1_0", "onesb"), writes=("ps5",))
            R.op("dve", RED(nkt[:, g5:g5 + 1], ps[5][:, 0:wd], ALU.max), reads=("ps5",), writes=("nkt",))
        R.op("dve", RED(nkt[:, 7:8], nkt[:, 0:5], ALU.max), reads=("nkt",), writes=("nkt7",))
        R.op("act", ACTF(nkt[:, 7:8], nkt[:, 7:8], AF.Sqrt), reads=("nkt7",), writes=("nkt7",))
        R.op("dve", TSC(negnk[:, 1 + s:2 + s], nkt[:, 7:8], -1.03, None, ALU.mult), reads=("nkt7",), writes=("negnk",))
        R.op("dve", TSC(qr[64:65, :, s * 64:(s + 1) * 64], qnrm[64:65, :, s * 64:(s + 1) * 64], negnk[64:65, 1 + s:2 + s], None, ALU.mult),
             reads=("qnrm", "negnk", "qr"), writes=("qr",))
        acc = ACC_A
        attend(17, 512,
               lambda c, s=s: (qlat[:, :, c, s * 64:(s + 1) * 64] if c < 2 else qr[0:65, :, s * 64:(s + 1) * 64]),
               lambda j: 128 if j < 16 else 64, lambda j: None, acc, "s")

        def proj_s(s=s, acc=acc):
            fns = []
            for h in range(8):
                for c in range(2):
                    fns.append(MM(ps[5][:, h * 64:(h + 1) * 64], wuv[:, c, h * 128:(h + 1) * 128], olat[:, c, h * 64:(h + 1) * 64], c == 0, c == 1))
            R.op("pe", fns, reads=("olat", "attw1"), writes=("ps5",))
            R.op("act", ACTF(oas[:, :, s * 64:(s + 1) * 64], ps[5][:, 0:512].rearrange("p (h t) -> p h t", h=8), AF.Copy),
                 reads=("ps5",), writes=("oas",))
        finish_head(512, acc, proj_s)
        flush()
    R.dma("pool", DMA(oasp[NT, :, 0:8 * TS].rearrange("p (h t) -> p h t", h=8), oas[:, :, :]), "st_oas", reads=("oas",), writes=(f"oasp{NT}",))

    if STAGE == 5:
        return finish()
    if not KSAMPLE:
        for r in range(2):
            for li in range(NT):
                gt = 2 * li + r
                base = (r * 2 + (li % 2)) * XBR
                xb_out_ap = xb_out_aps[li // 2]
                xbk = f"xb_out{li // 2}"
                R.dma("pool", DMA(KT[:, :, gt * 512:(gt + 1) * 512], xb_out_ap[base:base + 128, :].rearrange("p (c t) -> p c t", c=2)),
                      "kvres", reads=(xbk,), writes=("KT",), join=True)
                R.dma("pool", DMA(VV[:, gt * 4:(gt + 1) * 4, :], xb_out_ap[base + 128:base + 256, :].rearrange("p (b c) -> p b c", b=4)),
                      "kvres", reads=(xbk,), writes=("V",), join=True)
                R.dma("pool", DMA(KR[0:64, gt * 512:(gt + 1) * 512], xb_out_ap[base + 256:base + 288, :].rearrange("r (s c) -> (r s) c", s=2)),
                      "kvres", reads=(xbk,), writes=("KR",), join=True)
        tk = R.lastw["KR"]
        R.lastw["KT"] = tk
        R.lastw["V"] = tk
        R.dma("pool", DMAS(nkt[:, 5:6], xf_out_ap[NT * 128 + 128:NT * 128 + 256, 0:1]), "ld_nk", reads=("xf_out",), writes=("nkt56",))
        R.dma("pool", DMAS(nkt[:, 6:7], xf_out_ap[XFR + NT * 128 + 128:XFR + NT * 128 + 256, 0:1]), "ld_nk", reads=("xf_out",), writes=("nkt56",))
        R.op("dve", TT(nkt[:, 7:8], nkt[:, 5:6], nkt[:, 6:7], ALU.max), reads=("nkt56", "nkt7"), writes=("nkt7",))
        R.op("act", ACTF(nkt[:, 7:8], nkt[:, 7:8], AF.Sqrt), reads=("nkt7",), writes=("nkt7",))
        R.op("dve", TSC(negnk[:, 0:1], nkt[:, 7:8], -1.03, None, ALU.mult), reads=("nkt7",), writes=("negnk",))

    for li in range(0 if KSAMPLE else NT):
        flush()
        q_proj(li, TP, False)
        nblocks = 8 * li + 8
        for h in range(8):
            acc = ACC_A if h % 2 == 0 else ACC_B
            attend(nblocks, TP,
                   lambda c, h=h: (qlat[:, h, c, :] if c < 2 else qr[0:65, h, :]),
                   lambda j: 128,
                   lambda j, li=li: (maskb[:, j - 8 * li, :] if j >= 8 * li else None), acc, "p")

            def proj_p(h=h, acc=acc, li=li):
                o0 = acc[0]
                R.op("pe", [MM(ps[o0][:, 0:TP], wuv[:, c, h * 128:(h + 1) * 128], olat[:, c, :], c == 0, c == 1) for c in range(2)],
                     reads=("olat", "attw1"), writes=(f"ps{o0}",))
                ob = oab[h % 2]
                R.op("act", ACTF(ob[:, :], ps[o0][:, 0:TP], AF.Copy), reads=(f"ps{o0}",), writes=(f"oab{h % 2}",))
                R.dma("pool", DMA(oasp[li, :, h * TP:(h + 1) * TP], ob[:, :]), f"st_oa{h % 2}", reads=(f"oab{h % 2}",), writes=(f"oasp{li}",))
            finish_head(TP, acc, proj_p)
    flush()

    R.barrier()

    cv = Carver()
    x32 = cv.take([16, TP], F32)
    xb = cv.take([16, TP], BF16)
    aT = cv.take([11, TP], BF16)
    lnsq = cv.take([8, TP], BF16)
    uext = cv.take([8, 528], F32)
    sA = cv.take([528], F32)
    sB = cv.take([528], F32)
    dT = cv.take([8, TP], BF16)
    h0 = cv.take([8, 16], F32)
    h1 = cv.take([8, 16], F32)
    ptab = cv.take([64], F32)
    ptok = cv.take([4, 256], F32)
    pT = cv.take([2, TP], BF16)
    ytok = [cv.take([D], F32), cv.take([D], F32)]
    wppb = cv.take([2, D], BF16)
    R.dma("pool", DMA(wppb[:, :, :], w_pp.rearrange("(k p) d -> p k d", p=128)), "ld_wpp", writes=("wppb",))

    def layer_norm2(T, lnidx, post_scale, pre=None, write_xb=True):
        layer_norm(T, x32, xb, lnidx, "x32_", "xb_", post_scale, pre=pre, write_xb=write_xb)

    pool_ready = set()
    oa_ready = set()

    def pool_prepare(ti, T, is_sample):
        if ti in pool_ready:
            return []
        pool_ready.add(ti)
        nseq = 4 if is_sample else 1
        L = T // nseq
        E = 16 + L
        ue = uext[:, :, 0:nseq * E].rearrange("p c (s e) -> p c s e", s=nseq)
        for s in range(nseq):
            R.dma("pool", DMA(ue[:, :, s, 16:E], usp[ti, :, 0:8 * T].rearrange("p (c t) -> p c t", c=8)[:, :, s * L:(s + 1) * L]),
                  "ld_u", reads=(f"usp{ti}",), writes=("uext",))
        R.dma("pool", DMA(ptab[:, :], ptab_d[ti, :, :]), "ld_ptab", writes=("ptab",))
        if not is_sample:
            zrow = NT * 128
            r1row = XFR + (ti - 1) * 128 if ti > 0 else zrow
            R.dma("pool", DMA(h0[:, :, :], xf_out_ap[r1row:r1row + 128, :].rearrange("p (c t) -> p c t", c=8)), "ld_h0",
                  reads=("xf_out",), writes=("h0",))
            R.dma("pool", DMA(h1[:, :, :], xf_out_ap[ti * 128:(ti + 1) * 128, :].rearrange("p (c t) -> p c t", c=8)), "ld_h1",
                  reads=("xf_out",), writes=("h1",))
            R.op("dve", TSC(h0[:, :, :], h0[:, :, :], sel[:, 0:1], None, ALU.mult), reads=("h0", "sel"), writes=("h0",))
            R.op("dve", STT(ue[:, :, 0, 0:16], h1[:, :, :], sel[:, 1:2], h0[:, :, :], ALU.mult, ALU.add),
                 reads=("h0", "h1", "sel", "uext"), writes=("uext",))
        else:
            hv = histT.rearrange("p (c s t) -> p c s t", c=8, s=4)
            for c in range(8):
                R.dma("pool", DMA(ue[:, c, :, 0:16], hv[:, c, :, :]), "ld_u", reads=(), writes=("uext",))
        def pool_chunk(c):
            g = c // 2
            u_c = ue[:, c, :, :]
            cur, curk = u_c, "uext"
            bufs = [(sA, "sA"), (sB, "sB")]
            sh = 1
            for step in range(g + 1):
                dstt, dk = bufs[step % 2]
                dst = dstt[:, 0:nseq * E].rearrange("p (s e) -> p s e", s=nseq)
                lo = 2 * sh - 1
                R.op("dve", TT(dst[:, :, lo:E], cur[:, :, lo:E], cur[:, :, lo - sh:E - sh], ALU.add), reads=(curk,), writes=(dk,))
                cur, curk = dst, dk
                sh *= 2
            w = WINS[g]
            dv = dT[:, c, 0:T].rearrange("p (s l) -> p s l", s=nseq)
            R.op("dve", STT(dv, cur[:, :, 16:E], 1.0 / w, u_c[:, :, 16:E], ALU.mult, ALU.subtract), reads=(curk, "uext"), writes=(f"dT{c}",))
            if not is_sample:
                tf = tmp[5][:, 0:16]
                R.op("dve", TT(tf, cur[:, 0, 16:32], ptab[:, g * 16:(g + 1) * 16], ALU.mult), reads=(curk, "ptab"), writes=("tmp5",))
                R.op("dve", TT(dT[:, c, 0:16], tf, u_c[:, 0, 16:32], ALU.subtract), reads=("tmp5", "uext"), writes=(f"dT{c}",))

        return [(lambda c=c: pool_chunk(c)) for c in range(8)]

    p_ready = set()

    def p_prepare(ti, T, p_tok):
        if ti in p_ready:
            return
        p_ready.add(ti)
        for blk in range(T // 128):
            R.dma("pool", DMA(ptok[:, blk, :], p_tok[blk * 128:(blk + 1) * 128, :]), "ld_p", writes=("ptok",))
        for blk in range(T // 128):
            bnk = 6 + (blk % 2)
            R.op("pe", [TR(ps[bnk][:, i * 128:(i + 1) * 128], ptok[:, blk, i * 128:(i + 1) * 128], id32[:]) for i in range(2)],
                 reads=("ptok", "id32"), writes=(f"ps{bnk}",))
            R.op("dve", CP(pT[:, :, blk * 128:(blk + 1) * 128], ps[bnk][:, 0:256].rearrange("p (a b) -> p a b", a=2)),
                 reads=(f"ps{bnk}",), writes=("pT",))

    def phaseB2(ti, T, p_tok, y_out, is_sample, nxt=None):
        nseq = 4 if is_sample else 1
        L = T // nseq
        E = 16 + L
        xkeys = tuple(f"x32_{c}" for c in range(NCH))
        R.dma("pool", DMA(x32[:, :, 0:T], x1sp[ti, :, 0:NCH * T].rearrange("p (c t) -> p c t", c=NCH)), "ld_x1",
              reads=(f"x1sp{ti}",), writes=xkeys)
        if ti not in oa_ready:
            R.dma("pool", DMA(xb[:, 0:8, 0:T], oasp[ti, :, 0:8 * T].rearrange("p (h t) -> p h t", h=8)), "ld_oa",
                  reads=(f"oasp{ti}",), writes=tuple(f"xb_{c}" for c in range(8)))
        p_prepare(ti, T, p_tok)
        for f_ in pool_prepare(ti, T, is_sample):
            f_()
        slot, wk = wload([(0, w_pool.rearrange("(k p) d -> p k d", p=128))], bid=("wpool",))
        wp = wview(slot, 0, 8, 256)
        for g in range(4):
            for dc in range(2):
                bnk = (2 * g + dc) % 4
                R.op("pe", [MM(ps[bnk][:, 0:T], wp[:, 2 * g + cc, dc * 128:(dc + 1) * 128], dT[:, 2 * g + cc, 0:T], cc == 0, cc == 1) for cc in range(2)],
                     reads=(f"dT{2 * g}", f"dT{2 * g + 1}", wk), writes=(f"ps{bnk}",))
                R.op("dve", TSC(xb[:, 8 + 2 * g + dc, 0:T], ps[bnk][:, 0:T], psc[:, 2 * g + dc:2 * g + dc + 1], None, ALU.mult),
                     reads=(f"ps{bnk}", "psc"), writes=(f"xb_{8 + 2 * g + dc}",))
        xbkeys = tuple(f"xb_{c}" for c in range(NCH))
        wov = w_o.rearrange("(k p) f -> p k f", p=128)
        hk2 = StatsHook(T, x32, "x32_", 0, 1)
        for dg in range(4):
            slot, wk = wload([(0, wov[:, :, dg * 512:(dg + 1) * 512])], bid=("wo", dg))
            wv = wview(slot, 0, 16, 512)
            for dc in range(4):
                c = dg * 4 + dc
                bnk = 4 + dc
                R.op("pe", [MM(ps[bnk][:, 0:T], wv[:, k, dc * 128:(dc + 1) * 128], xb[:, k, 0:T], k == 0, k == 15) for k in range(16)],
                     reads=xbkeys + (wk,), writes=(f"ps{bnk}",))
                R.op("dve", STT(x32[:, c, 0:T], x32[:, c, 0:T], ALPHA, ps[bnk][:, 0:T], ALU.mult, ALU.add),
                     reads=(f"ps{bnk}", f"x32_{c}"), writes=(f"x32_{c}",))
                hk2.chunk(c)
        def dbg(k):
            if KSUB == k:
                R.dma("pool", DMA(x1sp[ti, :, 0:NCH * T].rearrange("p (c t) -> p c t", c=NCH), x32[:, :, 0:T]), "st_dbg", reads=xkeys)
                raise _Stop()
        dbg(20)
        layer_norm2(T, 1, ALPHA, pre=hk2)
        dbg(21)
        bg = pool_prepare(*nxt[0:3]) if nxt is not None else []
        hk3 = StatsHook(T, x32, "x32_", 0, 1)
        ffn(T, x32, xb, aT, w_f2gu, w_f2d, "x32_", "xb_", "f2", hook=hk3, bgw=bg)
        layer_norm2(T, 2, ALPHA, pre=hk3)
        dbg(22)
        wpp, wkp = wppb, "wppb"
        wgv = w_pg.rearrange("(k p) f -> p k f", p=128)
        hk4 = StatsHook(T, x32, "x32_", 4, 5)
        for dg in range(4):
            slot, wk = wload([(0, wgv[:, :, dg * 512:(dg + 1) * 512])], bid=("wpg", dg))
            wv = wview(slot, 0, 16, 512)
            for dc in range(4):
                c = dg * 4 + dc
                par = c % 2
                bg, bp = 2 * par, 2 * par + 1
                R.op("pe", [MM(ps[bg][:, 0:T], wv[:, k, dc * 128:(dc + 1) * 128], xb[:, k, 0:T], k == 0, k == 15) for k in range(16)],
                     reads=xbkeys + (wk,), writes=(f"ps{bg}",))
                R.op("pe", [MM(ps[bp][:, 0:T], wpp[:, k, c * 128:(c + 1) * 128], pT[:, k, 0:T], k == 0, k == 1) for k in range(2)],
                     reads=("pT", wkp), writes=(f"ps{bp}",))
                tg = tmp[par][:, 0:T]
                R.op("act", ACTF(tg, ps[bg][:, 0:T], AF.Sigmoid), reads=(f"ps{bg}",), writes=(f"tmp{par}",))
                R.op("dve", TT(tg, tg, ps[bp][:, 0:T], ALU.mult), reads=(f"tmp{par}", f"ps{bp}"), writes=(f"tmp{par}",))
                R.op("dve", TT(x32[:, c, 0:T], x32[:, c, 0:T], tg, ALU.add), reads=(f"tmp{par}", f"x32_{c}"), writes=(f"x32_{c}",))
                hk4.chunk(c)
        dbg(23)
        if nxt is not None:
            nti, nT = nxt[0], nxt[1]
            R.dma("pool", DMA(xb[:, 0:8, 0:nT], oasp[nti, :, 0:8 * nT].rearrange("p (h t) -> p h t", h=8)), "ld_oa",
                  reads=(f"oasp{nti}",), writes=tuple(f"xb_{c}" for c in range(8)))
            oa_ready.add(nti)
            p_prepare(nti, nT, nxt[3])
        layer_norm2(T, 3, None, pre=hk4, write_xb=False)
        for blk in range(T // 128):
            yt = ytok[blk % 2]
            ytk = f"ytok{blk % 2}"
            for g in range(4):
                bnk = 4 + g
                R.op("pe", [TR(ps[bnk][:, i * 128:(i + 1) * 128], x32[:, 4 * g + i, blk * 128:(blk + 1) * 128], id32[:]) for i in range(4)],
                     reads=tuple(f"x32_{4 * g + i}" for i in range(4)) + ("id32",), writes=(f"ps{bnk}",))
                if g % 2 == 0:
                    R.op("act", ACTF(yt[:, g * 512:(g + 1) * 512], ps[bnk][:, :], AF.Copy), reads=(f"ps{bnk}",), writes=(ytk,))
                else:
                    R.op("dve", CP(yt[:, g * 512:(g + 1) * 512], ps[bnk][:, :]), reads=(f"ps{bnk}",), writes=(ytk,))
            R.dma("pool", DMA(y_out[blk * 128:(blk + 1) * 128, :], yt[:, :]), f"st_y{blk % 2}", reads=(ytk,))

    if STAGE == 6:
        return finish()
    order = [(NT, TS, pss, ys, True)] if KSAMPLE else \
        [(0, TP, pp[0], yp[0], False), (NT, TS, pss, ys, True)] + [(li, TP, pp[li], yp[li], False) for li in range(1, NT)]
    try:
        for i, (ti_, T_, p_, y_, smp_) in enumerate(order):
            nxt = (order[i + 1][0], order[i + 1][1], order[i + 1][4], order[i + 1][2]) if i + 1 < len(order) else None
            phaseB2(ti_, T_, p_, y_, smp_, nxt=nxt)
            if STAGE == 7 and smp_:
                return finish()
    except _Stop:
        return finish()
    R.barrier()
    return nc, R, es


def emit(nc, R, es):
    sems = {}
    for name in R.semnames:
        sems[name] = es.enter_context(nc.semaphore(name))
    block = es.enter_context(nc.Block())

    def replay(eng, q):
        for it in q:
            if it[0] == "w":
                eng.wait_ge(sems[it[1]], it[2])
            else:
                ins = it[1](eng)
                if it[2] is not None:
                    if it[3] is None:
                        ins.then_inc(sems[it[2]])
                    else:
                        ins.then_inc(sems[it[2]], it[3])

    @block.tensor
    def _(e):
        replay(e, R.q["pe"])

    @block.scalar
    def _(e):
        replay(e, R.q["act"])

    @block.vector
    def _(e):
        replay(e, R.q["dve"])

    @block.gpsimd
    def _(e):
        replay(e, R.q["pool"])

    @block.sync
    def _(e):
        replay(e, R.q["sp"])


_CACHE = {}


def get_program():
    if "nc" not in _CACHE:
        nc, R, es = build_program()
        emit(nc, R, es)
        es.close()
        _CACHE["nc"] = nc
    return _CACHE["nc"]


def _rope_tab(pos):
    inv = (1.0 / (np.float32(10000.0) ** (np.arange(0, 64, 2, dtype=np.float32) / np.float32(64)))).astype(np.float32)
    ang = pos.astype(np.float32)[:, None] * inv[None, :]
    ang = np.concatenate([ang, ang], -1)
    cos = np.cos(ang).astype(np.float32).T
    sin = np.sin(ang).astype(np.float32).T
    sin_signed = sin.copy()
    sin_signed[0:32] *= -1.0
    return cos, sin_signed


def kernel(x_prompt, x_sample, cache_kv_latent, cache_k_rope, state_pool, p_prompt, p_sample,
           ffn1_gu, ffn1_down, ln1_g, ln1_b, w_in, g_q, g_kv, w_uq_nope, w_uq_rope, w_uk, w_uv,
           w_pool, pool_scale, w_o, ln2_g, ln2_b, ffn2_gu, ffn2_down, ln3_g, ln3_b,
           w_ple_gate, w_ple_proj, ln4_g, ln4_b):
    f32 = np.float32
    A = lambda a: np.ascontiguousarray(np.asarray(a, dtype=f32))
    x_prompt = np.asarray(x_prompt, f32); x_sample = np.asarray(x_sample, f32)
    p_prompt = np.asarray(p_prompt, f32); p_sample = np.asarray(p_sample, f32)
    cache_kv_latent = np.asarray(cache_kv_latent, f32); cache_k_rope = np.asarray(cache_k_rope, f32)
    state_pool = np.asarray(state_pool, f32)
    perm = (np.arange(64) + 32) % 64
    w_in0 = np.asarray(w_in, f32)[0]
    w_uqr0 = np.asarray(w_uq_rope, f32)[0]
    fm = lambda v, n: A(np.asarray(v, f32).reshape(n, 128).T)
    lnp = np.concatenate([fm(ln1_g[0], 16), fm(ln1_b[0], 16), fm(ln2_g[0], 16), fm(ln2_b[0], 16),
                          fm(ln3_g[0], 16), fm(ln3_b[0], 16), fm(ln4_g[0], 16), fm(ln4_b[0], 16)], axis=1)
    shared = {
        "ffn1_gu": A(ffn1_gu[0]), "ffn1_down": A(ffn1_down[0]), "ffn2_gu": A(ffn2_gu[0]), "ffn2_down": A(ffn2_down[0]),
        "w_in": A(w_in0), "w_in_perm": A(w_in0[:, 768:832][:, perm]),
        "w_uqn": A(np.asarray(w_uq_nope, f32)[0].reshape(512, 1024)),
        "w_uqr": A(w_uqr0.reshape(512, 512)), "w_uqr_perm": A(w_uqr0[:, :, perm].reshape(512, 512)),
        "w_ukT": A(np.asarray(w_uk, f32)[0].transpose(2, 1, 0).reshape(128, 2048)),
        "w_uv": A(np.asarray(w_uv, f32)[0].reshape(256, 1024)),
        "w_pool": A(np.asarray(w_pool, f32)[0].reshape(1024, 256)), "w_o": A(w_o[0]),
        "w_ple_gate": A(w_ple_gate[0]), "w_ple_proj": A(w_ple_proj[0]),
        "lnp": A(lnp), "gq": fm(g_q[0], 4), "gkv": fm(g_kv[0], 2), "pscale": fm(pool_scale[0], 8),
        "ident32": np.eye(128, dtype=f32), "identb": np.eye(128, dtype=f32).astype(ml_dtypes.bfloat16),
    }
    in_maps = []
    for c in range(8):
        b, h = c // 2, c % 2
        m = dict(shared)
        m["xp"] = A(x_prompt[b].reshape(16, TP, D)[h::2])
        m["pp"] = A(p_prompt[0, b].reshape(16, TP, 256)[h::2])
        m["xs"] = A(x_sample[4 * c:4 * c + 4].reshape(TS, D))
        m["ps"] = A(p_sample[0, 4 * c:4 * c + 4].reshape(TS, 256))
        m["ckvc"] = A(cache_kv_latent[0, 4 * c:4 * c + 4])
        m["ckrc"] = A(cache_k_rope[0, 4 * c:4 * c + 4])
        hist = state_pool[0, 4 * c:4 * c + 4]
        hp = np.zeros((4, 16, 1024), f32)
        hp[:, 1:16] = hist
        m["histT"] = A(hp.reshape(4, 16, 8, 128).transpose(3, 2, 0, 1).reshape(128, 8 * 4 * 16))
        rt = np.zeros((NT + 1, 64, 2 * TP), f32)
        ptb = np.zeros((NT + 1, 128, 64), f32)
        for li in range(NT):
            gt = 2 * li + h
            cos, sn = _rope_tab(gt * TP + np.arange(TP))
            rt[li, :, 0:TP] = cos
            rt[li, :, TP:] = sn
            for g, w in enumerate(WINS):
                pos = gt * TP + np.arange(16)
                ptb[li, :, g * 16:(g + 1) * 16] = (1.0 / np.minimum(w, pos + 1).astype(f32))[None, :]
        cos, sn = _rope_tab(2048 + (np.arange(TS) % 64))
        rt[NT, :, 0:TS] = cos
        rt[NT, :, TP:TP + TS] = sn
        for g, w in enumerate(WINS):
            ptb[NT, :, g * 16:(g + 1) * 16] = 1.0 / w
        m["ropetab"] = rt
        m["pooltab"] = ptb
        mk = np.zeros((128, 8, TP), f32)
        kk = np.arange(128)[:, None]
        qq = np.arange(TP)[None, :]
        for jj in range(8):
            key_chunk = jj * 2 + kk // 64
            q_chunk = h * 8 + qq // 64
            mk[:, jj, :] = np.where(key_chunk <= q_chunk, 0.0, NEG)
        m["maskb"] = A(mk.reshape(128, 8 * TP)).astype(ml_dtypes.bfloat16)
        sel = np.zeros((128, 2), f32)
        sel[:, h] = 1.0
        m["sel"] = sel
        in_maps.append(m)

    nc = get_program()
    res = run_bass_kernel_spmd(nc, in_maps[:KCORES], core_ids=list(range(KCORES)))
    outs = list(res.results) + [res.results[0]] * (8 - KCORES)
    y_p = np.zeros((4, 8192, D), f32); y_s = np.zeros((32, 64, D), f32)
    kv_p = np.zeros((1, 4, 8192, 256), f32); kr_p = np.zeros((1, 4, 8192, 64), f32); pool_p = np.zeros((1, 4, 15, 1024), f32)
    kv_s = np.zeros((1, 32, 64, 256), f32); kr_s = np.zeros((1, 32, 64, 64), f32); pool_s = np.zeros((1, 32, 15, 1024), f32)
    for c in range(8):
        b, h = c // 2, c % 2
        o = outs[c]
        y_p[b].reshape(16, TP, D)[h::2] = np.asarray(o["yp"], f32)
        kv_p[0, b].reshape(16, TP, 256)[h::2] = np.asarray(o["okvp"], f32)
        kr_p[0, b].reshape(16, TP, 64)[h::2] = np.asarray(o["okrp"], f32)
        if h == 1:
            pool_p[0, b] = np.asarray(o["opoolp"], f32)[1:16]
        y_s[4 * c:4 * c + 4] = np.asarray(o["ys"], f32).reshape(4, 64, D)
        kv_s[0, 4 * c:4 * c + 4] = np.asarray(o["okvs"], f32).reshape(4, 64, 256)
        kr_s[0, 4 * c:4 * c + 4] = np.asarray(o["okrs"], f32).reshape(4, 64, 64)
        pool_s[0, 4 * c:4 * c + 4] = np.asarray(o["opools"], f32).reshape(16, 4, 1024).transpose(1, 0, 2)[:, 1:16]
    return (y_p, y_s, kv_p, kr_p, pool_p, kv_s, kr_s, pool_s)
```
